# Optimizing a Trainium2 kernel written in Bass

```python
import math
import jax, jax.numpy as jnp
from jax import lax
import numpy as np

D_MODEL = 2048
BATCH = 2
SEQ = 8192
DEPTH = 2
DEC_BATCH = 8
DEC_SEQ = 4096
PAST_LEN = 128

N_META = 16
MIX_WIDTH = D_MODEL
F_GROUPS = 4
F_GROUP_DIM = 128
F_WIDTH = F_GROUPS * F_GROUP_DIM
MLA_HEADS = 8
Q_LORA = 768
KV_LORA = 512
QK_NOPE = 128
QK_ROPE = 64
V_HEAD = 128
MLA_WIDTH = MLA_HEADS * V_HEAD
ROPE_THETA = 10000.0
DIFF_HEADS = 4
DIFF_QK = 64
DIFF_V = 2 * DIFF_QK
DIFF_WIDTH = DIFF_HEADS * DIFF_V
REL_BUCKETS = 32
REL_MAX_DIST = 128
Q_BLOCK = 128
NORM_EPS = 1e-6

IN_SIZES = [F_WIDTH, Q_LORA, KV_LORA, QK_ROPE,
            DIFF_HEADS * 2 * DIFF_QK, DIFF_HEADS * 2 * DIFF_QK, DIFF_WIDTH, MIX_WIDTH]
IN_WIDTH = sum(IN_SIZES)
IN_SPLITS = [int(v) for v in np.cumsum(IN_SIZES[:-1])]

kernel_name = 'hybrid_fnet_mla_diffattn_encoder'


def rmsnorm(x, g, eps=NORM_EPS):
    xf = x.astype(jnp.float32)
    y = xf * lax.rsqrt(jnp.mean(xf * xf, axis=-1, keepdims=True) + eps)
    return (y * g.astype(jnp.float32)).astype(x.dtype)


def rope_tables(seq_len):
    pos = jnp.arange(seq_len, dtype=jnp.float32)
    inv_freq = ROPE_THETA ** (-jnp.arange(0, QK_ROPE, 2, dtype=jnp.float32) / QK_ROPE)
    ang = pos[:, None] * inv_freq[None, :]
    return jnp.cos(ang), jnp.sin(ang)


def apply_rope(x, cos, sin):
    xf = x.astype(jnp.float32)
    x1, x2 = xf[..., :QK_ROPE // 2], xf[..., QK_ROPE // 2:]
    return jnp.concatenate([x1 * cos - x2 * sin, x2 * cos + x1 * sin], axis=-1).astype(x.dtype)


def t5_bucket(rel):
    nb = REL_BUCKETS // 2
    max_exact = nb // 2
    ret = (rel > 0).astype(jnp.int32) * nb
    n = jnp.abs(rel)
    nf = jnp.maximum(n, 1).astype(jnp.float32)
    large = max_exact + (jnp.log(nf / max_exact) / math.log(REL_MAX_DIST / max_exact)
                         * (nb - max_exact)).astype(jnp.int32)
    large = jnp.minimum(large, nb - 1)
    return ret + jnp.where(n < max_exact, n, large)


def sweep_queries(attend, qs, seq_len):
    pos = jnp.arange(seq_len, dtype=jnp.int32)
    head = attend(tuple(q[:, :N_META] for q in qs), pos[:N_META])
    n_blk = (seq_len - N_META) // Q_BLOCK

    def to_blocks(q):
        r = q[:, N_META:]
        r = r.reshape((r.shape[0], n_blk, Q_BLOCK) + r.shape[2:])
        return jnp.moveaxis(r, 1, 0)

    blks = tuple(to_blocks(q) for q in qs)
    pos_b = pos[N_META:].reshape(n_blk, Q_BLOCK)
    out = lax.map(lambda a: attend(a[0], a[1]), (blks, pos_b))
    out = jnp.moveaxis(out, 0, 1)
    out = out.reshape((out.shape[0], n_blk * Q_BLOCK) + out.shape[3:])
    return jnp.concatenate([head, out], axis=1)


def mixer_layer(x, l, rel_bias, norm_w, w_in, w_fmix, q_norm, w_uq, kv_norm, w_ukv,
                lam_q1, lam_k1, lam_q2, lam_k2, diff_norm, w_o):
    B, S, _ = x.shape
    dt = x.dtype
    h = rmsnorm(x, norm_w[l])
    proj = jnp.einsum('bsd,de->bse', h, w_in[l])
    u_f, c_q, c_kv, k_r, q_d, k_d, v_d, gate = jnp.split(proj, IN_SPLITS, axis=-1)

    uf = u_f.reshape(B, S, F_GROUPS, F_GROUP_DIM).astype(jnp.float32)
    f = jnp.fft.fft2(uf, axes=(1, 3), norm='ortho').real.astype(dt)
    y_f = jnp.einsum('bsgc,gcd->bsgd', f, w_fmix[l]).reshape(B, S, F_WIDTH)

    cos, sin = rope_tables(S)
    cq = rmsnorm(c_q, q_norm[l])
    q = jnp.einsum('bsr,re->bse', cq, w_uq[l]).reshape(B, S, MLA_HEADS, QK_NOPE + QK_ROPE)
    q_nope = q[..., :QK_NOPE]
    q_rope = apply_rope(q[..., QK_NOPE:], cos[:, None, :], sin[:, None, :])
    ckv = rmsnorm(c_kv, kv_norm[l])
    kv = jnp.einsum('bsr,re->bse', ckv, w_ukv[l]).reshape(B, S, MLA_HEADS, QK_NOPE + V_HEAD)
    k_nope = kv[..., :QK_NOPE]
    v_mla = kv[..., QK_NOPE:]
    k_rope = apply_rope(k_r, cos, sin)
    mla_scale = 1.0 / math.sqrt(QK_NOPE + QK_ROPE)

    def attend_mla(qb, pb):
        qn, qr = qb
        s = (jnp.einsum('bthd,bshd->bhts', qn, k_nope)
             + jnp.einsum('bthd,bsd->bhts', qr, k_rope)).astype(jnp.float32) * mla_scale
        p = jax.nn.softmax(s, axis=-1)
        return jnp.einsum('bhts,bshd->bthd', p.astype(dt), v_mla)

    y_mla = sweep_queries(attend_mla, (q_nope, q_rope), S).reshape(B, S, MLA_WIDTH)

    qd = q_d.reshape(B, S, DIFF_HEADS, 2, DIFF_QK)
    kd = k_d.reshape(B, S, DIFF_HEADS, 2, DIFF_QK)
    q1, q2 = qd[..., 0, :], qd[..., 1, :]
    k1, k2 = kd[..., 0, :], kd[..., 1, :]
    vd = v_d.reshape(B, S, DIFF_HEADS, DIFF_V)
    lam_init = 0.8 - 0.6 * math.exp(-0.3 * l)
    lam = (jnp.exp(jnp.sum(lam_q1[l].astype(jnp.float32) * lam_k1[l].astype(jnp.float32)))
           - jnp.exp(jnp.sum(lam_q2[l].astype(jnp.float32) * lam_k2[l].astype(jnp.float32)))
           + lam_init)
    diff_scale = 1.0 / math.sqrt(DIFF_QK)
    kpos = jnp.arange(S, dtype=jnp.int32)
    table = rel_bias.astype(jnp.float32)

    def attend_diff(qb, pb):
        qa, qb2 = qb
        bucket = t5_bucket(kpos[None, :] - pb[:, None])
        bias = jnp.transpose(table[bucket], (2, 0, 1))[None]
        s1 = jnp.einsum('bthd,bshd->bhts', qa, k1).astype(jnp.float32) * diff_scale + bias
        s2 = jnp.einsum('bthd,bshd->bhts', qb2, k2).astype(jnp.float32) * diff_scale + bias
        a = jax.nn.softmax(s1, axis=-1) - lam * jax.nn.softmax(s2, axis=-1)
        return jnp.einsum('bhts,bshd->bthd', a.astype(dt), vd)

    o_d = sweep_queries(attend_diff, (q1, q2), S)
    o_d = rmsnorm(o_d, diff_norm[l], eps=1e-5) * jnp.asarray(1.0 - lam_init, dt)
    y_d = o_d.reshape(B, S, DIFF_WIDTH)

    y = jnp.concatenate([y_f, y_mla, y_d], axis=-1) * jax.nn.silu(gate)
    return x + jnp.einsum('bse,ed->bsd', y, w_o[l])


def encode(x, meta_tokens, rel_bias, final_norm, norm_w, w_in, w_fmix, q_norm, w_uq,
           kv_norm, w_ukv, lam_q1, lam_k1, lam_q2, lam_k2, diff_norm, w_o):
    B = x.shape[0]
    meta = jnp.broadcast_to(meta_tokens.astype(x.dtype)[None], (B, N_META, D_MODEL))
    h = jnp.concatenate([meta, x], axis=1)
    for l in range(DEPTH):
        h = mixer_layer(h, l, rel_bias, norm_w, w_in, w_fmix, q_norm, w_uq, kv_norm, w_ukv,
                        lam_q1, lam_k1, lam_q2, lam_k2, diff_norm, w_o)
    h = rmsnorm(h, final_norm)
    return h[:, N_META:]


def setup_inputs(seed: int = 0) -> dict:
    key = jax.random.key(seed)
    ks = jax.random.split(key, 20)
    nrm = jax.random.normal
    f32 = jnp.float32
    return {
        'x_prompt': nrm(ks[0], (BATCH, SEQ, D_MODEL), f32),
        'x_sample': nrm(ks[1], (DEC_BATCH, DEC_SEQ, D_MODEL), f32),
        'meta_tokens': nrm(ks[2], (N_META, D_MODEL), f32),
        'rel_bias': 0.1 * nrm(ks[3], (REL_BUCKETS, DIFF_HEADS), f32),
        'final_norm': 1.0 + 0.01 * nrm(ks[4], (D_MODEL,), f32),
        'norm_w': 1.0 + 0.01 * nrm(ks[5], (DEPTH, D_MODEL), f32),
        'w_in': nrm(ks[6], (DEPTH, D_MODEL, IN_WIDTH), f32) * D_MODEL ** -0.5,
        'w_fmix': nrm(ks[7], (DEPTH, F_GROUPS, F_GROUP_DIM, F_GROUP_DIM), f32) * F_GROUP_DIM ** -0.5,
        'q_norm': 1.0 + 0.01 * nrm(ks[8], (DEPTH, Q_LORA), f32),
        'w_uq': nrm(ks[9], (DEPTH, Q_LORA, MLA_HEADS * (QK_NOPE + QK_ROPE)), f32) * Q_LORA ** -0.5,
        'kv_norm': 1.0 + 0.01 * nrm(ks[10], (DEPTH, KV_LORA), f32),
        'w_ukv': nrm(ks[11], (DEPTH, KV_LORA, MLA_HEADS * (QK_NOPE + V_HEAD)), f32) * KV_LORA ** -0.5,
        'lam_q1': 0.1 * nrm(ks[12], (DEPTH, DIFF_QK), f32),
        'lam_k1': 0.1 * nrm(ks[13], (DEPTH, DIFF_QK), f32),
        'lam_q2': 0.1 * nrm(ks[14], (DEPTH, DIFF_QK), f32),
        'lam_k2': 0.1 * nrm(ks[15], (DEPTH, DIFF_QK), f32),
        'diff_norm': 1.0 + 0.01 * nrm(ks[16], (DEPTH, DIFF_V), f32),
        'w_o': nrm(ks[17], (DEPTH, MIX_WIDTH, D_MODEL), f32) * MIX_WIDTH ** -0.5,
    }


def reference(x_prompt, x_sample, meta_tokens, rel_bias, final_norm, norm_w, w_in, w_fmix,
              q_norm, w_uq, kv_norm, w_ukv, lam_q1, lam_k1, lam_q2, lam_k2, diff_norm, w_o):
    y_prompt = encode(x_prompt, meta_tokens, rel_bias, final_norm, norm_w, w_in, w_fmix,
                      q_norm, w_uq, kv_norm, w_ukv, lam_q1, lam_k1, lam_q2, lam_k2, diff_norm, w_o)
    y_sample = encode(x_sample, meta_tokens, rel_bias, final_norm, norm_w, w_in, w_fmix,
                      q_norm, w_uq, kv_norm, w_ukv, lam_q1, lam_k1, lam_q2, lam_k2, diff_norm, w_o)
    return (y_prompt, y_sample)
```

```python
import math
from contextlib import ExitStack
import numpy as np
import concourse.bass as bass
import concourse.mybir as mybir
from concourse.bass_utils import run_bass_kernel_spmd

F32 = mybir.dt.float32
BF16 = mybir.dt.bfloat16
I32 = mybir.dt.int32
AF = mybir.ActivationFunctionType
ALU = mybir.AluOpType

D = 2048
N_META = 16
NL = 2
INW = 5440
C_UF, C_CQ, C_CKV, C_KR, C_QD, C_KD, C_VD, C_G = 0, 512, 1280, 1792, 1856, 2368, 2880, 3392
EPS = 1e-6
RMAX = 768
WFULL = 1600
PI_S = 3.14159


class Buf:
    def __init__(self, name, excl=False):
        self.name = name
        self.excl = excl
        self.writers = {}
        self.readers = {}


def PB(name):
    return Buf(name, True)


class Op:
    __slots__ = ("eng", "fn", "deps", "sig", "seq", "key", "cum")


SEM_SKIP = [0]
EMIT_N = [0]
DBG = {}
MAXQ = [4]
ENGS = ["pe", "act", "dve", "pool", "sp"]


class Prog:
    def __init__(self):
        self.ops = {e: [] for e in ENGS}
        self.keycum = {}

    def op(self, eng, fn, reads=(), writes=(), key=None):
        o = Op()
        o.eng, o.fn, o.deps, o.sig, o.key, o.seq, o.cum = eng, fn, [], False, key, 0, 0
        if key is not None:
            self.keycum[key] = self.keycum.get(key, 0) + 16
            o.cum = self.keycum[key]
        ex = [b for b in reads if b.excl]
        if ex:
            reads = [b for b in reads if not b.excl]
            writes = list(writes) + [b for b in ex if b not in writes]
        for b in reads:
            for w in b.writers.values():
                self._dep(o, w)
        for b in writes:
            for r in b.readers.values():
                self._dep(o, r)
            for w in b.writers.values():
                self._dep(o, w)
        for b in reads:
            b.readers[(eng, key)] = o
        for b in writes:
            b.writers[(eng, key)] = o
            b.readers = {}
        self.ops[eng].append(o)
        return o

    def _dep(self, o, p):
        if p is o:
            return
        if p.key is None and o.key is None and p.eng == o.eng and p.eng == "pe":
            return
        if p.key is None and p.eng == o.eng:
            pass
        o.deps.append(p)
        p.sig = True

    def emit(self, nc, name):
        EMIT_N[0] += 1
        name = f"{name}_{EMIT_N[0]}"
        with ExitStack() as es:
            engsem = {e: nc.alloc_semaphore(name=f"{name}_s_{e}") for e in ENGS}
            keysem = {k: nc.alloc_semaphore(name=f"{name}_k{i}") for i, k in enumerate(self.keycum)}
            allsems = list(engsem.values()) + list(keysem.values())
            for e in ENGS:
                c = 0
                for o in self.ops[e]:
                    if o.key is None and o.sig:
                        c += 1
                        o.seq = c
            block = es.enter_context(nc.Block())

            def run(E, e):
                waited = {}
                inflight = []
                for o in self.ops[e]:
                    for p in o.deps:
                        if p.key is not None:
                            sid, sem, val = ("k", p.key), keysem[p.key], p.cum
                        else:
                            sid, sem, val = ("e", p.eng), engsem[p.eng], p.seq
                        if waited.get(sid, 0) < val:
                            E.wait_ge(sem, val)
                            waited[sid] = val
                    if o.key is not None and len(inflight) >= MAXQ[0]:
                        k0_, c0_ = inflight.pop(0)
                        if waited.get(("k", k0_), 0) < c0_:
                            E.wait_ge(keysem[k0_], c0_)
                            waited[("k", k0_)] = c0_
                    inst = o.fn(E)
                    if o.key is not None:
                        inflight.append((o.key, o.cum))
                        inst.then_inc(keysem[o.key], 16)
                    elif o.sig:
                        inst.then_inc(engsem[e], 1)
                if e == "sp":
                    for k, tot in self.keycum.items():
                        if waited.get(("k", k), 0) < tot:
                            E.wait_ge(keysem[k], tot)

            @block.tensor
            def _(E):
                run(E, "pe")

            @block.scalar
            def _(E):
                run(E, "act")

            @block.vector
            def _(E):
                run(E, "dve")

            @block.gpsimd
            def _(E):
                run(E, "pool")

            @block.sync
            def _(E):
                run(E, "sp")
        nc.clear_and_free_semaphores(allsems)
        nc.all_engine_barrier()


def blocks(n, size):
    return [(s, min(size, n - s)) for s in range(0, n, size)]


class Ctx:
    pass


def t5_bucket_np(rel):
    nb = 16
    max_exact = 8
    ret = (rel > 0).astype(np.int64) * nb
    n = np.abs(rel)
    nf = np.maximum(n, 1).astype(np.float32)
    large = max_exact + (np.log(nf / np.float32(max_exact)) / np.float32(math.log(128 / max_exact))
                         * np.float32(nb - max_exact)).astype(np.int32)
    large = np.minimum(large, nb - 1)
    return ret + np.where(n < max_exact, n, large)


def host_consts(T):
    pos = np.arange(T, dtype=np.float32)
    inv = (10000.0 ** (-np.arange(0, 64, 2, dtype=np.float32) / 64)).astype(np.float32)
    ang = pos[:, None] * inv[None, :]
    cos, sin = np.cos(ang).astype(np.float32).T, np.sin(ang).astype(np.float32).T
    cos2 = np.concatenate([cos, cos], 0)
    sin2 = np.concatenate([-sin, sin], 0)
    return np.ascontiguousarray(cos2), np.ascontiguousarray(sin2)


def dft_consts(T):
    s = np.arange(T, dtype=np.int64)
    j = np.arange(512, dtype=np.int64)
    a = 2 * np.pi * ((s[:, None] * j[None, :]) % T).astype(np.float64) / T
    tj = np.stack([np.cos(a), np.sin(a)], 1).astype(np.float32)
    k0 = np.arange(0, T, 512, dtype=np.int64)
    a = 2 * np.pi * ((s[:, None] * k0[None, :]) % T).astype(np.float64) / T
    tk = np.stack([np.cos(a), np.sin(a)], 2).astype(np.float32)
    return np.ascontiguousarray(tj), np.ascontiguousarray(tk)


def misc_consts():
    ident = np.eye(128, dtype=np.float32)
    c = np.arange(128)
    ang = 2 * np.pi * np.outer(c, c) / 128.0
    cs = np.concatenate([np.cos(ang), np.sin(ang)], 1).astype(np.float32)
    n = np.arange(WFULL)
    rel = RMAX - n
    bk = t5_bucket_np(rel)
    oh = np.zeros((32, WFULL), np.float32)
    oh[bk, n] = 8.0
    return ident, cs, oh


def build(segT, debug=False):
    nc = bass.Bass("TRN2", target_bir_lowering=False)
    G = Ctx()
    nseg = len(segT)

    def din(name, shape, dt=F32):
        return nc.dram_tensor(name, list(shape), dt, kind="ExternalInput").ap()

    def dsc(name, shape, dt=BF16):
        t = nc.dram_tensor(name, list(shape), dt, kind=("ExternalOutput" if debug else "Internal")).ap()
        dbg[name] = t
        return t

    x_in = [din(f"x{s}", [segT[s], D]) for s in range(nseg)]
    y_out = [nc.dram_tensor(f"y{s}", [segT[s] - N_META, D], F32, kind="ExternalOutput").ap() for s in range(nseg)]
    cos2 = [din(f"cos2_{s}", [64, segT[s]]) for s in range(nseg)]
    sin2 = [din(f"sin2_{s}", [64, segT[s]]) for s in range(nseg)]
    tjd = [din(f"tj_{s}", [segT[s], 2, 512]) for s in range(nseg)]
    tkd = [din(f"tk_{s}", [segT[s], len(blocks(segT[s], 512)), 2]) for s in range(nseg)]
    ident_d = din("ident", [128, 128])
    cs_d = din("cs", [128, 256])
    oh_d = din("oh", [32, WFULL])
    rel_bias = din("rel_bias", [32, 4])
    final_norm = din("final_norm", [D])
    norm_w = din("norm_w", [NL, D])
    w_in = din("w_in", [NL, D, INW])
    w_fmix = din("w_fmix", [NL, 4, 128, 128])
    q_norm = din("q_norm", [NL, 768])
    w_uq = din("w_uq", [NL, 768, 1536])
    kv_norm = din("kv_norm", [NL, 512])
    w_ukv = din("w_ukv", [NL, 512, 2048])
    lam_in = [din(n, [NL, 64]) for n in ("lam_q1", "lam_k1", "lam_q2", "lam_k2")]
    diff_norm = din("diff_norm", [NL, 128])
    w_o = din("w_o", [NL, D, D])

    dbg = {}
    S = []
    for s in range(nseg):
        T = segT[s]
        c = Ctx()
        c.T = T
        c.x1 = dsc(f"x1_{s}", [T, D], F32)
        c.AB = dsc(f"AB_{s}", [T, 4, 256])
        c.cqg = dsc(f"cqg_{s}", [6, 128, T])
        c.cqs = dsc(f"cqs_{s}", [6, 128, T])
        c.ckg = dsc(f"ckg_{s}", [4, 128, T])
        c.cks = dsc(f"cks_{s}", [4, 128, T])
        c.KrT = dsc(f"KrT_{s}", [64, T])
        c.qdT = dsc(f"qdT_{s}", [8, 64, T])
        c.kdT = dsc(f"kdT_{s}", [8, 64, T])
        c.Vd = dsc(f"Vd_{s}", [T, 512])
        c.gT = dsc(f"gT_{s}", [16, 128, T])
        c.QnT = dsc(f"QnT_{s}", [8, 128, T])
        c.QrT = dsc(f"QrT_{s}", [8, 64, T])
        c.KnT = dsc(f"KnT_{s}", [8, 128, T])
        c.V = dsc(f"V_{s}", [T, 1024])
        c.yT = dsc(f"yT_{s}", [16, 128, T])
        c.CT = dsc(f"CT_{s}", [T, T])
        c.NST = dsc(f"NST_{s}", [T, T])
        S.append(c)
    Gb = dsc("Gb", [4, 132, WFULL])

    uid = [0]

    def sb(es, name, shape, dt):
        uid[0] += 1
        return es.enter_context(nc.sbuf_tensor(f"sb{uid[0]}_{name}", list(shape), dt))

    def ps(es, name, shape=(128, 512), dt=F32):
        uid[0] += 1
        return es.enter_context(nc.psum_tensor(f"ps{uid[0]}_{name}", list(shape), dt))

    nc_allow = nc.allow_non_contiguous_dma("small strided const loads")
    nc_allow.__enter__()
    nc_lp = nc.allow_low_precision("bf16 matmul operands by design")
    nc_lp.__enter__()

    def phase_tables(si, variant=""):
        c = S[si]
        T = c.T
        nkb = len(blocks(T, 512))
        with ExitStack() as es:
            P = Prog()
            tj = [sb(es, f"tj{i}", [128, 2, 512], F32) for i in range(2)]
            tk = [sb(es, f"tk{i}", [128, nkb, 2], F32) for i in range(2)]
            ntk = [sb(es, f"ntk{i}", [128, nkb, 2], F32) for i in range(2)]
            tmp = [sb(es, f"tmp{i}", [128, 2, 512], F32) for i in range(2)]
            tm2 = [sb(es, f"tm2{i}", [128, 2, 512], F32) for i in range(2)]
            ot = [sb(es, f"ot{i}", [128, 2, 512], BF16) for i in range(2)]
            Btj = [Buf("tj0"), Buf("tj1")]
            Btk = [Buf("tk0"), Buf("tk1")]
            Btmp = [Buf("tmp0"), Buf("tmp1")]
            Bot = [Buf("ot0"), Buf("ot1")]
            it = 0
            for si_, (s0, ns) in enumerate(blocks(T, 128)):
                a_ = si_ % 2
                P.op("sp", lambda E, a_=a_, s0=s0, ns=ns: E.dma_start(out=tj[a_][:ns], in_=tjd[si][s0:s0 + ns]), writes=[Btj[a_]], key=f"tj{a_}")
                if "B" in variant:
                    P.op("dve", lambda E, a_=a_, ns=ns: E.memset(tk[a_][:ns], 0.5), writes=[Btk[a_]])
                else:
                    P.op("sp", lambda E, a_=a_, s0=s0, ns=ns: E.dma_start(out=tk[a_][:ns], in_=tkd[si][s0:s0 + ns]), writes=[Btk[a_]], key=f"tk{a_}")
                P.op("dve", lambda E, a_=a_, ns=ns: E.tensor_scalar(ntk[a_][:ns], tk[a_][:ns], -1.0, None, ALU.mult), reads=[Btk[a_]], writes=[Btk[a_]])
                for kb, (k0, nk) in enumerate(blocks(T, 512)):
                    b = it % 2
                    it += 1
                    ck, sk = tk[a_][:ns, kb, 0:1], tk[a_][:ns, kb, 1:2]
                    nck, nsk = ntk[a_][:ns, kb, 0:1], ntk[a_][:ns, kb, 1:2]
                    P.op("dve", lambda E, a_=a_, b=b, ns=ns, nk=nk, ck=ck: E.tensor_scalar(tmp[b][:ns, 0, :nk], tj[a_][:ns, 0, :nk], ck, None, ALU.mult),
                         reads=[Btj[a_], Btk[a_]], writes=[Btmp[b]])
                    P.op("dve", lambda E, a_=a_, b=b, ns=ns, nk=nk, nsk=nsk: E.tensor_scalar(tm2[b][:ns, 0, :nk], tj[a_][:ns, 1, :nk], nsk, None, ALU.mult),
                         reads=[Btj[a_], Btk[a_]], writes=[Btmp[b]])
                    P.op("dve", lambda E, a_=a_, b=b, ns=ns, nk=nk: E.tensor_tensor(ot[b][:ns, 0, :nk], tm2[b][:ns, 0, :nk], tmp[b][:ns, 0, :nk], ALU.add),
                         reads=[Btmp[b]], writes=[Bot[b]])
                    P.op("dve", lambda E, a_=a_, b=b, ns=ns, nk=nk, nck=nck: E.tensor_scalar(tmp[b][:ns, 1, :nk], tj[a_][:ns, 1, :nk], nck, None, ALU.mult),
                         reads=[Btj[a_], Btk[a_]], writes=[Btmp[b]])
                    P.op("dve", lambda E, a_=a_, b=b, ns=ns, nk=nk, nsk=nsk: E.tensor_scalar(tm2[b][:ns, 1, :nk], tj[a_][:ns, 0, :nk], nsk, None, ALU.mult),
                         reads=[Btj[a_], Btk[a_]], writes=[Btmp[b]])
                    P.op("dve", lambda E, a_=a_, b=b, ns=ns, nk=nk: E.tensor_tensor(ot[b][:ns, 1, :nk], tm2[b][:ns, 1, :nk], tmp[b][:ns, 1, :nk], ALU.add),
                         reads=[Btmp[b]], writes=[Bot[b]])
                    P.op("sp", lambda E, b=b, s0=s0, ns=ns, k0=k0, nk=nk: E.dma_start(out=c.CT[s0:s0 + ns, k0:k0 + nk], in_=ot[b][:ns, 0, :nk]),
                         reads=[Bot[b]], key=f"c{b}")
                    if "A" not in variant:
                        P.op("sp", lambda E, b=b, s0=s0, ns=ns, k0=k0, nk=nk: E.dma_start(out=c.NST[s0:s0 + ns, k0:k0 + nk], in_=ot[b][:ns, 1, :nk]),
                             reads=[Bot[b]], key=f"n{b}")
            P.emit(nc, f"tb{si}")

    def phase_bias():
        with ExitStack() as es:
            P = Prog()
            oh = sb(es, "oh", [32, WFULL], F32)
            rb = sb(es, "rb", [32, 4], F32)
            ohh = sb(es, "ohh", [32, WFULL], F32)
            one = sb(es, "one32", [32, 128], F32)
            gsb = sb(es, "gsb", [128, WFULL], BF16)
            pss = [ps(es, f"pb{i}") for i in range(4)]
            Boh, Brb, Bohh, Bone, Bg = Buf("oh"), Buf("rb"), Buf("ohh"), Buf("one"), Buf("g")
            Bps = [PB(f"pb{i}") for i in range(4)]
            P.op("sp", lambda E: E.dma_start(out=oh[:], in_=oh_d[:, :]), writes=[Boh], key="l0")
            P.op("sp", lambda E: E.dma_start(out=rb[:], in_=rel_bias[:, :]), writes=[Brb], key="l1")
            P.op("dve", lambda E: E.memset(one[:], 1.0), writes=[Bone])
            for h in range(4):
                P.op("dve", lambda E, h=h: E.tensor_scalar(ohh[:], oh[:], rb[:, h:h + 1], None, ALU.mult), reads=[Boh, Brb], writes=[Bohh])
                for i, (n0, nn) in enumerate(blocks(WFULL, 512)):
                    P.op("pe", lambda E, i=i, n0=n0, nn=nn: E.matmul(pss[i][:, :nn], one[:, :], ohh[:, n0:n0 + nn], start=True, stop=True),
                         reads=[Bohh, Bone], writes=[Bps[i]])
                    P.op("act", lambda E, i=i, n0=n0, nn=nn: E.activation(gsb[:, n0:n0 + nn], pss[i][:, :nn], AF.Copy), reads=[Bps[i]], writes=[Bg])
                P.op("sp", lambda E, h=h: E.dma_start(out=Gb[h, 0:128, :], in_=gsb[:]), reads=[Bg], key="st")
            P.emit(nc, "bias")

    SG = 2064

    def phase_inproj(l, si, xsrc, variant=""):
        c = S[si]
        T = c.T
        w_l = w_in[l].rearrange("(c p) n -> p c n", p=128)
        chunks = [("uf", C_UF, 512), ("cq", C_CQ, 512), ("cq", C_CQ + 512, 256), ("ckv", C_CKV, 512), ("kr", C_KR, 64),
                  ("qd", C_QD, 512), ("kd", C_KD, 512), ("vd", C_VD, 512)] + [("gate", C_G + 512 * i, 512) for i in range(4)]
        with ExitStack() as es:
            P = Prog()
            hT = sb(es, "hT", [128, 16, SG], BF16)
            wb = [sb(es, f"wb{i}", [128, 16, 512], BF16) for i in range(2)]
            wsw = sb(es, "wsw", [128, 16, 64], BF16)
            wf = sb(es, "wf", [128, 16, 512], F32)
            Bwf = Buf("wf")
            xt = [sb(es, f"xt{i}", [128, D], F32) for i in range(2)]
            hb = [sb(es, f"hb{i}", [128, D], BF16) for i in range(2)]
            junk = sb(es, "junk", [128, D], BF16)
            gt = sb(es, "gt", [128, D], F32)
            grow = sb(es, "grow", [1, D], F32)
            one1 = sb(es, "one1", [1, 128], F32)
            ss = [sb(es, f"ss{i}", [128, 2], F32) for i in range(2)]
            identf = sb(es, "identf", [128, 128], F32)
            ident = sb(es, "ident", [128, 128], BF16)
            csf = sb(es, "csf", [128, 256], F32)
            csb = sb(es, "csb", [128, 256], BF16)
            gq = sb(es, "gq", [128, 6], F32)
            gk = sb(es, "gk", [128, 4], F32)
            cst = [sb(es, f"cst{i}", [64, 512], F32) for i in range(2)]
            snt = [sb(es, f"snt{i}", [64, 512], F32) for i in range(2)]
            uT = [sb(es, f"uT{i}", [128, 512], BF16) for i in range(2)]
            stA = [sb(es, f"stA{i}", [128, 512], BF16) for i in range(3)]
            stB = [sb(es, f"stB{i}", [128, 512], BF16) for i in range(3)]
            r1 = [sb(es, f"r1_{i}", [64, 512], F32) for i in range(2)]
            r2 = [sb(es, f"r2_{i}", [64, 512], F32) for i in range(2)]
            pT = [ps(es, f"pT{i}", (128, 1024), BF16) for i in range(2)]
            pm = [ps(es, f"pm{i}") for i in range(4)]
            p2 = [ps(es, f"p2{i}") for i in range(2)]
            BhT = Buf("hT")
            Bwb = [Buf("wb0"), Buf("wb1")]
            Bwsw = Buf("wsw")
            Bxt = [Buf("xt0"), Buf("xt1")]
            Bhb = [Buf("hb0"), Buf("hb1")]
            Bjunk, Bgt, Bgrow, Bone1 = Buf("junk"), Buf("gt"), Buf("grow"), Buf("one1")
            Bss = [Buf("ss0"), Buf("ss1")]
            Bid, Bcs, Bgq = Buf("id"), Buf("cs"), Buf("gq")
            Bcst = [Buf("cst0"), Buf("cst1")]
            BuT = [Buf("uT0"), Buf("uT1")]
            BstA = [Buf(f"stA{i}") for i in range(3)]
            BstB = [Buf(f"stB{i}") for i in range(3)]
            Br = [Buf("r0"), Buf("r1")]
            BpT = [PB("pT0"), PB("pT1")]
            Bpm = [PB(f"pm{i}") for i in range(4)]
            Bp2 = [PB("p20"), PB("p21")]
            P.op("sp", lambda E: E.dma_start(out=identf[:], in_=ident_d[:, :]), writes=[Bid], key="c0")
            P.op("dve", lambda E: E.tensor_copy(ident[:], identf[:]), reads=[Bid], writes=[Bid])
            P.op("sp", lambda E: E.dma_start(out=csf[:], in_=cs_d[:, :]), writes=[Bcs], key="c1")
            P.op("dve", lambda E: E.tensor_copy(csb[:], csf[:]), reads=[Bcs], writes=[Bcs])
            P.op("sp", lambda E: E.dma_start(out=gq[:], in_=q_norm[l].rearrange("(c p) -> p c", p=128), allow_slow_non_contiguous=True), writes=[Bgq], key="c2")
            P.op("sp", lambda E: E.dma_start(out=gk[:], in_=kv_norm[l].rearrange("(c p) -> p c", p=128), allow_slow_non_contiguous=True), writes=[Bgq], key="c3")
            P.op("sp", lambda E: E.dma_start(out=grow[:], in_=norm_w[l:l + 1, :]), writes=[Bgrow], key="c4")
            P.op("dve", lambda E: E.memset(one1[:], 1.0), writes=[Bone1])
            for i in range(4):
                P.op("pe", lambda E, i=i: E.matmul(pm[i][:, :], one1[:, :], grow[:, i * 512:(i + 1) * 512], start=True, stop=True),
                     reads=[Bone1, Bgrow], writes=[Bpm[i]])
                P.op("dve", lambda E, i=i: E.tensor_copy(gt[:, i * 512:(i + 1) * 512], pm[i][:, :]), reads=[Bpm[i]], writes=[Bgt])
            cnt = {"x": 0, "w": 0, "pm": 0, "p2": 0, "sa": 0, "sb": 0, "u": 0, "r": 0, "cs": 0, "pT": 0}

            def nxt(k, n):
                v = cnt[k] % n
                cnt[k] += 1
                return v

            for (g0, gn) in blocks(T, SG):
                for (t0, tn) in blocks(gn, 128):
                    b = nxt("x", 2)
                    P.op("sp", lambda E, b=b, t0=t0, tn=tn, g0=g0: E.dma_start(out=xt[b][:tn, :], in_=xsrc[g0 + t0:g0 + t0 + tn, :]),
                         writes=[Bxt[b]], key=f"x{b}")
                    P.op("act", lambda E, b=b, tn=tn: E.activation(junk[:tn, :], xt[b][:tn, :], AF.Square, accum_out=ss[b][:tn, 0:1]),
                         reads=[Bxt[b]], writes=[Bjunk, Bss[b]])
                    P.op("dve", lambda E, b=b, tn=tn: E.tensor_scalar(ss[b][:tn, 1:2], ss[b][:tn, 0:1], 1.0 / D, EPS, ALU.mult, ALU.add),
                         reads=[Bss[b]], writes=[Bss[b]])
                    P.op("act", lambda E, b=b, tn=tn: E.activation(ss[b][:tn, 1:2], ss[b][:tn, 1:2], AF.Sqrt), reads=[Bss[b]], writes=[Bss[b]])
                    P.op("dve", lambda E, b=b, tn=tn: E.reciprocal(ss[b][:tn, 1:2], ss[b][:tn, 1:2]), reads=[Bss[b]], writes=[Bss[b]])
                    P.op("dve", lambda E, b=b, tn=tn: E.tensor_scalar(xt[b][:tn, :], xt[b][:tn, :], ss[b][:tn, 1:2], None, ALU.mult),
                         reads=[Bxt[b], Bss[b]], writes=[Bxt[b]])
                    P.op("dve", lambda E, b=b, tn=tn: E.tensor_tensor(hb[b][:tn, :], xt[b][:tn, :], gt[:tn, :], ALU.mult),
                         reads=[Bxt[b], Bgt], writes=[Bhb[b]])
                    for half in range(2):
                        pb = nxt("pT", 2)
                        for cc in range(8):
                            ch = half * 8 + cc
                            P.op("pe", lambda E, b=b, pb=pb, cc=cc, ch=ch, tn=tn: E.transpose(pT[pb][:, cc * 128:cc * 128 + tn], hb[b][:tn, ch * 128:(ch + 1) * 128], ident[:tn, :tn]),
                                 reads=[Bhb[b], Bid], writes=[BpT[pb]])
                        src = lambda pb=pb, tn=tn: pT[pb][:, :].rearrange("p (c t) -> p c t", t=128)[:, :, :tn]
                        dst = lambda half=half, t0=t0, tn=tn: hT[:, half * 8:half * 8 + 8, t0:t0 + tn]
                        if False:
                            P.op("act", lambda E, src=src, dst=dst: E.activation(dst(), src(), AF.Copy), reads=[BpT[pb]], writes=[BhT])
                        else:
                            P.op("dve", lambda E, src=src, dst=dst: E.tensor_copy(dst(), src()), reads=[BpT[pb]], writes=[BhT])
                for (kind, col0, ncols) in chunks:
                    if variant and kind not in variant.split(","):
                        continue
                    wbi = nxt("w", 2)
                    for hf in range(2):
                        P.op("sp", lambda E, col0=col0, ncols=ncols, hf=hf: E.dma_start(out=wf[:, hf * 8:hf * 8 + 8, :ncols], in_=w_l[:, hf * 8:hf * 8 + 8, col0:col0 + ncols]),
                             writes=[Bwf], key="wf")
                    P.op("dve", lambda E, wbi=wbi, ncols=ncols: E.tensor_copy(wb[wbi][:, :, :ncols], wf[:, :, :ncols]), reads=[Bwf], writes=[Bwb[wbi]])
                    if kind == "kr":
                        P.op("dve", lambda E: E.tensor_copy(wsw[:, :, 0:32], wf[:, :, 32:64]), reads=[Bwf], writes=[Bwsw])
                        P.op("dve", lambda E: E.tensor_copy(wsw[:, :, 32:64], wf[:, :, 0:32]), reads=[Bwf], writes=[Bwsw])
                    for (q0, qn) in blocks(gn, 512):
                        tg = g0 + q0
                        if kind == "vd":
                            for (t0, tn) in blocks(qn, 128):
                                pb = nxt("pm", 4)
                                for ch in range(16):
                                    P.op("pe", lambda E, pb=pb, ch=ch, tn=tn, a=q0 + t0, wbi=wbi: E.matmul(pm[pb][:tn, :512], hT[:, ch, a:a + tn], wb[wbi][:, ch, :512], start=(ch == 0), stop=(ch == 15)),
                                         reads=[BhT, Bwb[wbi]], writes=[Bpm[pb]])
                                sa = nxt("sa", 3)
                                P.op("act", lambda E, pb=pb, sa=sa, tn=tn: E.activation(stA[sa][:tn, :], pm[pb][:tn, :], AF.Copy), reads=[Bpm[pb]], writes=[BstA[sa]])
                                P.op("sp", lambda E, sa=sa, tn=tn, a=tg + t0: E.dma_start(out=c.Vd[a:a + tn, :], in_=stA[sa][:tn, :]), reads=[BstA[sa]], key=f"sa{sa}")
                            continue
                        if kind == "kr":
                            cb = nxt("cs", 2)
                            P.op("sp", lambda E, cb=cb, qn=qn, tg=tg: E.dma_start(out=cst[cb][:, :qn], in_=cos2[si][:, tg:tg + qn]), writes=[Bcst[cb]], key=f"cs{cb}")
                            P.op("sp", lambda E, cb=cb, qn=qn, tg=tg: E.dma_start(out=snt[cb][:, :qn], in_=sin2[si][:, tg:tg + qn]), writes=[Bcst[cb]], key=f"sn{cb}")
                            pu, pv = nxt("pm", 4), nxt("pm", 4)
                            for ch in range(16):
                                P.op("pe", lambda E, pu=pu, ch=ch, qn=qn, q0=q0, wbi=wbi: E.matmul(pm[pu][:64, :qn], wb[wbi][:, ch, 0:64], hT[:, ch, q0:q0 + qn], start=(ch == 0), stop=(ch == 15)),
                                     reads=[BhT, Bwb[wbi]], writes=[Bpm[pu]])
                            for ch in range(16):
                                P.op("pe", lambda E, pv=pv, ch=ch, qn=qn, q0=q0: E.matmul(pm[pv][:64, :qn], wsw[:, ch, 0:64], hT[:, ch, q0:q0 + qn], start=(ch == 0), stop=(ch == 15)),
                                     reads=[BhT, Bwsw], writes=[Bpm[pv]])
                            rb_ = nxt("r", 2)
                            sa = nxt("sa", 3)
                            P.op("dve", lambda E, pu=pu, rb_=rb_, cb=cb, qn=qn: E.tensor_tensor(r1[rb_][:, :qn], pm[pu][:64, :qn], cst[cb][:, :qn], ALU.mult),
                                 reads=[Bpm[pu], Bcst[cb]], writes=[Br[rb_]])
                            P.op("dve", lambda E, pv=pv, rb_=rb_, cb=cb, qn=qn: E.tensor_tensor(r2[rb_][:, :qn], pm[pv][:64, :qn], snt[cb][:, :qn], ALU.mult),
                                 reads=[Bpm[pv], Bcst[cb]], writes=[Br[rb_]])
                            P.op("dve", lambda E, rb_=rb_, sa=sa, qn=qn: E.tensor_tensor(stA[sa][:64, :qn], r1[rb_][:, :qn], r2[rb_][:, :qn], ALU.add),
                                 reads=[Br[rb_]], writes=[BstA[sa]])
                            P.op("sp", lambda E, sa=sa, qn=qn, tg=tg: E.dma_start(out=c.KrT[:, tg:tg + qn], in_=stA[sa][:64, :qn]), reads=[BstA[sa]], key=f"sa{sa}")
                            continue
                        for (m0, mn) in blocks(ncols, 128):
                            pb = nxt("pm", 4)
                            for ch in range(16):
                                P.op("pe", lambda E, pb=pb, ch=ch, qn=qn, q0=q0, m0=m0, mn=mn, wbi=wbi: E.matmul(pm[pb][:mn, :qn], wb[wbi][:, ch, m0:m0 + mn], hT[:, ch, q0:q0 + qn], start=(ch == 0), stop=(ch == 15)),
                                     reads=[BhT, Bwb[wbi]], writes=[Bpm[pb]])
                            gi = (col0 + m0)
                            if kind == "uf":
                                ub = nxt("u", 2)
                                g = m0 // 128
                                P.op("act", lambda E, pb=pb, ub=ub, qn=qn: E.activation(uT[ub][:, :qn], pm[pb][:, :qn], AF.Copy), reads=[Bpm[pb]], writes=[BuT[ub]])
                                for (t0, tn) in blocks(qn, 128):
                                    p2b = nxt("p2", 2)
                                    P.op("pe", lambda E, p2b=p2b, ub=ub, t0=t0, tn=tn: E.matmul(p2[p2b][:tn, :256], uT[ub][:, t0:t0 + tn], csb[:, :], start=True, stop=True),
                                         reads=[BuT[ub], Bcs], writes=[Bp2[p2b]])
                                    sb_ = nxt("sb", 3)
                                    P.op("dve", lambda E, p2b=p2b, sb_=sb_, tn=tn: E.tensor_copy(stB[sb_][:tn, :256], p2[p2b][:tn, :256]), reads=[Bp2[p2b]], writes=[BstB[sb_]])
                                    P.op("sp", lambda E, sb_=sb_, tn=tn, a=tg + t0, g=g: E.dma_start(out=c.AB[a:a + tn, g, :], in_=stB[sb_][:tn, :256]), reads=[BstB[sb_]], key=f"sb{sb_}")
                            elif kind in ("cq", "ckv"):
                                ci = (gi - (C_CQ if kind == "cq" else C_CKV)) // 128
                                gv = gq if kind == "cq" else gk
                                dg, dsq = (c.cqg, c.cqs) if kind == "cq" else (c.ckg, c.cks)
                                sa, sb_ = nxt("sa", 3), nxt("sb", 3)
                                P.op("dve", lambda E, pb=pb, sa=sa, qn=qn, gv=gv, ci=ci: E.tensor_scalar(stA[sa][:, :qn], pm[pb][:, :qn], gv[:, ci:ci + 1], None, ALU.mult),
                                     reads=[Bpm[pb], Bgq], writes=[BstA[sa]])
                                P.op("act", lambda E, pb=pb, sb_=sb_, qn=qn: E.activation(stB[sb_][:, :qn], pm[pb][:, :qn], AF.Square), reads=[Bpm[pb]], writes=[BstB[sb_]])
                                P.op("sp", lambda E, sa=sa, qn=qn, tg=tg, dg=dg, ci=ci: E.dma_start(out=dg[ci, :, tg:tg + qn], in_=stA[sa][:, :qn]), reads=[BstA[sa]], key=f"sa{sa}")
                                P.op("sp", lambda E, sb_=sb_, qn=qn, tg=tg, dsq=dsq, ci=ci: E.dma_start(out=dsq[ci, :, tg:tg + qn], in_=stB[sb_][:, :qn]), reads=[BstB[sb_]], key=f"sb{sb_}")
                            elif kind in ("qd", "kd"):
                                h = m0 // 128
                                dd = c.qdT if kind == "qd" else c.kdT
                                sa = nxt("sa", 3)
                                P.op("act", lambda E, pb=pb, sa=sa, qn=qn: E.activation(stA[sa][:, :qn], pm[pb][:, :qn], AF.Copy), reads=[Bpm[pb]], writes=[BstA[sa]])
                                P.op("sp", lambda E, sa=sa, qn=qn, tg=tg, dd=dd, h=h: E.dma_start(out=dd[2 * h, :, tg:tg + qn], in_=stA[sa][0:64, :qn]), reads=[BstA[sa]], key=f"sa{sa}")
                                P.op("sp", lambda E, sa=sa, qn=qn, tg=tg, dd=dd, h=h: E.dma_start(out=dd[2 * h + 1, :, tg:tg + qn], in_=stA[sa][64:128, :qn]), reads=[BstA[sa]], key=f"sa{sa}")
                            elif kind == "gate":
                                ci = (gi - C_G) // 128
                                sb_ = nxt("sb", 3)
                                P.op("act", lambda E, pb=pb, sb_=sb_, qn=qn: E.activation(stB[sb_][:, :qn], pm[pb][:, :qn], AF.Silu), reads=[Bpm[pb]], writes=[BstB[sb_]])
                                P.op("sp", lambda E, sb_=sb_, qn=qn, tg=tg, ci=ci: E.dma_start(out=c.gT[ci, :, tg:tg + qn], in_=stB[sb_][:, :qn]), reads=[BstB[sb_]], key=f"sb{sb_}")
            if debug:
                dh = dsc(f"dbg_hT{l}{si}", [128, 128])
                dw = dsc(f"dbg_wb{l}{si}", [128, 128])
                dp = dsc(f"dbg_hb{l}{si}", [128, 128])
                P.op("sp", lambda E: E.dma_start(out=dh[:, :], in_=hT[:, 0, 0:128]), reads=[BhT], key="dbg0")
                P.op("sp", lambda E: E.dma_start(out=dw[:, :], in_=wb[0][:, 0, 0:128]), reads=[Bwb[0]], key="dbg1")
                P.op("sp", lambda E: E.dma_start(out=dp[:, :], in_=hb[0][:, 0:128]), reads=[Bhb[0]], key="dbg2")
            P.emit(nc, f"ip{l}{si}")


    def rstd_ops(P, src_ap, dst_ap, mult, eps, Bsrc, Bdst):
        P.op("dve", lambda E: E.tensor_scalar(dst_ap(), src_ap(), mult, eps, ALU.mult, ALU.add), reads=[Bsrc], writes=[Bdst])
        P.op("act", lambda E: E.activation(dst_ap(), dst_ap(), AF.Sqrt), reads=[Bdst], writes=[Bdst])
        P.op("dve", lambda E: E.reciprocal(dst_ap(), dst_ap()), reads=[Bdst], writes=[Bdst])

    def phase_q(l, si):
        c = S[si]
        T = c.T
        wq_l = w_uq[l].rearrange("(c p) n -> p c n", p=128)
        with ExitStack() as es:
            P = Prog()
            wf = sb(es, "wf", [128, 6, 1536], F32)
            wq = sb(es, "wq", [128, 6, 1536], BF16)
            wsw = sb(es, "wsw", [128, 6, 8, 64], BF16)
            ones = sb(es, "ones", [128, 128], BF16)
            cg = [sb(es, f"cg{i}", [128, 6, 512], BF16) for i in range(2)]
            cq2 = [sb(es, f"cq2{i}", [128, 6, 512], BF16) for i in range(2)]
            cst = [sb(es, f"cst{i}", [64, 512], F32) for i in range(2)]
            snt = [sb(es, f"snt{i}", [64, 512], F32) for i in range(2)]
            rst = [sb(es, f"rst{i}", [128, 512], F32) for i in range(2)]
            stn = [sb(es, f"stn{i}", [128, 512], BF16) for i in range(3)]
            r1 = [sb(es, f"r1{i}", [64, 512], F32) for i in range(2)]
            r2 = [sb(es, f"r2{i}", [64, 512], F32) for i in range(2)]
            pss = ps(es, "pss")
            pn = [ps(es, f"pn{i}") for i in range(2)]
            pu = [ps(es, f"pu{i}") for i in range(2)]
            pv = [ps(es, f"pv{i}") for i in range(2)]
            Bwf, Bwq, Bones = Buf("wf"), Buf("wq"), Buf("ones")
            Bcg = [Buf("cg0"), Buf("cg1")]
            Bcs = [Buf("cs0"), Buf("cs1")]
            Brst = [Buf("rst0"), Buf("rst1")]
            Bstn = [Buf(f"stn{i}") for i in range(3)]
            Br = [Buf("r0"), Buf("r1")]
            Bpss = PB("pss")
            Bpn, Bpu, Bpv = [PB("pn0"), PB("pn1")], [PB("pu0"), PB("pu1")], [PB("pv0"), PB("pv1")]
            P.op("sp", lambda E: E.dma_start(out=wf[:], in_=wq_l), writes=[Bwf], key="wf")
            P.op("dve", lambda E: E.tensor_copy(wq[:], wf[:]), reads=[Bwf], writes=[Bwq])
            wfv = wf[:].rearrange("p c (h f) -> p c h f", f=192)
            for cc in range(6):
                P.op("dve", lambda E, cc=cc: E.tensor_copy(wsw[:, cc, :, 0:32], wfv[:, cc, :, 160:192]), reads=[Bwf], writes=[Bwq])
                P.op("dve", lambda E, cc=cc: E.tensor_copy(wsw[:, cc, :, 32:64], wfv[:, cc, :, 128:160]), reads=[Bwf], writes=[Bwq])
            P.op("dve", lambda E: E.memset(ones[:], 1.0), writes=[Bones])
            cnt = {}

            def nxt(k, n):
                v = cnt.get(k, 0)
                cnt[k] = v + 1
                return v % n

            for (q0, qn) in blocks(T, 512):
                b = nxt("b", 2)
                P.op("sp", lambda E, b=b, q0=q0, qn=qn: E.dma_start(out=cg[b][:, :, :qn], in_=c.cqg[:, :, q0:q0 + qn].rearrange("c p t -> p c t")), writes=[Bcg[b]], key=f"cg{b}")
                P.op("sp", lambda E, b=b, q0=q0, qn=qn: E.dma_start(out=cq2[b][:, :, :qn], in_=c.cqs[:, :, q0:q0 + qn].rearrange("c p t -> p c t")), writes=[Bcg[b]], key=f"cs{b}")
                P.op("sp", lambda E, b=b, q0=q0, qn=qn: E.dma_start(out=cst[b][:, :qn], in_=cos2[si][:, q0:q0 + qn]), writes=[Bcs[b]], key=f"co{b}")
                P.op("sp", lambda E, b=b, q0=q0, qn=qn: E.dma_start(out=snt[b][:, :qn], in_=sin2[si][:, q0:q0 + qn]), writes=[Bcs[b]], key=f"si{b}")
                for cc in range(6):
                    P.op("pe", lambda E, b=b, cc=cc, qn=qn: E.matmul(pss[:, :qn], ones[:, :], cq2[b][:, cc, :qn], start=(cc == 0), stop=(cc == 5)),
                         reads=[Bones, Bcg[b]], writes=[Bpss])
                rstd_ops(P, lambda qn=qn: pss[:, :qn], lambda b=b, qn=qn: rst[b][:, :qn], 1.0 / 768, EPS, Bpss, Brst[b])
                for h in range(8):
                    i = nxt("pn", 2)
                    for cc in range(6):
                        P.op("pe", lambda E, i=i, b=b, cc=cc, qn=qn, h=h: E.matmul(pn[i][:, :qn], wq[:, cc, h * 192:h * 192 + 128], cg[b][:, cc, :qn], start=(cc == 0), stop=(cc == 5)),
                             reads=[Bwq, Bcg[b]], writes=[Bpn[i]])
                    for cc in range(6):
                        P.op("pe", lambda E, i=i, b=b, cc=cc, qn=qn, h=h: E.matmul(pu[i][:64, :qn], wq[:, cc, h * 192 + 128:h * 192 + 192], cg[b][:, cc, :qn], start=(cc == 0), stop=(cc == 5)),
                             reads=[Bwq, Bcg[b]], writes=[Bpu[i]])
                    for cc in range(6):
                        P.op("pe", lambda E, i=i, b=b, cc=cc, qn=qn, h=h: E.matmul(pv[i][:64, :qn], wsw[:, cc, h, :], cg[b][:, cc, :qn], start=(cc == 0), stop=(cc == 5)),
                             reads=[Bwq, Bcg[b]], writes=[Bpv[i]])
                    s1 = nxt("st", 3)
                    P.op("dve", lambda E, i=i, b=b, s1=s1, qn=qn: E.tensor_tensor(stn[s1][:, :qn], pn[i][:, :qn], rst[b][:, :qn], ALU.mult),
                         reads=[Bpn[i], Brst[b]], writes=[Bstn[s1]])
                    P.op("sp", lambda E, s1=s1, h=h, q0=q0, qn=qn: E.dma_start(out=c.QnT[h, :, q0:q0 + qn], in_=stn[s1][:, :qn]), reads=[Bstn[s1]], key=f"st{s1}")
                    rb_ = nxt("r", 2)
                    s2 = nxt("st", 3)
                    P.op("dve", lambda E, i=i, b=b, rb_=rb_, qn=qn: E.tensor_tensor(r1[rb_][:, :qn], pu[i][:64, :qn], cst[b][:, :qn], ALU.mult),
                         reads=[Bpu[i], Bcs[b]], writes=[Br[rb_]])
                    P.op("dve", lambda E, i=i, b=b, rb_=rb_, qn=qn: E.tensor_tensor(r2[rb_][:, :qn], pv[i][:64, :qn], snt[b][:, :qn], ALU.mult),
                         reads=[Bpv[i], Bcs[b]], writes=[Br[rb_]])
                    P.op("dve", lambda E, rb_=rb_, qn=qn: E.tensor_tensor(r1[rb_][:, :qn], r1[rb_][:, :qn], r2[rb_][:, :qn], ALU.add), reads=[Br[rb_]], writes=[Br[rb_]])
                    P.op("dve", lambda E, rb_=rb_, b=b, s2=s2, qn=qn: E.tensor_tensor(stn[s2][:64, :qn], r1[rb_][:, :qn], rst[b][:64, :qn], ALU.mult),
                         reads=[Br[rb_], Brst[b]], writes=[Bstn[s2]])
                    P.op("sp", lambda E, s2=s2, h=h, q0=q0, qn=qn: E.dma_start(out=c.QrT[h, :, q0:q0 + qn], in_=stn[s2][:64, :qn]), reads=[Bstn[s2]], key=f"st{s2}")
            P.emit(nc, f"q{l}{si}")

    def phase_kv(l, si):
        c = S[si]
        T = c.T
        wk_l = w_ukv[l].rearrange("(c p) n -> p c n", p=128)
        with ExitStack() as es:
            P = Prog()
            wf = sb(es, "wf", [128, 4, 2048], F32)
            wk = sb(es, "wk", [128, 4, 1024], BF16)
            wv = sb(es, "wv", [128, 4, 1024], BF16)
            ones = sb(es, "ones", [128, 128], BF16)
            cg = [sb(es, f"cg{i}", [128, 4, 512], BF16) for i in range(2)]
            cq2 = [sb(es, f"cq2{i}", [128, 4, 512], BF16) for i in range(2)]
            rst = [sb(es, f"rst{i}", [128, 512], F32) for i in range(2)]
            rcol = [sb(es, f"rcol{i}", [128, 1], F32) for i in range(2)]
            stn = [sb(es, f"stn{i}", [128, 512], BF16) for i in range(3)]
            pss = ps(es, "pss")
            pc = ps(es, "pc")
            pk = [ps(es, f"pk{i}") for i in range(3)]
            pvv = [ps(es, f"pvv{i}") for i in range(3)]
            Bwf, Bwk, Bones = Buf("wf"), Buf("wk"), Buf("ones")
            Bcg = [Buf("cg0"), Buf("cg1")]
            Brst = [Buf("rst0"), Buf("rst1")]
            Brc = [Buf("rc0"), Buf("rc1")]
            Bstn = [Buf(f"stn{i}") for i in range(3)]
            Bpss, Bpc = PB("pss"), PB("pc")
            Bpk = [PB(f"pk{i}") for i in range(3)]
            Bpvv = [PB(f"pvv{i}") for i in range(3)]
            P.op("sp", lambda E: E.dma_start(out=wf[:], in_=wk_l), writes=[Bwf], key="wf")
            wfv = wf[:].rearrange("p c (h two f) -> p c h two f", two=2, f=128)
            for cc in range(4):
                P.op("dve", lambda E, cc=cc: E.tensor_copy(wk[:, cc, :].rearrange("p (h f) -> p h f", f=128), wfv[:, cc, :, 0, :]), reads=[Bwf], writes=[Bwk])
                P.op("dve", lambda E, cc=cc: E.tensor_copy(wv[:, cc, :].rearrange("p (h f) -> p h f", f=128), wfv[:, cc, :, 1, :]), reads=[Bwf], writes=[Bwk])
            P.op("dve", lambda E: E.memset(ones[:], 1.0), writes=[Bones])
            cnt = {}

            def nxt(k, n):
                v = cnt.get(k, 0)
                cnt[k] = v + 1
                return v % n

            for (q0, qn) in blocks(T, 512):
                b = nxt("b", 2)
                P.op("sp", lambda E, b=b, q0=q0, qn=qn: E.dma_start(out=cg[b][:, :, :qn], in_=c.ckg[:, :, q0:q0 + qn].rearrange("c p t -> p c t")), writes=[Bcg[b]], key=f"cg{b}")
                P.op("sp", lambda E, b=b, q0=q0, qn=qn: E.dma_start(out=cq2[b][:, :, :qn], in_=c.cks[:, :, q0:q0 + qn].rearrange("c p t -> p c t")), writes=[Bcg[b]], key=f"cs{b}")
                for cc in range(4):
                    P.op("pe", lambda E, b=b, cc=cc, qn=qn: E.matmul(pss[:, :qn], ones[:, :], cq2[b][:, cc, :qn], start=(cc == 0), stop=(cc == 3)),
                         reads=[Bones, Bcg[b]], writes=[Bpss])
                rstd_ops(P, lambda qn=qn: pss[:, :qn], lambda b=b, qn=qn: rst[b][:, :qn], 1.0 / 512, EPS, Bpss, Brst[b])
                for h in range(8):
                    i = nxt("pk", 3)
                    for cc in range(4):
                        P.op("pe", lambda E, i=i, b=b, cc=cc, qn=qn, h=h: E.matmul(pk[i][:, :qn], wk[:, cc, h * 128:(h + 1) * 128], cg[b][:, cc, :qn], start=(cc == 0), stop=(cc == 3)),
                             reads=[Bwk, Bcg[b]], writes=[Bpk[i]])
                    s1 = nxt("st", 3)
                    P.op("dve", lambda E, i=i, b=b, s1=s1, qn=qn: E.tensor_tensor(stn[s1][:, :qn], pk[i][:, :qn], rst[b][:, :qn], ALU.mult),
                         reads=[Bpk[i], Brst[b]], writes=[Bstn[s1]])
                    P.op("sp", lambda E, s1=s1, h=h, q0=q0, qn=qn: E.dma_start(out=c.KnT[h, :, q0:q0 + qn], in_=stn[s1][:, :qn]), reads=[Bstn[s1]], key=f"st{s1}")
                for (t0, tn) in blocks(qn, 128):
                    rc = nxt("rc", 2)
                    for cc in range(4):
                        P.op("pe", lambda E, b=b, cc=cc, t0=t0, tn=tn: E.matmul(pc[:tn, 0:1], cq2[b][:, cc, t0:t0 + tn], ones[:, 0:1], start=(cc == 0), stop=(cc == 3)),
                             reads=[Bones, Bcg[b]], writes=[Bpc])
                    rstd_ops(P, lambda tn=tn: pc[:tn, 0:1], lambda rc=rc, tn=tn: rcol[rc][:tn, 0:1], 1.0 / 512, EPS, Bpc, Brc[rc])
                    for hh in range(2):
                        i = nxt("pvv", 3)
                        for cc in range(4):
                            P.op("pe", lambda E, i=i, b=b, cc=cc, t0=t0, tn=tn, hh=hh: E.matmul(pvv[i][:tn, :512], cg[b][:, cc, t0:t0 + tn], wv[:, cc, hh * 512:(hh + 1) * 512], start=(cc == 0), stop=(cc == 3)),
                                 reads=[Bwk, Bcg[b]], writes=[Bpvv[i]])
                        s1 = nxt("st", 3)
                        P.op("dve", lambda E, i=i, rc=rc, s1=s1, tn=tn: E.tensor_scalar(stn[s1][:tn, :], pvv[i][:tn, :], rcol[rc][:tn, 0:1], None, ALU.mult),
                             reads=[Bpvv[i], Brc[rc]], writes=[Bstn[s1]])
                        P.op("sp", lambda E, s1=s1, hh=hh, a=q0 + t0, tn=tn: E.dma_start(out=c.V[a:a + tn, hh * 512:(hh + 1) * 512], in_=stn[s1][:tn, :]), reads=[Bstn[s1]], key=f"st{s1}")
            P.emit(nc, f"kv{l}{si}")

    G.phase_q, G.phase_kv = phase_q, phase_kv

    MLA_SCALE = 1.0 / math.sqrt(192.0)

    def phase_mla(l, si):
        c = S[si]
        T = c.T
        kbs = blocks(T, 128)
        nkb = len(kbs)
        nfull = T // 128
        with ExitStack() as es:
            P = Prog()
            ones = sb(es, "ones", [128, 128], BF16)
            Kr = sb(es, "Kr", [64, T], BF16)
            Kn_ = [sb(es, f"Kn{i}", [128, T], BF16) for i in range(2)]
            Qn_ = [sb(es, f"Qn{i}", [128, T], BF16) for i in range(2)]
            Qr_ = [sb(es, f"Qr{i}", [64, T], BF16) for i in range(2)]
            Vh_ = [sb(es, f"Vh{i}", [128, nkb, 128], BF16) for i in range(2)]
            gt = [sb(es, f"gt{i}", [128, 512], BF16) for i in range(2)]
            PT = [sb(es, f"PT{i}", [128, 512], BF16) for i in range(4)]
            acc = [sb(es, f"acc{i}", [128, 512], F32) for i in range(2)]
            accb = sb(es, "accb", [128, 512], BF16)
            Bacc = [Buf("acc0"), Buf("acc1")]
            Baccb = Buf("accb")
            R = [sb(es, f"R{i}", [128, 512], F32) for i in range(2)]
            Y = [sb(es, f"Y{i}", [128, 512], F32) for i in range(2)]
            Yb = [sb(es, f"Yb{i}", [128, 512], BF16) for i in range(2)]
            pS = [ps(es, f"pS{i}") for i in range(3)]
            pO = [ps(es, f"pO{i}") for i in range(2)]
            pL = [ps(es, f"pL{i}") for i in range(2)]
            Bones, BKr = Buf("ones"), Buf("Kr")
            BK_, BQ_, BV_ = [Buf("K0"), Buf("K1")], [Buf("Q0"), Buf("Q1")], [Buf("V0"), Buf("V1")]
            Bgt = [Buf("gt0"), Buf("gt1")]
            BPT = [Buf(f"PT{i}") for i in range(4)]
            BR = [Buf("R0"), Buf("R1")]
            BY = [Buf("Y0"), Buf("Y1")]
            BYb = [Buf("Yb0"), Buf("Yb1")]
            BpS = [PB(f"pS{i}") for i in range(3)]
            BpO = [PB("pO0"), PB("pO1")]
            BpL = [PB("pL0"), PB("pL1")]
            P.op("dve", lambda E: E.memset(ones[:], 1.0), writes=[Bones])
            P.op("sp", lambda E: E.dma_start(out=Kr[:, :], in_=c.KrT[:, :]), writes=[BKr], key="kr")
            cnt = {}

            def nxt(k, n):
                v = cnt.get(k, 0)
                cnt[k] = v + 1
                return v % n

            def load_head(h):
                s = h % 2
                P.op("sp", lambda E: E.dma_start(out=Kn_[s][:, :], in_=c.KnT[h, :, :]), writes=[BK_[s]], key=f"kn{s}")
                P.op("sp", lambda E: E.dma_start(out=Qn_[s][:, :], in_=c.QnT[h, :, :]), writes=[BQ_[s]], key=f"qn{s}")
                P.op("sp", lambda E: E.dma_start(out=Qr_[s][:, :], in_=c.QrT[h, :, :]), writes=[BQ_[s]], key=f"qr{s}")
                for (b0, bn) in blocks(nfull, 8):
                    P.op("sp", lambda E, b0=b0, bn=bn: E.dma_start(out=Vh_[s][:, b0:b0 + bn, :], in_=c.V[b0 * 128:(b0 + bn) * 128, h * 128:(h + 1) * 128].rearrange("(k p) d -> p k d", p=128)),
                         writes=[BV_[s]], key=f"v{s}")
                if T % 128:
                    P.op("sp", lambda E: E.dma_start(out=Vh_[s][:T % 128, nfull, :], in_=c.V[nfull * 128:T, h * 128:(h + 1) * 128]), writes=[BV_[s]], key=f"w{s}")

            load_head(0)
            for h in range(8):
                if h + 1 < 8:
                    load_head(h + 1)
                Kn, Qn, Qr, Vh = Kn_[h % 2], Qn_[h % 2], Qr_[h % 2], Vh_[h % 2]
                BK, BQ, BV = BK_[h % 2], BQ_[h % 2], BV_[h % 2]
                for (q0, qn) in blocks(T, 512):
                    gb = nxt("g", 2)
                    ob = nxt("o", 2)
                    P.op("sp", lambda E, gb=gb, h=h, q0=q0, qn=qn: E.dma_start(out=gt[gb][:, :qn], in_=c.gT[4 + h, :, q0:q0 + qn]), writes=[Bgt[gb]], key=f"g{gb}")

                    def qk(kb, q0=q0, qn=qn, Kn=Kn, Qn=Qn, Qr=Qr, BK=BK, BQ=BQ):
                        k0, nk = kbs[kb]
                        sbk = kb % 3
                        P.op("pe", lambda E: E.matmul(pS[sbk][:nk, :qn], Kn[:, k0:k0 + nk], Qn[:, q0:q0 + qn], start=True, stop=False),
                             reads=[BK, BQ], writes=[BpS[sbk]])
                        P.op("pe", lambda E: E.matmul(pS[sbk][:nk, :qn], Kr[:, k0:k0 + nk], Qr[:, q0:q0 + qn], start=False, stop=True),
                             reads=[BKr, BQ], writes=[BpS[sbk]])

                    P.op("dve", lambda E, ob=ob, qn=qn: E.memset(acc[ob][:, :qn], 0.0), writes=[Bacc[ob]])
                    qk(0)
                    for kb in range(nkb):
                        k0, nk = kbs[kb]
                        if kb + 1 < nkb:
                            qk(kb + 1)
                        sbk = kb % 3
                        pb = nxt("pt", 4)
                        P.op("act", lambda E, sbk=sbk, pb=pb, nk=nk, qn=qn: E.activation(PT[pb][:nk, :qn], pS[sbk][:nk, :qn], AF.Exp, scale=MLA_SCALE),
                             reads=[BpS[sbk]], writes=[BPT[pb]])
                        P.op("pe", lambda E, pb=pb, ob=ob, kb=kb, nk=nk, qn=qn, Vh=Vh: E.matmul(pO[ob][:, :qn], Vh[:nk, kb, :], PT[pb][:nk, :qn], start=(kb == 0), stop=(kb == nkb - 1)),
                             reads=[BV, BPT[pb]], writes=[BpO[ob]])
                        if kb % 2 == 0:
                            P.op("pe", lambda E, pb=pb, ob=ob, kb=kb, nk=nk, qn=qn: E.matmul(pL[ob][:, :qn], ones[:nk, :], PT[pb][:nk, :qn], start=(kb == 0), stop=False),
                                 reads=[Bones, BPT[pb]], writes=[BpL[ob]])
                        else:
                            P.op("dve", lambda E, pb=pb, ob=ob, nk=nk, qn=qn: E.tensor_tensor(acc[ob][:nk, :qn], acc[ob][:nk, :qn], PT[pb][:nk, :qn], ALU.add),
                                 reads=[BPT[pb], Bacc[ob]], writes=[Bacc[ob]])
                    P.op("dve", lambda E, ob=ob, qn=qn: E.tensor_copy(accb[:, :qn], acc[ob][:, :qn]), reads=[Bacc[ob]], writes=[Baccb])
                    P.op("pe", lambda E, ob=ob, qn=qn: E.matmul(pL[ob][:, :qn], ones[:, :], accb[:, :qn], start=False, stop=True),
                         reads=[Bones, Baccb], writes=[BpL[ob]])
                    P.op("dve", lambda E, ob=ob, qn=qn: E.reciprocal(R[ob][:, :qn], pL[ob][:, :qn]), reads=[BpL[ob]], writes=[BR[ob]])
                    P.op("dve", lambda E, ob=ob, qn=qn: E.tensor_tensor(Y[ob][:, :qn], pO[ob][:, :qn], R[ob][:, :qn], ALU.mult), reads=[BpO[ob], BR[ob]], writes=[BY[ob]])
                    P.op("dve", lambda E, ob=ob, gb=gb, qn=qn: E.tensor_tensor(Yb[ob][:, :qn], Y[ob][:, :qn], gt[gb][:, :qn], ALU.mult), reads=[BY[ob], Bgt[gb]], writes=[BYb[ob]])
                    P.op("sp", lambda E, ob=ob, h=h, q0=q0, qn=qn: E.dma_start(out=c.yT[4 + h, :, q0:q0 + qn], in_=Yb[ob][:, :qn]), reads=[BYb[ob]], key=f"y{ob}")
            P.emit(nc, f"mla{l}{si}")

    def phase_diff(l, si):
        c = S[si]
        T = c.T
        kbs = blocks(T, 128)
        nkb = len(kbs)
        nfull = T // 128
        lam_init = 0.8 - 0.6 * math.exp(-0.3 * l)
        with ExitStack() as es:
            P = Prog()
            ones = sb(es, "ones", [128, 128], BF16)
            onesf = sb(es, "onesf", [128, 128], F32)
            identf = sb(es, "identf", [128, 128], F32)
            ident = sb(es, "ident", [128, 128], BF16)
            lv = sb(es, "lv", [64, 4], F32)
            lp = sb(es, "lp", [64, 2], BF16)
            ee = sb(es, "ee", [128, 2], F32)
            neglam = sb(es, "neglam", [128, 1], F32)
            gd = sb(es, "gd", [128, 1], F32)
            Kd_ = [[sb(es, f"Kd{s}{a}", [64, T], BF16) for a in range(2)] for s in range(2)]
            Qd_ = [[sb(es, f"Qd{s}{a}", [64, T], BF16) for a in range(2)] for s in range(2)]
            Vh_ = [sb(es, f"Vh{s}", [128, nkb, 128], BF16) for s in range(2)]
            Bt_ = [sb(es, f"Bt{s}", [128, 6, 512], BF16) for s in range(2)]
            cfb_ = [sb(es, f"cfb{s}", [128, 2], BF16) for s in range(2)]
            cf_ = [sb(es, f"cf{s}", [128, 2], F32) for s in range(2)]
            gt = [sb(es, f"gt{i}", [128, 512], BF16) for i in range(2)]
            PT = [sb(es, f"PT{i}", [128, 512], BF16) for i in range(4)]
            acc = [sb(es, f"acc{i}", [128, 512], F32) for i in range(2)]
            accb = [sb(es, f"accb{i}", [128, 512], BF16) for i in range(2)]
            Bacc = [Buf("acc0"), Buf("acc1")]
            Baccb = [Buf("accb0"), Buf("accb1")]
            R = [sb(es, f"R{a}", [128, 512], F32) for a in range(2)]
            On = [sb(es, f"On{a}", [128, 512], F32) for a in range(2)]
            Dm = sb(es, "Dm", [128, 512], F32)
            D2 = sb(es, "D2", [128, 512], BF16)
            rs = sb(es, "rs", [128, 512], F32)
            Yb = [sb(es, f"Yb{i}", [128, 512], BF16) for i in range(2)]
            pS = [ps(es, f"pS{i}") for i in range(3)]
            pO = [ps(es, f"pO{i}") for i in range(2)]
            pL = [ps(es, f"pL{i}") for i in range(2)]
            pX = ps(es, "pX")
            Bones, Bid, Blv, Bee, Bnl, Bgd, Bcf = Buf("ones"), Buf("id"), Buf("lv"), Buf("ee"), Buf("nl"), Buf("gd"), Buf("cf")
            BK_, BQ_, BV_, BBt_, Bcf_ = ([Buf("K0"), Buf("K1")], [Buf("Q0"), Buf("Q1")], [Buf("V0"), Buf("V1")],
                                         [Buf("Bt0"), Buf("Bt1")], [Buf("cf0"), Buf("cf1")])
            Bgt = [Buf("gt0"), Buf("gt1")]
            BPT = [Buf(f"PT{i}") for i in range(4)]
            BR = [Buf("R0"), Buf("R1")]
            BOn = [Buf("On0"), Buf("On1")]
            BDm, BD2, Brs = Buf("Dm"), Buf("D2"), Buf("rs")
            BYb = [Buf("Yb0"), Buf("Yb1")]
            BpS = [PB(f"pS{i}") for i in range(3)]
            BpO = [PB("pO0"), PB("pO1")]
            BpL = [PB("pL0"), PB("pL1")]
            BpX = PB("pX")
            P.op("dve", lambda E: E.memset(ones[:], 1.0), writes=[Bones])
            P.op("dve", lambda E: E.memset(onesf[:], 1.0), writes=[Bones])
            P.op("sp", lambda E: E.dma_start(out=identf[:], in_=ident_d[:, :]), writes=[Bid], key="id")
            P.op("dve", lambda E: E.tensor_copy(ident[:], identf[:]), reads=[Bid], writes=[Bid])
            for i in range(4):
                P.op("sp", lambda E, i=i: E.dma_start(out=lv[:, i:i + 1], in_=lam_in[i][l].rearrange("(p o) -> p o", o=1)), writes=[Blv], key=f"lv{i}")
            P.op("sp", lambda E: E.dma_start(out=gd[:, :], in_=diff_norm[l].rearrange("(p o) -> p o", o=1)), writes=[Bgd], key="gd")
            P.op("dve", lambda E: E.tensor_scalar(gd[:, :], gd[:, :], 1.0 - lam_init, None, ALU.mult), reads=[Bgd], writes=[Bgd])
            P.op("dve", lambda E: E.tensor_tensor(lp[:, 0:1], lv[:, 0:1], lv[:, 1:2], ALU.mult), reads=[Blv], writes=[Blv])
            P.op("dve", lambda E: E.tensor_tensor(lp[:, 1:2], lv[:, 2:3], lv[:, 3:4], ALU.mult), reads=[Blv], writes=[Blv])
            P.op("pe", lambda E: E.matmul(pX[:, 0:2], ones[:64, :], lp[:, 0:2], start=True, stop=True), reads=[Bones, Blv], writes=[BpX])
            P.op("act", lambda E: E.activation(ee[:, :], pX[:, 0:2], AF.Exp), reads=[BpX], writes=[Bee])
            P.op("dve", lambda E: E.tensor_tensor(neglam[:, :], ee[:, 1:2], ee[:, 0:1], ALU.subtract), reads=[Bee], writes=[Bnl])
            P.op("dve", lambda E: E.tensor_scalar(neglam[:, :], neglam[:, :], -lam_init, None, ALU.add), reads=[Bnl], writes=[Bnl])
            cnt = {}

            def nxt(k, n):
                v = cnt.get(k, 0)
                cnt[k] = v + 1
                return v % n

            def load_head(h):
                s = h % 2
                for a in range(2):
                    P.op("sp", lambda E, a=a: E.dma_start(out=Kd_[s][a][:, :], in_=c.kdT[2 * h + a, :, :]), writes=[BK_[s]], key=f"k{s}{a}")
                    P.op("sp", lambda E, a=a: E.dma_start(out=Qd_[s][a][:, :], in_=c.qdT[2 * h + a, :, :]), writes=[BQ_[s]], key=f"q{s}{a}")
                for (b0, bn) in blocks(nfull, 8):
                    P.op("sp", lambda E, b0=b0, bn=bn: E.dma_start(out=Vh_[s][:, b0:b0 + bn, :], in_=c.Vd[b0 * 128:(b0 + bn) * 128, h * 128:(h + 1) * 128].rearrange("(k p) d -> p k d", p=128)),
                         writes=[BV_[s]], key=f"v{s}")
                if T % 128:
                    P.op("sp", lambda E: E.dma_start(out=Vh_[s][:T % 128, nfull, :], in_=c.Vd[nfull * 128:T, h * 128:(h + 1) * 128]), writes=[BV_[s]], key=f"w{s}")
                gflat = Gb[h].rearrange("r w -> (r w)")
                for i in range(6):
                    off = RMAX - (-128 + 128 * i)
                    P.op("sp", lambda E, i=i, off=off: E.dma_start(out=Bt_[s][:, i, :], in_=gflat[off:off + 128 * (WFULL - 1)].rearrange("(p w) -> p w", w=WFULL - 1)[:, 0:512]),
                         writes=[BBt_[s]], key=f"bt{s}")
                P.op("sp", lambda E: E.dma_start(out=cfb_[s][:, 0:1], in_=Gb[h, 0:128, 0:1], allow_slow_non_contiguous=True), writes=[Bcf_[s]], key=f"cfa{s}")
                P.op("sp", lambda E: E.dma_start(out=cfb_[s][:, 1:2], in_=Gb[h, 0:128, WFULL - 1:WFULL], allow_slow_non_contiguous=True), writes=[Bcf_[s]], key=f"cfb{s}")
                P.op("dve", lambda E: E.tensor_scalar(cf_[s][:, :], cfb_[s][:, :], 0.125, None, ALU.mult), reads=[Bcf_[s]], writes=[Bcf_[s]])

            load_head(0)
            for h in range(4):
                if h + 1 < 4:
                    load_head(h + 1)
                Kd, Qd, Vh, Bt, cf = Kd_[h % 2], Qd_[h % 2], Vh_[h % 2], Bt_[h % 2], cf_[h % 2]
                BK, BQ, BV, BBt, Bcf = BK_[h % 2], BQ_[h % 2], BV_[h % 2], BBt_[h % 2], Bcf_[h % 2]
                for (q0, qn) in blocks(T, 512):
                    gb = nxt("g", 2)
                    P.op("sp", lambda E, gb=gb, h=h, q0=q0, qn=qn: E.dma_start(out=gt[gb][:, :qn], in_=c.gT[12 + h, :, q0:q0 + qn]), writes=[Bgt[gb]], key=f"g{gb}")
                    steps = [(kb, a) for kb in range(nkb) for a in range(2)]

                    def klass(kb, q0=q0, qn=qn):
                        k0, nk = kbs[kb]
                        relmin, relmax = k0 - (q0 + qn - 1), k0 + nk - 1 - q0
                        if relmin >= 91:
                            return ("far", 0)
                        if relmax <= -91:
                            return ("far", 1)
                        d = k0 - q0
                        i = (d + 128) // 128
                        assert (d + 128) % 128 == 0 and 0 <= i < 6, (d, i)
                        return ("near", i)

                    def qk(st, q0=q0, qn=qn, Kd=Kd, Qd=Qd, Bt=Bt, BK=BK, BQ=BQ, BBt=BBt):
                        kb, a = steps[st]
                        k0, nk = kbs[kb]
                        sbk = st % 3
                        kl = klass(kb)
                        near = kl[0] == "near" and not DBG.get("nobias")
                        P.op("pe", lambda E: E.matmul(pS[sbk][:nk, :qn], Kd[a][:, k0:k0 + nk], Qd[a][:, q0:q0 + qn], start=True, stop=not near),
                             reads=[BK, BQ], writes=[BpS[sbk]])
                        if near:
                            P.op("pe", lambda E: E.matmul(pS[sbk][:nk, :qn], ident[:nk, :nk], Bt[:nk, kl[1], :qn], start=False, stop=True),
                                 reads=[Bid, BBt], writes=[BpS[sbk]])

                    for a in range(2):
                        P.op("dve", lambda E, a=a, qn=qn: E.memset(acc[a][:, :qn], 0.0), writes=[Bacc[a]])
                    qk(0)
                    for st in range(len(steps)):
                        kb, a = steps[st]
                        k0, nk = kbs[kb]
                        if st + 1 < len(steps):
                            qk(st + 1)
                        sbk = st % 3
                        pb = nxt("pt", 4)
                        kl = klass(kb)
                        if kl[0] == "far":
                            P.op("act", lambda E, sbk=sbk, pb=pb, nk=nk, qn=qn, j=kl[1], cf=cf: E.activation(PT[pb][:nk, :qn], pS[sbk][:nk, :qn], AF.Exp, bias=cf[:nk, j:j + 1], scale=0.125),
                                 reads=[BpS[sbk], Bcf], writes=[BPT[pb]])
                        else:
                            P.op("act", lambda E, sbk=sbk, pb=pb, nk=nk, qn=qn: E.activation(PT[pb][:nk, :qn], pS[sbk][:nk, :qn], AF.Exp, scale=0.125),
                                 reads=[BpS[sbk]], writes=[BPT[pb]])
                        P.op("pe", lambda E, pb=pb, a=a, kb=kb, nk=nk, qn=qn, Vh=Vh: E.matmul(pO[a][:, :qn], Vh[:nk, kb, :], PT[pb][:nk, :qn], start=(kb == 0), stop=(kb == nkb - 1)),
                             reads=[BV, BPT[pb]], writes=[BpO[a]])
                        if kb % 2 == 0:
                            P.op("pe", lambda E, pb=pb, a=a, kb=kb, nk=nk, qn=qn: E.matmul(pL[a][:, :qn], ones[:nk, :], PT[pb][:nk, :qn], start=(kb == 0), stop=False),
                                 reads=[Bones, BPT[pb]], writes=[BpL[a]])
                        else:
                            P.op("dve", lambda E, pb=pb, a=a, nk=nk, qn=qn: E.tensor_tensor(acc[a][:nk, :qn], acc[a][:nk, :qn], PT[pb][:nk, :qn], ALU.add),
                                 reads=[BPT[pb], Bacc[a]], writes=[Bacc[a]])
                    for a in range(2):
                        P.op("dve", lambda E, a=a, qn=qn: E.tensor_copy(accb[a][:, :qn], acc[a][:, :qn]), reads=[Bacc[a]], writes=[Baccb[a]])
                        P.op("pe", lambda E, a=a, qn=qn: E.matmul(pL[a][:, :qn], ones[:, :], accb[a][:, :qn], start=False, stop=True),
                             reads=[Bones, Baccb[a]], writes=[BpL[a]])
                    for a in range(2):
                        pass
                    for a in range(2):
                        P.op("dve", lambda E, a=a, qn=qn: E.tensor_copy(R[a][:, :qn], pL[a][:, :qn]), reads=[BpL[a]], writes=[BR[a]])
                        P.op("dve", lambda E, a=a, qn=qn: E.tensor_copy(On[a][:, :qn], pO[a][:, :qn]), reads=[BpO[a]], writes=[BOn[a]])
                    for a in range(2):
                        P.op("dve", lambda E, a=a, qn=qn: E.reciprocal(R[a][:, :qn], R[a][:, :qn]), reads=[BR[a]], writes=[BR[a]])
                        P.op("dve", lambda E, a=a, qn=qn: E.tensor_tensor(On[a][:, :qn], On[a][:, :qn], R[a][:, :qn], ALU.mult), reads=[BOn[a], BR[a]], writes=[BOn[a]])
                    if debug and h == 3 and q0 == 0:
                        d0 = dsc(f"dbg_On0_{l}{si}", [128, 512], F32)
                        d1 = dsc(f"dbg_On1_{l}{si}", [128, 512], F32)
                        d2 = dsc(f"dbg_nl_{l}{si}", [128, 1], F32)
                        P.op("sp", lambda E, qn=qn: E.dma_start(out=d0[:, :qn], in_=On[0][:, :qn]), reads=[BOn[0]], key="dbg0")
                        P.op("sp", lambda E, qn=qn: E.dma_start(out=d1[:, :qn], in_=On[1][:, :qn]), reads=[BOn[1]], key="dbg1")
                        P.op("sp", lambda E: E.dma_start(out=d2[:, :], in_=neglam[:, :]), reads=[Bnl], key="dbg2")
                    P.op("dve", lambda E, qn=qn: E.tensor_scalar(On[1][:, :qn], On[1][:, :qn], neglam[:, 0:1], None, ALU.mult), reads=[BOn[1], Bnl], writes=[BOn[1]])
                    P.op("dve", lambda E, qn=qn: E.tensor_tensor(Dm[:, :qn], On[0][:, :qn], On[1][:, :qn], ALU.add), reads=[BOn[0], BOn[1]], writes=[BDm])
                    P.op("dve", lambda E, qn=qn: E.tensor_tensor(D2[:, :qn], Dm[:, :qn], Dm[:, :qn], ALU.mult), reads=[BDm], writes=[BD2])
                    P.op("pe", lambda E, qn=qn: E.matmul(pX[:, :qn], ones[:, :], D2[:, :qn], start=True, stop=True), reads=[Bones, BD2], writes=[BpX])
                    rstd_ops(P, lambda qn=qn: pX[:, :qn], lambda qn=qn: rs[:, :qn], 1.0 / 128, 1e-5, BpX, Brs)
                    P.op("dve", lambda E, qn=qn: E.tensor_tensor(Dm[:, :qn], Dm[:, :qn], rs[:, :qn], ALU.mult), reads=[BDm, Brs], writes=[BDm])
                    P.op("dve", lambda E, qn=qn: E.tensor_scalar(Dm[:, :qn], Dm[:, :qn], gd[:, 0:1], None, ALU.mult), reads=[BDm, Bgd], writes=[BDm])
                    if debug and h == 3 and q0 == 0:
                        d3 = dsc(f"dbg_lv_{l}{si}", [64, 4], F32)
                        d4 = dsc(f"dbg_ee_{l}{si}", [128, 2], F32)
                        d5 = dsc(f"dbg_rs_{l}{si}", [128, 512], F32)
                        d6 = dsc(f"dbg_Dm_{l}{si}", [128, 512], F32)
                        P.op("sp", lambda E: E.dma_start(out=d3[:, :], in_=lv[:, :]), reads=[Blv], key="dbg0")
                        P.op("sp", lambda E: E.dma_start(out=d4[:, :], in_=ee[:, :]), reads=[Bee], key="dbg1")
                        P.op("sp", lambda E, qn=qn: E.dma_start(out=d5[:, :qn], in_=rs[:, :qn]), reads=[Brs], key="dbg2")
                        P.op("sp", lambda E, qn=qn: E.dma_start(out=d6[:, :qn], in_=Dm[:, :qn]), reads=[BDm], key="dbg0")
                    yb = nxt("yb", 2)
                    P.op("dve", lambda E, yb=yb, gb=gb, qn=qn: E.tensor_tensor(Yb[yb][:, :qn], Dm[:, :qn], gt[gb][:, :qn], ALU.mult), reads=[BDm, Bgt[gb]], writes=[BYb[yb]])
                    P.op("sp", lambda E, yb=yb, h=h, q0=q0, qn=qn: E.dma_start(out=c.yT[12 + h, :, q0:q0 + qn], in_=Yb[yb][:, :qn]), reads=[BYb[yb]], key=f"y{yb}")
            P.emit(nc, f"df{l}{si}")

    def phase_dft(l, si):
        c = S[si]
        T = c.T
        sbs = blocks(T, 128)
        fscale = 1.0 / math.sqrt(T * 128.0)
        with ExitStack() as es:
            P = Prog()
            wff = sb(es, "wff", [128, 4, 128], F32)
            wfm = sb(es, "wfm", [128, 4, 128], BF16)
            ABt = [sb(es, f"ABt{i}", [128, 4, 256], BF16) for i in range(3)]
            Ct = [sb(es, f"Ct{i}", [128, 512], BF16) for i in range(3)]
            Nt = [sb(es, f"Nt{i}", [128, 512], BF16) for i in range(3)]
            fT = [sb(es, f"fT{i}", [128, 512], BF16) for i in range(2)]
            gt = [sb(es, f"gt{i}", [128, 512], BF16) for i in range(2)]
            Yb = [sb(es, f"Yb{i}", [128, 512], BF16) for i in range(2)]
            pF = [ps(es, f"pF{i}") for i in range(4)]
            pY = [ps(es, f"pY{i}") for i in range(2)]
            Bw = Buf("w")
            BAB = [Buf(f"AB{i}") for i in range(3)]
            BC = [Buf(f"C{i}") for i in range(3)]
            BfT = [Buf("fT0"), Buf("fT1")]
            Bgt = [Buf("gt0"), Buf("gt1")]
            BYb = [Buf("Yb0"), Buf("Yb1")]
            BpF = [PB(f"pF{i}") for i in range(4)]
            BpY = [PB("pY0"), PB("pY1")]
            P.op("sp", lambda E: E.dma_start(out=wff[:], in_=w_fmix[l].rearrange("g c d -> c g d")), writes=[Bw], key="w")
            P.op("dve", lambda E: E.tensor_copy(wfm[:], wff[:]), reads=[Bw], writes=[Bw])
            cnt = {}

            def nxt(k, n):
                v = cnt.get(k, 0)
                cnt[k] = v + 1
                return v % n

            for (k0, nk) in blocks(T, 512):
                for si_, (s0, ns) in enumerate(sbs):
                    b = nxt("in", 3)
                    P.op("sp", lambda E, b=b, s0=s0, ns=ns: E.dma_start(out=ABt[b][:ns], in_=c.AB[s0:s0 + ns]), writes=[BAB[b]], key=f"ab{b}")
                    P.op("sp", lambda E, b=b, s0=s0, ns=ns, k0=k0, nk=nk: E.dma_start(out=Ct[b][:ns, :nk], in_=c.CT[s0:s0 + ns, k0:k0 + nk]), writes=[BC[b]], key=f"ct{b}")
                    P.op("sp", lambda E, b=b, s0=s0, ns=ns, k0=k0, nk=nk: E.dma_start(out=Nt[b][:ns, :nk], in_=c.NST[s0:s0 + ns, k0:k0 + nk]), writes=[BC[b]], key=f"nt{b}")
                    for g in range(4):
                        P.op("pe", lambda E, b=b, g=g, ns=ns, nk=nk, si_=si_: E.matmul(pF[g][:, :nk], ABt[b][:ns, g, 0:128], Ct[b][:ns, :nk], start=(si_ == 0), stop=False),
                             reads=[BAB[b], BC[b]], writes=[BpF[g]])
                        P.op("pe", lambda E, b=b, g=g, ns=ns, nk=nk, si_=si_: E.matmul(pF[g][:, :nk], ABt[b][:ns, g, 128:256], Nt[b][:ns, :nk], start=False, stop=(si_ == len(sbs) - 1)),
                             reads=[BAB[b], BC[b]], writes=[BpF[g]])
                for g in range(4):
                    fb = nxt("f", 2)
                    P.op("act", lambda E, fb=fb, g=g, nk=nk: E.activation(fT[fb][:, :nk], pF[g][:, :nk], AF.Copy, scale=fscale), reads=[BpF[g]], writes=[BfT[fb]])
                    yb = nxt("y", 2)
                    P.op("pe", lambda E, fb=fb, yb=yb, g=g, nk=nk: E.matmul(pY[yb][:, :nk], wfm[:, g, :], fT[fb][:, :nk], start=True, stop=True),
                         reads=[Bw, BfT[fb]], writes=[BpY[yb]])
                    gb = nxt("g", 2)
                    P.op("sp", lambda E, gb=gb, g=g, k0=k0, nk=nk: E.dma_start(out=gt[gb][:, :nk], in_=c.gT[g, :, k0:k0 + nk]), writes=[Bgt[gb]], key=f"g{gb}")
                    P.op("dve", lambda E, yb=yb, gb=gb, nk=nk: E.tensor_tensor(Yb[yb][:, :nk], pY[yb][:, :nk], gt[gb][:, :nk], ALU.mult), reads=[BpY[yb], Bgt[gb]], writes=[BYb[yb]])
                    P.op("sp", lambda E, yb=yb, g=g, k0=k0, nk=nk: E.dma_start(out=c.yT[g, :, k0:k0 + nk], in_=Yb[yb][:, :nk]), reads=[BYb[yb]], key=f"y{yb}")
            P.emit(nc, f"dft{l}{si}")

    def phase_out(l, si, xsrc, xdst, final):
        c = S[si]
        T = c.T
        wo_l = w_o[l].rearrange("(c p) n -> p c n", p=128)
        with ExitStack() as es:
            P = Prog()
            wf = sb(es, "wf", [128, 16, 256], F32)
            wo = sb(es, "wo", [128, 16, 2048], BF16)
            yt = sb(es, "yt", [128, 16, 512], BF16)
            xt = [sb(es, f"xt{i}", [128, D], F32) for i in range(2)]
            xo = [sb(es, f"xo{i}", [128, D], F32) for i in range(2)]
            junk = sb(es, "junk", [128, D], BF16)
            gt = sb(es, "gt", [128, D], F32)
            grow = sb(es, "grow", [1, D], F32)
            one1 = sb(es, "one1", [1, 128], F32)
            ss = [sb(es, f"ss{i}", [128, 2], F32) for i in range(2)]
            po = [ps(es, f"po{i}") for i in range(4)]
            Bwf, Bwo, Byt = Buf("wf"), Buf("wo"), Buf("yt")
            Bxt = [Buf("xt0"), Buf("xt1")]
            Bxo = [Buf("xo0"), Buf("xo1")]
            Bjunk, Bgt, Bgrow, Bone1 = Buf("junk"), Buf("gt"), Buf("grow"), Buf("one1")
            Bss = [Buf("ss0"), Buf("ss1")]
            Bpo = [PB(f"po{i}") for i in range(4)]
            for i in range(8):
                for hf in range(2):
                    P.op("sp", lambda E, i=i, hf=hf: E.dma_start(out=wf[:, hf * 8:hf * 8 + 8, :], in_=wo_l[:, hf * 8:hf * 8 + 8, i * 256:(i + 1) * 256]), writes=[Bwf], key="wf")
                P.op("dve", lambda E, i=i: E.tensor_copy(wo[:, :, i * 256:(i + 1) * 256], wf[:]), reads=[Bwf], writes=[Bwo])
            if final:
                P.op("sp", lambda E: E.dma_start(out=grow[:], in_=final_norm.rearrange("(o n) -> o n", o=1)), writes=[Bgrow], key="c4")
                P.op("dve", lambda E: E.memset(one1[:], 1.0), writes=[Bone1])
                for i in range(4):
                    P.op("pe", lambda E, i=i: E.matmul(po[i][:, :], one1[:, :], grow[:, i * 512:(i + 1) * 512], start=True, stop=True),
                         reads=[Bone1, Bgrow], writes=[Bpo[i]])
                    P.op("dve", lambda E, i=i: E.tensor_copy(gt[:, i * 512:(i + 1) * 512], po[i][:, :]), reads=[Bpo[i]], writes=[Bgt])
            cnt = {}

            def nxt(k, n):
                v = cnt.get(k, 0)
                cnt[k] = v + 1
                return v % n

            for (q0, qn) in blocks(T, 512):
                for hf in range(2):
                    P.op("sp", lambda E, q0=q0, qn=qn, hf=hf: E.dma_start(out=yt[:, hf * 8:hf * 8 + 8, :qn], in_=c.yT[hf * 8:hf * 8 + 8, :, q0:q0 + qn].rearrange("c p t -> p c t")), writes=[Byt], key="yt")
                for (t0, tn) in blocks(qn, 128):
                    a = q0 + t0
                    b = nxt("x", 2)
                    P.op("sp", lambda E, b=b, a=a, tn=tn: E.dma_start(out=xt[b][:tn, :], in_=xsrc[a:a + tn, :]), writes=[Bxt[b]], key=f"x{b}")
                    for dc in range(4):
                        for cc in range(16):
                            P.op("pe", lambda E, dc=dc, cc=cc, t0=t0, tn=tn: E.matmul(po[dc][:tn, :], yt[:, cc, t0:t0 + tn], wo[:, cc, dc * 512:(dc + 1) * 512], start=(cc == 0), stop=(cc == 15)),
                                 reads=[Byt, Bwo], writes=[Bpo[dc]])
                        P.op("dve", lambda E, dc=dc, b=b, tn=tn: E.tensor_tensor(xo[b][:tn, dc * 512:(dc + 1) * 512], po[dc][:tn, :], xt[b][:tn, dc * 512:(dc + 1) * 512], ALU.add),
                             reads=[Bpo[dc], Bxt[b]], writes=[Bxo[b]])
                    if not final:
                        P.op("sp", lambda E, b=b, a=a, tn=tn: E.dma_start(out=xdst[a:a + tn, :], in_=xo[b][:tn, :]), reads=[Bxo[b]], key=f"o{b}")
                    else:
                        P.op("act", lambda E, b=b, tn=tn: E.activation(junk[:tn, :], xo[b][:tn, :], AF.Square, accum_out=ss[b][:tn, 0:1]),
                             reads=[Bxo[b]], writes=[Bjunk, Bss[b]])
                        rstd_ops(P, lambda b=b, tn=tn: ss[b][:tn, 0:1], lambda b=b, tn=tn: ss[b][:tn, 1:2], 1.0 / D, EPS, Bss[b], Bss[b])
                        P.op("dve", lambda E, b=b, tn=tn: E.tensor_scalar(xo[b][:tn, :], xo[b][:tn, :], ss[b][:tn, 1:2], None, ALU.mult), reads=[Bxo[b], Bss[b]], writes=[Bxo[b]])
                        P.op("dve", lambda E, b=b, tn=tn: E.tensor_tensor(xo[b][:tn, :], xo[b][:tn, :], gt[:tn, :], ALU.mult), reads=[Bxo[b], Bgt], writes=[Bxo[b]])
                        lo = max(a, N_META)
                        if lo < a + tn:
                            P.op("sp", lambda E, b=b, a=a, tn=tn, lo=lo: E.dma_start(out=xdst[lo - N_META:a + tn - N_META, :], in_=xo[b][lo - a:tn, :]), reads=[Bxo[b]], key=f"o{b}")
            P.emit(nc, f"out{l}{si}")

    def full():
        for si in range(nseg):
            phase_tables(si)
        phase_bias()
        for l in range(NL):
            for si in range(nseg):
                xs = x_in[si] if l == 0 else S[si].x1
                phase_inproj(l, si, xs)
                phase_q(l, si)
                phase_kv(l, si)
                phase_dft(l, si)
                phase_mla(l, si)
                phase_diff(l, si)
                if l == NL - 1:
                    phase_out(l, si, xs, y_out[si], True)
                else:
                    phase_out(l, si, xs, S[si].x1, False)

    G.phase_mla, G.phase_diff, G.phase_dft, G.phase_out, G.full = phase_mla, phase_diff, phase_dft, phase_out, full
    G.phase_tables, G.phase_bias, G.phase_inproj = phase_tables, phase_bias, phase_inproj
    G.nc, G.S, G.x_in, G.y_out, G.Gb = nc, S, x_in, y_out, Gb
    G.din = dict(w_uq=w_uq, w_ukv=w_ukv, q_norm=q_norm, kv_norm=kv_norm, cos2=cos2, sin2=sin2, w_fmix=w_fmix, w_o=w_o,
                 lam=lam_in, diff_norm=diff_norm, final_norm=final_norm, rel_bias=rel_bias, ident=ident_d)
    G.sb, G.ps = sb, ps
    return G


SEG_T = [4096 + N_META, 8192 + N_META]


def kernel(x_prompt, x_sample, meta_tokens, rel_bias, final_norm, norm_w, w_in, w_fmix, q_norm, w_uq,
           kv_norm, w_ukv, lam_q1, lam_k1, lam_q2, lam_k2, diff_norm, w_o):
    f32 = lambda a: np.ascontiguousarray(np.asarray(a, dtype=np.float32))
    x_prompt, x_sample, meta = f32(x_prompt), f32(x_sample), f32(meta_tokens)
    G = build(SEG_T, debug=False)
    G.full()
    ident, cs, oh = misc_consts()
    shared = dict(ident=ident, cs=cs, oh=oh, rel_bias=f32(rel_bias), final_norm=f32(final_norm), norm_w=f32(norm_w),
                  w_in=f32(w_in), w_fmix=f32(w_fmix), q_norm=f32(q_norm), w_uq=f32(w_uq), kv_norm=f32(kv_norm),
                  w_ukv=f32(w_ukv), lam_q1=f32(lam_q1), lam_k1=f32(lam_k1), lam_q2=f32(lam_q2), lam_k2=f32(lam_k2),
                  diff_norm=f32(diff_norm), w_o=f32(w_o))
    for s, T in enumerate(SEG_T):
        shared[f"cos2_{s}"], shared[f"sin2_{s}"] = host_consts(T)
        shared[f"tj_{s}"], shared[f"tk_{s}"] = dft_consts(T)
    xp = [np.concatenate([meta, x_prompt[g]], 0) for g in range(x_prompt.shape[0])]
    in_maps = []
    for c in range(8):
        m = dict(shared)
        m["x0"] = np.concatenate([meta, x_sample[c]], 0)
        m["x1"] = xp[c // 4]
        in_maps.append(m)
    res = run_bass_kernel_spmd(G.nc, in_maps, core_ids=list(range(8)))
    y_sample = np.stack([np.asarray(res.results[c]["y0"], dtype=np.float32) for c in range(8)], 0)
    y_prompt = np.stack([np.asarray(res.results[4 * g]["y1"], dtype=np.float32) for g in range(2)], 0)
    return (y_prompt, y_sample)
```

```python
import math
from contextlib import ExitStack
import numpy as np
import concourse.bass as bass
import concourse.mybir as mybir
from concourse.bass_utils import run_bass_kernel_spmd

F32 = mybir.dt.float32
BF16 = mybir.dt.bfloat16
I32 = mybir.dt.int32
AF = mybir.ActivationFunctionType
ALU = mybir.AluOpType

D = 2048
N_META = 16
NL = 2
INW = 5440
C_UF, C_CQ, C_CKV, C_KR, C_QD, C_KD, C_VD, C_G = 0, 512, 1280, 1792, 1856, 2368, 2880, 3392
EPS = 1e-6
RMAX = 768
WFULL = 1600
PI_S = 3.14159


class Buf:
    def __init__(self, name, excl=False):
        self.name = name
        self.excl = excl
        self.writers = {}
        self.readers = {}


def PB(name):
    return Buf(name, True)


class Op:
    __slots__ = ("eng", "fn", "deps", "sig", "seq", "key", "cum")


SEM_SKIP = [0]
EMIT_N = [0]
DBG = {}
MAXQ = [4]
ENGS = ["pe", "act", "dve", "pool", "sp"]


class Prog:
    def __init__(self):
        self.ops = {e: [] for e in ENGS}
        self.keycum = {}

    def op(self, eng, fn, reads=(), writes=(), key=None):
        o = Op()
        o.eng, o.fn, o.deps, o.sig, o.key, o.seq, o.cum = eng, fn, [], False, key, 0, 0
        if key is not None:
            self.keycum[key] = self.keycum.get(key, 0) + 16
            o.cum = self.keycum[key]
        ex = [b for b in reads if b.excl]
        if ex:
            reads = [b for b in reads if not b.excl]
            writes = list(writes) + [b for b in ex if b not in writes]
        for b in reads:
            for w in b.writers.values():
                self._dep(o, w)
        for b in writes:
            for r in b.readers.values():
                self._dep(o, r)
            for w in b.writers.values():
                self._dep(o, w)
        for b in reads:
            b.readers[(eng, key)] = o
        for b in writes:
            b.writers[(eng, key)] = o
            b.readers = {}
        self.ops[eng].append(o)
        return o

    def _dep(self, o, p):
        if p is o:
            return
        if p.key is None and o.key is None and p.eng == o.eng and p.eng == "pe":
            return
        if p.key is None and p.eng == o.eng:
            pass
        o.deps.append(p)
        p.sig = True

    def emit(self, nc, name):
        EMIT_N[0] += 1
        name = f"{name}_{EMIT_N[0]}"
        with ExitStack() as es:
            engsem = {e: nc.alloc_semaphore(name=f"{name}_s_{e}") for e in ENGS}
            keysem = {k: nc.alloc_semaphore(name=f"{name}_k{i}") for i, k in enumerate(self.keycum)}
            allsems = list(engsem.values()) + list(keysem.values())
            for e in ENGS:
                c = 0
                for o in self.ops[e]:
                    if o.key is None and o.sig:
                        c += 1
                        o.seq = c
            block = es.enter_context(nc.Block())

            def run(E, e):
                waited = {}
                inflight = []
                for o in self.ops[e]:
                    for p in o.deps:
                        if p.key is not None:
                            sid, sem, val = ("k", p.key), keysem[p.key], p.cum
                        else:
                            sid, sem, val = ("e", p.eng), engsem[p.eng], p.seq
                        if waited.get(sid, 0) < val:
                            E.wait_ge(sem, val)
                            waited[sid] = val
                    if o.key is not None and len(inflight) >= MAXQ[0]:
                        k0_, c0_ = inflight.pop(0)
                        if waited.get(("k", k0_), 0) < c0_:
                            E.wait_ge(keysem[k0_], c0_)
                            waited[("k", k0_)] = c0_
                    inst = o.fn(E)
                    if o.key is not None:
                        inflight.append((o.key, o.cum))
                        inst.then_inc(keysem[o.key], 16)
                    elif o.sig:
                        inst.then_inc(engsem[e], 1)
                if e == "sp":
                    for k, tot in self.keycum.items():
                        if waited.get(("k", k), 0) < tot:
                            E.wait_ge(keysem[k], tot)

            @block.tensor
            def _(E):
                run(E, "pe")

            @block.scalar
            def _(E):
                run(E, "act")

            @block.vector
            def _(E):
                run(E, "dve")

            @block.gpsimd
            def _(E):
                run(E, "pool")

            @block.sync
            def _(E):
                run(E, "sp")
        nc.clear_and_free_semaphores(allsems)
        nc.all_engine_barrier()


def blocks(n, size):
    return [(s, min(size, n - s)) for s in range(0, n, size)]


class Ctx:
    pass


def t5_bucket_np(rel):
    nb = 16
    max_exact = 8
    ret = (rel > 0).astype(np.int64) * nb
    n = np.abs(rel)
    nf = np.maximum(n, 1).astype(np.float32)
    large = max_exact + (np.log(nf / np.float32(max_exact)) / np.float32(math.log(128 / max_exact))
                         * np.float32(nb - max_exact)).astype(np.int32)
    large = np.minimum(large, nb - 1)
    return ret + np.where(n < max_exact, n, large)


def host_consts(T):
    pos = np.arange(T, dtype=np.float32)
    inv = (10000.0 ** (-np.arange(0, 64, 2, dtype=np.float32) / 64)).astype(np.float32)
    ang = pos[:, None] * inv[None, :]
    cos, sin = np.cos(ang).astype(np.float32).T, np.sin(ang).astype(np.float32).T
    cos2 = np.concatenate([cos, cos], 0)
    sin2 = np.concatenate([-sin, sin], 0)
    return np.ascontiguousarray(cos2), np.ascontiguousarray(sin2)


def dft_consts(T):
    s = np.arange(T, dtype=np.int64)
    j = np.arange(512, dtype=np.int64)
    a = 2 * np.pi * ((s[:, None] * j[None, :]) % T).astype(np.float64) / T
    tj = np.stack([np.cos(a), np.sin(a)], 1).astype(np.float32)
    k0 = np.arange(0, T, 512, dtype=np.int64)
    a = 2 * np.pi * ((s[:, None] * k0[None, :]) % T).astype(np.float64) / T
    tk = np.stack([np.cos(a), np.sin(a)], 2).astype(np.float32)
    return np.ascontiguousarray(tj), np.ascontiguousarray(tk)


def misc_consts():
    ident = np.eye(128, dtype=np.float32)
    c = np.arange(128)
    ang = 2 * np.pi * np.outer(c, c) / 128.0
    cs = np.concatenate([np.cos(ang), np.sin(ang)], 1).astype(np.float32)
    n = np.arange(WFULL)
    rel = RMAX - n
    bk = t5_bucket_np(rel)
    oh = np.zeros((32, WFULL), np.float32)
    oh[bk, n] = 8.0
    return ident, cs, oh


def build(segT, debug=False):
    nc = bass.Bass("TRN2", target_bir_lowering=False)
    G = Ctx()
    nseg = len(segT)

    def din(name, shape, dt=F32):
        return nc.dram_tensor(name, list(shape), dt, kind="ExternalInput").ap()

    def dsc(name, shape, dt=BF16):
        t = nc.dram_tensor(name, list(shape), dt, kind=("ExternalOutput" if debug else "Internal")).ap()
        dbg[name] = t
        return t

    x_in = [din(f"x{s}", [segT[s], D]) for s in range(nseg)]
    y_out = [nc.dram_tensor(f"y{s}", [segT[s] - N_META, D], F32, kind="ExternalOutput").ap() for s in range(nseg)]
    cos2 = [din(f"cos2_{s}", [64, segT[s]]) for s in range(nseg)]
    sin2 = [din(f"sin2_{s}", [64, segT[s]]) for s in range(nseg)]
    tjd = [din(f"tj_{s}", [segT[s], 2, 512]) for s in range(nseg)]
    tkd = [din(f"tk_{s}", [segT[s], len(blocks(segT[s], 512)), 2]) for s in range(nseg)]
    ident_d = din("ident", [128, 128])
    cs_d = din("cs", [128, 256])
    oh_d = din("oh", [32, WFULL])
    rel_bias = din("rel_bias", [32, 4])
    final_norm = din("final_norm", [D])
    norm_w = din("norm_w", [NL, D])
    w_in = din("w_in", [NL, D, INW])
    w_fmix = din("w_fmix", [NL, 4, 128, 128])
    q_norm = din("q_norm", [NL, 768])
    w_uq = din("w_uq", [NL, 768, 1536])
    kv_norm = din("kv_norm", [NL, 512])
    w_ukv = din("w_ukv", [NL, 512, 2048])
    lam_in = [din(n, [NL, 64]) for n in ("lam_q1", "lam_k1", "lam_q2", "lam_k2")]
    diff_norm = din("diff_norm", [NL, 128])
    w_o = din("w_o", [NL, D, D])

    dbg = {}
    S = []
    for s in range(nseg):
        T = segT[s]
        c = Ctx()
        c.T = T
        c.x1 = dsc(f"x1_{s}", [T, D], F32)
        c.AB = dsc(f"AB_{s}", [T, 4, 256])
        c.cqg = dsc(f"cqg_{s}", [6, 128, T])
        c.cqs = dsc(f"cqs_{s}", [6, 128, T])
        c.ckg = dsc(f"ckg_{s}", [4, 128, T])
        c.cks = dsc(f"cks_{s}", [4, 128, T])
        c.KrT = dsc(f"KrT_{s}", [64, T])
        c.qdT = dsc(f"qdT_{s}", [8, 64, T])
        c.kdT = dsc(f"kdT_{s}", [8, 64, T])
        c.Vd = dsc(f"Vd_{s}", [T, 512])
        c.gT = dsc(f"gT_{s}", [16, 128, T])
        c.QnT = dsc(f"QnT_{s}", [8, 128, T])
        c.QrT = dsc(f"QrT_{s}", [8, 64, T])
        c.KnT = dsc(f"KnT_{s}", [8, 128, T])
        c.V = dsc(f"V_{s}", [T, 1024])
        c.yT = dsc(f"yT_{s}", [16, 128, T])
        c.CT = dsc(f"CT_{s}", [T, T])
        c.NST = dsc(f"NST_{s}", [T, T])
        S.append(c)
    Gb = dsc("Gb", [4, 132, WFULL])

    uid = [0]

    def sb(es, name, shape, dt):
        uid[0] += 1
        return es.enter_context(nc.sbuf_tensor(f"sb{uid[0]}_{name}", list(shape), dt))

    def ps(es, name, shape=(128, 512), dt=F32):
        uid[0] += 1
        return es.enter_context(nc.psum_tensor(f"ps{uid[0]}_{name}", list(shape), dt))

    nc_allow = nc.allow_non_contiguous_dma("small strided const loads")
    nc_allow.__enter__()
    nc_lp = nc.allow_low_precision("bf16 matmul operands by design")
    nc_lp.__enter__()

    def phase_tables(si, variant=""):
        c = S[si]
        T = c.T
        nkb = len(blocks(T, 512))
        with ExitStack() as es:
            P = Prog()
            tj = [sb(es, f"tj{i}", [128, 2, 512], F32) for i in range(2)]
            tk = [sb(es, f"tk{i}", [128, nkb, 2], F32) for i in range(2)]
            ntk = [sb(es, f"ntk{i}", [128, nkb, 2], F32) for i in range(2)]
            tmp = [sb(es, f"tmp{i}", [128, 2, 512], F32) for i in range(2)]
            tm2 = [sb(es, f"tm2{i}", [128, 2, 512], F32) for i in range(2)]
            ot = [sb(es, f"ot{i}", [128, 2, 512], BF16) for i in range(2)]
            Btj = [Buf("tj0"), Buf("tj1")]
            Btk = [Buf("tk0"), Buf("tk1")]
            Btmp = [Buf("tmp0"), Buf("tmp1")]
            Bot = [Buf("ot0"), Buf("ot1")]
            it = 0
            for si_, (s0, ns) in enumerate(blocks(T, 128)):
                a_ = si_ % 2
                P.op("sp", lambda E, a_=a_, s0=s0, ns=ns: E.dma_start(out=tj[a_][:ns], in_=tjd[si][s0:s0 + ns]), writes=[Btj[a_]], key=f"tj{a_}")
                if "B" in variant:
                    P.op("dve", lambda E, a_=a_, ns=ns: E.memset(tk[a_][:ns], 0.5), writes=[Btk[a_]])
                else:
                    P.op("sp", lambda E, a_=a_, s0=s0, ns=ns: E.dma_start(out=tk[a_][:ns], in_=tkd[si][s0:s0 + ns]), writes=[Btk[a_]], key=f"tk{a_}")
                P.op("dve", lambda E, a_=a_, ns=ns: E.tensor_scalar(ntk[a_][:ns], tk[a_][:ns], -1.0, None, ALU.mult), reads=[Btk[a_]], writes=[Btk[a_]])
                for kb, (k0, nk) in enumerate(blocks(T, 512)):
                    b = it % 2
                    it += 1
                    ck, sk = tk[a_][:ns, kb, 0:1], tk[a_][:ns, kb, 1:2]
                    nck, nsk = ntk[a_][:ns, kb, 0:1], ntk[a_][:ns, kb, 1:2]
                    P.op("dve", lambda E, a_=a_, b=b, ns=ns, nk=nk, ck=ck: E.tensor_scalar(tmp[b][:ns, 0, :nk], tj[a_][:ns, 0, :nk], ck, None, ALU.mult),
                         reads=[Btj[a_], Btk[a_]], writes=[Btmp[b]])
                    P.op("dve", lambda E, a_=a_, b=b, ns=ns, nk=nk, nsk=nsk: E.tensor_scalar(tm2[b][:ns, 0, :nk], tj[a_][:ns, 1, :nk], nsk, None, ALU.mult),
                         reads=[Btj[a_], Btk[a_]], writes=[Btmp[b]])
                    P.op("dve", lambda E, a_=a_, b=b, ns=ns, nk=nk: E.tensor_tensor(ot[b][:ns, 0, :nk], tm2[b][:ns, 0, :nk], tmp[b][:ns, 0, :nk], ALU.add),
                         reads=[Btmp[b]], writes=[Bot[b]])
                    P.op("dve", lambda E, a_=a_, b=b, ns=ns, nk=nk, nck=nck: E.tensor_scalar(tmp[b][:ns, 1, :nk], tj[a_][:ns, 1, :nk], nck, None, ALU.mult),
                         reads=[Btj[a_], Btk[a_]], writes=[Btmp[b]])
                    P.op("dve", lambda E, a_=a_, b=b, ns=ns, nk=nk, nsk=nsk: E.tensor_scalar(tm2[b][:ns, 1, :nk], tj[a_][:ns, 0, :nk], nsk, None, ALU.mult),
                         reads=[Btj[a_], Btk[a_]], writes=[Btmp[b]])
                    P.op("dve", lambda E, a_=a_, b=b, ns=ns, nk=nk: E.tensor_tensor(ot[b][:ns, 1, :nk], tm2[b][:ns, 1, :nk], tmp[b][:ns, 1, :nk], ALU.add),
                         reads=[Btmp[b]], writes=[Bot[b]])
                    P.op("sp", lambda E, b=b, s0=s0, ns=ns, k0=k0, nk=nk: E.dma_start(out=c.CT[s0:s0 + ns, k0:k0 + nk], in_=ot[b][:ns, 0, :nk]),
                         reads=[Bot[b]], key=f"c{b}")
                    if "A" not in variant:
                        P.op("sp", lambda E, b=b, s0=s0, ns=ns, k0=k0, nk=nk: E.dma_start(out=c.NST[s0:s0 + ns, k0:k0 + nk], in_=ot[b][:ns, 1, :nk]),
                             reads=[Bot[b]], key=f"n{b}")
            P.emit(nc, f"tb{si}")

    def phase_bias():
        with ExitStack() as es:
            P = Prog()
            oh = sb(es, "oh", [32, WFULL], F32)
            rb = sb(es, "rb", [32, 4], F32)
            ohh = sb(es, "ohh", [32, WFULL], F32)
            one = sb(es, "one32", [32, 128], F32)
            gsb = sb(es, "gsb", [128, WFULL], BF16)
            pss = [ps(es, f"pb{i}") for i in range(4)]
            Boh, Brb, Bohh, Bone, Bg = Buf("oh"), Buf("rb"), Buf("ohh"), Buf("one"), Buf("g")
            Bps = [PB(f"pb{i}") for i in range(4)]
            P.op("sp", lambda E: E.dma_start(out=oh[:], in_=oh_d[:, :]), writes=[Boh], key="l0")
            P.op("sp", lambda E: E.dma_start(out=rb[:], in_=rel_bias[:, :]), writes=[Brb], key="l1")
            P.op("dve", lambda E: E.memset(one[:], 1.0), writes=[Bone])
            for h in range(4):
                P.op("dve", lambda E, h=h: E.tensor_scalar(ohh[:], oh[:], rb[:, h:h + 1], None, ALU.mult), reads=[Boh, Brb], writes=[Bohh])
                for i, (n0, nn) in enumerate(blocks(WFULL, 512)):
                    P.op("pe", lambda E, i=i, n0=n0, nn=nn: E.matmul(pss[i][:, :nn], one[:, :], ohh[:, n0:n0 + nn], start=True, stop=True),
                         reads=[Bohh, Bone], writes=[Bps[i]])
                    P.op("act", lambda E, i=i, n0=n0, nn=nn: E.activation(gsb[:, n0:n0 + nn], pss[i][:, :nn], AF.Copy), reads=[Bps[i]], writes=[Bg])
                P.op("sp", lambda E, h=h: E.dma_start(out=Gb[h, 0:128, :], in_=gsb[:]), reads=[Bg], key="st")
            P.emit(nc, "bias")

    SG = 2064

    def phase_inproj(l, si, xsrc, variant=""):
        c = S[si]
        T = c.T
        w_l = w_in[l].rearrange("(c p) n -> p c n", p=128)
        chunks = [("uf", C_UF, 512), ("cq", C_CQ, 512), ("cq", C_CQ + 512, 256), ("ckv", C_CKV, 512), ("kr", C_KR, 64),
                  ("qd", C_QD, 512), ("kd", C_KD, 512), ("vd", C_VD, 512)] + [("gate", C_G + 512 * i, 512) for i in range(4)]
        with ExitStack() as es:
            P = Prog()
            hT = sb(es, "hT", [128, 16, SG], BF16)
            wb = [sb(es, f"wb{i}", [128, 16, 512], BF16) for i in range(2)]
            wsw = sb(es, "wsw", [128, 16, 64], BF16)
            wf = sb(es, "wf", [128, 16, 512], F32)
            Bwf = Buf("wf")
            xt = [sb(es, f"xt{i}", [128, D], F32) for i in range(2)]
            hb = [sb(es, f"hb{i}", [128, D], BF16) for i in range(2)]
            junk = sb(es, "junk", [128, D], BF16)
            gt = sb(es, "gt", [128, D], F32)
            grow = sb(es, "grow", [1, D], F32)
            one1 = sb(es, "one1", [1, 128], F32)
            ss = [sb(es, f"ss{i}", [128, 2], F32) for i in range(2)]
            identf = sb(es, "identf", [128, 128], F32)
            ident = sb(es, "ident", [128, 128], BF16)
            csf = sb(es, "csf", [128, 256], F32)
            csb = sb(es, "csb", [128, 256], BF16)
            gq = sb(es, "gq", [128, 6], F32)
            gk = sb(es, "gk", [128, 4], F32)
            cst = [sb(es, f"cst{i}", [64, 512], F32) for i in range(2)]
            snt = [sb(es, f"snt{i}", [64, 512], F32) for i in range(2)]
            uT = [sb(es, f"uT{i}", [128, 512], BF16) for i in range(2)]
            stA = [sb(es, f"stA{i}", [128, 512], BF16) for i in range(3)]
            stB = [sb(es, f"stB{i}", [128, 512], BF16) for i in range(3)]
            r1 = [sb(es, f"r1_{i}", [64, 512], F32) for i in range(2)]
            r2 = [sb(es, f"r2_{i}", [64, 512], F32) for i in range(2)]
            pT = [ps(es, f"pT{i}", (128, 1024), BF16) for i in range(2)]
            pm = [ps(es, f"pm{i}") for i in range(4)]
            p2 = [ps(es, f"p2{i}") for i in range(2)]
            BhT = Buf("hT")
            Bwb = [Buf("wb0"), Buf("wb1")]
            Bwsw = Buf("wsw")
            Bxt = [Buf("xt0"), Buf("xt1")]
            Bhb = [Buf("hb0"), Buf("hb1")]
            Bjunk, Bgt, Bgrow, Bone1 = Buf("junk"), Buf("gt"), Buf("grow"), Buf("one1")
            Bss = [Buf("ss0"), Buf("ss1")]
            Bid, Bcs, Bgq = Buf("id"), Buf("cs"), Buf("gq")
            Bcst = [Buf("cst0"), Buf("cst1")]
            BuT = [Buf("uT0"), Buf("uT1")]
            BstA = [Buf(f"stA{i}") for i in range(3)]
            BstB = [Buf(f"stB{i}") for i in range(3)]
            Br = [Buf("r0"), Buf("r1")]
            BpT = [PB("pT0"), PB("pT1")]
            Bpm = [PB(f"pm{i}") for i in range(4)]
            Bp2 = [PB("p20"), PB("p21")]
            P.op("sp", lambda E: E.dma_start(out=identf[:], in_=ident_d[:, :]), writes=[Bid], key="c0")
            P.op("dve", lambda E: E.tensor_copy(ident[:], identf[:]), reads=[Bid], writes=[Bid])
            P.op("sp", lambda E: E.dma_start(out=csf[:], in_=cs_d[:, :]), writes=[Bcs], key="c1")
            P.op("dve", lambda E: E.tensor_copy(csb[:], csf[:]), reads=[Bcs], writes=[Bcs])
            P.op("sp", lambda E: E.dma_start(out=gq[:], in_=q_norm[l].rearrange("(c p) -> p c", p=128), allow_slow_non_contiguous=True), writes=[Bgq], key="c2")
            P.op("sp", lambda E: E.dma_start(out=gk[:], in_=kv_norm[l].rearrange("(c p) -> p c", p=128), allow_slow_non_contiguous=True), writes=[Bgq], key="c3")
            P.op("sp", lambda E: E.dma_start(out=grow[:], in_=norm_w[l:l + 1, :]), writes=[Bgrow], key="c4")
            P.op("dve", lambda E: E.memset(one1[:], 1.0), writes=[Bone1])
            for i in range(4):
                P.op("pe", lambda E, i=i: E.matmul(pm[i][:, :], one1[:, :], grow[:, i * 512:(i + 1) * 512], start=True, stop=True),
                     reads=[Bone1, Bgrow], writes=[Bpm[i]])
                P.op("dve", lambda E, i=i: E.tensor_copy(gt[:, i * 512:(i + 1) * 512], pm[i][:, :]), reads=[Bpm[i]], writes=[Bgt])
            cnt = {"x": 0, "w": 0, "pm": 0, "p2": 0, "sa": 0, "sb": 0, "u": 0, "r": 0, "cs": 0, "pT": 0}

            def nxt(k, n):
                v = cnt[k] % n
                cnt[k] += 1
                return v

            for (g0, gn) in blocks(T, SG):
                for (t0, tn) in blocks(gn, 128):
                    b = nxt("x", 2)
                    P.op("sp", lambda E, b=b, t0=t0, tn=tn, g0=g0: E.dma_start(out=xt[b][:tn, :], in_=xsrc[g0 + t0:g0 + t0 + tn, :]),
                         writes=[Bxt[b]], key=f"x{b}")
                    P.op("act", lambda E, b=b, tn=tn: E.activation(junk[:tn, :], xt[b][:tn, :], AF.Square, accum_out=ss[b][:tn, 0:1]),
                         reads=[Bxt[b]], writes=[Bjunk, Bss[b]])
                    P.op("dve", lambda E, b=b, tn=tn: E.tensor_scalar(ss[b][:tn, 1:2], ss[b][:tn, 0:1], 1.0 / D, EPS, ALU.mult, ALU.add),
                         reads=[Bss[b]], writes=[Bss[b]])
                    P.op("act", lambda E, b=b, tn=tn: E.activation(ss[b][:tn, 1:2], ss[b][:tn, 1:2], AF.Sqrt), reads=[Bss[b]], writes=[Bss[b]])
                    P.op("dve", lambda E, b=b, tn=tn: E.reciprocal(ss[b][:tn, 1:2], ss[b][:tn, 1:2]), reads=[Bss[b]], writes=[Bss[b]])
                    P.op("dve", lambda E, b=b, tn=tn: E.tensor_scalar(xt[b][:tn, :], xt[b][:tn, :], ss[b][:tn, 1:2], None, ALU.mult),
                         reads=[Bxt[b], Bss[b]], writes=[Bxt[b]])
                    P.op("dve", lambda E, b=b, tn=tn: E.tensor_tensor(hb[b][:tn, :], xt[b][:tn, :], gt[:tn, :], ALU.mult),
                         reads=[Bxt[b], Bgt], writes=[Bhb[b]])
                    for half in range(2):
                        pb = nxt("pT", 2)
                        for cc in range(8):
                            ch = half * 8 + cc
                            P.op("pe", lambda E, b=b, pb=pb, cc=cc, ch=ch, tn=tn: E.transpose(pT[pb][:, cc * 128:cc * 128 + tn], hb[b][:tn, ch * 128:(ch + 1) * 128], ident[:tn, :tn]),
                                 reads=[Bhb[b], Bid], writes=[BpT[pb]])
                        src = lambda pb=pb, tn=tn: pT[pb][:, :].rearrange("p (c t) -> p c t", t=128)[:, :, :tn]
                        dst = lambda half=half, t0=t0, tn=tn: hT[:, half * 8:half * 8 + 8, t0:t0 + tn]
                        if False:
                            P.op("act", lambda E, src=src, dst=dst: E.activation(dst(), src(), AF.Copy), reads=[BpT[pb]], writes=[BhT])
                        else:
                            P.op("dve", lambda E, src=src, dst=dst: E.tensor_copy(dst(), src()), reads=[BpT[pb]], writes=[BhT])
                for (kind, col0, ncols) in chunks:
                    if variant and kind not in variant.split(","):
                        continue
                    wbi = nxt("w", 2)
                    for hf in range(2):
                        P.op("sp", lambda E, col0=col0, ncols=ncols, hf=hf: E.dma_start(out=wf[:, hf * 8:hf * 8 + 8, :ncols], in_=w_l[:, hf * 8:hf * 8 + 8, col0:col0 + ncols]),
                             writes=[Bwf], key="wf")
                    P.op("dve", lambda E, wbi=wbi, ncols=ncols: E.tensor_copy(wb[wbi][:, :, :ncols], wf[:, :, :ncols]), reads=[Bwf], writes=[Bwb[wbi]])
                    if kind == "kr":
                        P.op("dve", lambda E: E.tensor_copy(wsw[:, :, 0:32], wf[:, :, 32:64]), reads=[Bwf], writes=[Bwsw])
                        P.op("dve", lambda E: E.tensor_copy(wsw[:, :, 32:64], wf[:, :, 0:32]), reads=[Bwf], writes=[Bwsw])
                    for (q0, qn) in blocks(gn, 512):
                        tg = g0 + q0
                        if kind == "vd":
                            for (t0, tn) in blocks(qn, 128):
                                pb = nxt("pm", 4)
                                for ch in range(16):
                                    P.op("pe", lambda E, pb=pb, ch=ch, tn=tn, a=q0 + t0, wbi=wbi: E.matmul(pm[pb][:tn, :512], hT[:, ch, a:a + tn], wb[wbi][:, ch, :512], start=(ch == 0), stop=(ch == 15)),
                                         reads=[BhT, Bwb[wbi]], writes=[Bpm[pb]])
                                sa = nxt("sa", 3)
                                P.op("act", lambda E, pb=pb, sa=sa, tn=tn: E.activation(stA[sa][:tn, :], pm[pb][:tn, :], AF.Copy), reads=[Bpm[pb]], writes=[BstA[sa]])
                                P.op("sp", lambda E, sa=sa, tn=tn, a=tg + t0: E.dma_start(out=c.Vd[a:a + tn, :], in_=stA[sa][:tn, :]), reads=[BstA[sa]], key=f"sa{sa}")
                            continue
                        if kind == "kr":
                            cb = nxt("cs", 2)
                            P.op("sp", lambda E, cb=cb, qn=qn, tg=tg: E.dma_start(out=cst[cb][:, :qn], in_=cos2[si][:, tg:tg + qn]), writes=[Bcst[cb]], key=f"cs{cb}")
                            P.op("sp", lambda E, cb=cb, qn=qn, tg=tg: E.dma_start(out=snt[cb][:, :qn], in_=sin2[si][:, tg:tg + qn]), writes=[Bcst[cb]], key=f"sn{cb}")
                            pu, pv = nxt("pm", 4), nxt("pm", 4)
                            for ch in range(16):
                                P.op("pe", lambda E, pu=pu, ch=ch, qn=qn, q0=q0, wbi=wbi: E.matmul(pm[pu][:64, :qn], wb[wbi][:, ch, 0:64], hT[:, ch, q0:q0 + qn], start=(ch == 0), stop=(ch == 15)),
                                     reads=[BhT, Bwb[wbi]], writes=[Bpm[pu]])
                            for ch in range(16):
                                P.op("pe", lambda E, pv=pv, ch=ch, qn=qn, q0=q0: E.matmul(pm[pv][:64, :qn], wsw[:, ch, 0:64], hT[:, ch, q0:q0 + qn], start=(ch == 0), stop=(ch == 15)),
                                     reads=[BhT, Bwsw], writes=[Bpm[pv]])
                            rb_ = nxt("r", 2)
                            sa = nxt("sa", 3)
                            P.op("dve", lambda E, pu=pu, rb_=rb_, cb=cb, qn=qn: E.tensor_tensor(r1[rb_][:, :qn], pm[pu][:64, :qn], cst[cb][:, :qn], ALU.mult),
                                 reads=[Bpm[pu], Bcst[cb]], writes=[Br[rb_]])
                            P.op("dve", lambda E, pv=pv, rb_=rb_, cb=cb, qn=qn: E.tensor_tensor(r2[rb_][:, :qn], pm[pv][:64, :qn], snt[cb][:, :qn], ALU.mult),
                                 reads=[Bpm[pv], Bcst[cb]], writes=[Br[rb_]])
                            P.op("dve", lambda E, rb_=rb_, sa=sa, qn=qn: E.tensor_tensor(stA[sa][:64, :qn], r1[rb_][:, :qn], r2[rb_][:, :qn], ALU.add),
                                 reads=[Br[rb_]], writes=[BstA[sa]])
                            P.op("sp", lambda E, sa=sa, qn=qn, tg=tg: E.dma_start(out=c.KrT[:, tg:tg + qn], in_=stA[sa][:64, :qn]), reads=[BstA[sa]], key=f"sa{sa}")
                            continue
                        for (m0, mn) in blocks(ncols, 128):
                            pb = nxt("pm", 4)
                            for ch in range(16):
                                P.op("pe", lambda E, pb=pb, ch=ch, qn=qn, q0=q0, m0=m0, mn=mn, wbi=wbi: E.matmul(pm[pb][:mn, :qn], wb[wbi][:, ch, m0:m0 + mn], hT[:, ch, q0:q0 + qn], start=(ch == 0), stop=(ch == 15)),
                                     reads=[BhT, Bwb[wbi]], writes=[Bpm[pb]])
                            gi = (col0 + m0)
                            if kind == "uf":
                                ub = nxt("u", 2)
                                g = m0 // 128
                                P.op("act", lambda E, pb=pb, ub=ub, qn=qn: E.activation(uT[ub][:, :qn], pm[pb][:, :qn], AF.Copy), reads=[Bpm[pb]], writes=[BuT[ub]])
                                for (t0, tn) in blocks(qn, 128):
                                    p2b = nxt("p2", 2)
                                    P.op("pe", lambda E, p2b=p2b, ub=ub, t0=t0, tn=tn: E.matmul(p2[p2b][:tn, :256], uT[ub][:, t0:t0 + tn], csb[:, :], start=True, stop=True),
                                         reads=[BuT[ub], Bcs], writes=[Bp2[p2b]])
                                    sb_ = nxt("sb", 3)
                                    P.op("dve", lambda E, p2b=p2b, sb_=sb_, tn=tn: E.tensor_copy(stB[sb_][:tn, :256], p2[p2b][:tn, :256]), reads=[Bp2[p2b]], writes=[BstB[sb_]])
                                    P.op("sp", lambda E, sb_=sb_, tn=tn, a=tg + t0, g=g: E.dma_start(out=c.AB[a:a + tn, g, :], in_=stB[sb_][:tn, :256]), reads=[BstB[sb_]], key=f"sb{sb_}")
                            elif kind in ("cq", "ckv"):
                                ci = (gi - (C_CQ if kind == "cq" else C_CKV)) // 128
                                gv = gq if kind == "cq" else gk
                                dg, dsq = (c.cqg, c.cqs) if kind == "cq" else (c.ckg, c.cks)
                                sa, sb_ = nxt("sa", 3), nxt("sb", 3)
                                P.op("dve", lambda E, pb=pb, sa=sa, qn=qn, gv=gv, ci=ci: E.tensor_scalar(stA[sa][:, :qn], pm[pb][:, :qn], gv[:, ci:ci + 1], None, ALU.mult),
                                     reads=[Bpm[pb], Bgq], writes=[BstA[sa]])
                                P.op("act", lambda E, pb=pb, sb_=sb_, qn=qn: E.activation(stB[sb_][:, :qn], pm[pb][:, :qn], AF.Square), reads=[Bpm[pb]], writes=[BstB[sb_]])
                                P.op("sp", lambda E, sa=sa, qn=qn, tg=tg, dg=dg, ci=ci: E.dma_start(out=dg[ci, :, tg:tg + qn], in_=stA[sa][:, :qn]), reads=[BstA[sa]], key=f"sa{sa}")
                                P.op("sp", lambda E, sb_=sb_, qn=qn, tg=tg, dsq=dsq, ci=ci: E.dma_start(out=dsq[ci, :, tg:tg + qn], in_=stB[sb_][:, :qn]), reads=[BstB[sb_]], key=f"sb{sb_}")
                            elif kind in ("qd", "kd"):
                                h = m0 // 128
                                dd = c.qdT if kind == "qd" else c.kdT
                                sa = nxt("sa", 3)
                                P.op("act", lambda E, pb=pb, sa=sa, qn=qn: E.activation(stA[sa][:, :qn], pm[pb][:, :qn], AF.Copy), reads=[Bpm[pb]], writes=[BstA[sa]])
                                P.op("sp", lambda E, sa=sa, qn=qn, tg=tg, dd=dd, h=h: E.dma_start(out=dd[2 * h, :, tg:tg + qn], in_=stA[sa][0:64, :qn]), reads=[BstA[sa]], key=f"sa{sa}")
                                P.op("sp", lambda E, sa=sa, qn=qn, tg=tg, dd=dd, h=h: E.dma_start(out=dd[2 * h + 1, :, tg:tg + qn], in_=stA[sa][64:128, :qn]), reads=[BstA[sa]], key=f"sa{sa}")
                            elif kind == "gate":
                                ci = (gi - C_G) // 128
                                sb_ = nxt("sb", 3)
                                P.op("act", lambda E, pb=pb, sb_=sb_, qn=qn: E.activation(stB[sb_][:, :qn], pm[pb][:, :qn], AF.Silu), reads=[Bpm[pb]], writes=[BstB[sb_]])
                                P.op("sp", lambda E, sb_=sb_, qn=qn, tg=tg, ci=ci: E.dma_start(out=c.gT[ci, :, tg:tg + qn], in_=stB[sb_][:, :qn]), reads=[BstB[sb_]], key=f"sb{sb_}")
            if debug:
                dh = dsc(f"dbg_hT{l}{si}", [128, 128])
                dw = dsc(f"dbg_wb{l}{si}", [128, 128])
                dp = dsc(f"dbg_hb{l}{si}", [128, 128])
                P.op("sp", lambda E: E.dma_start(out=dh[:, :], in_=hT[:, 0, 0:128]), reads=[BhT], key="dbg0")
                P.op("sp", lambda E: E.dma_start(out=dw[:, :], in_=wb[0][:, 0, 0:128]), reads=[Bwb[0]], key="dbg1")
                P.op("sp", lambda E: E.dma_start(out=dp[:, :], in_=hb[0][:, 0:128]), reads=[Bhb[0]], key="dbg2")
            P.emit(nc, f"ip{l}{si}")


    def rstd_ops(P, src_ap, dst_ap, mult, eps, Bsrc, Bdst):
        P.op("dve", lambda E: E.tensor_scalar(dst_ap(), src_ap(), mult, eps, ALU.mult, ALU.add), reads=[Bsrc], writes=[Bdst])
        P.op("act", lambda E: E.activation(dst_ap(), dst_ap(), AF.Sqrt), reads=[Bdst], writes=[Bdst])
        P.op("dve", lambda E: E.reciprocal(dst_ap(), dst_ap()), reads=[Bdst], writes=[Bdst])

    def phase_q(l, si):
        c = S[si]
        T = c.T
        wq_l = w_uq[l].rearrange("(c p) n -> p c n", p=128)
        with ExitStack() as es:
            P = Prog()
            wf = sb(es, "wf", [128, 6, 1536], F32)
            wq = sb(es, "wq", [128, 6, 1536], BF16)
            wsw = sb(es, "wsw", [128, 6, 8, 64], BF16)
            ones = sb(es, "ones", [128, 128], BF16)
            cg = [sb(es, f"cg{i}", [128, 6, 512], BF16) for i in range(2)]
            cq2 = [sb(es, f"cq2{i}", [128, 6, 512], BF16) for i in range(2)]
            cst = [sb(es, f"cst{i}", [64, 512], F32) for i in range(2)]
            snt = [sb(es, f"snt{i}", [64, 512], F32) for i in range(2)]
            rst = [sb(es, f"rst{i}", [128, 512], F32) for i in range(2)]
            stn = [sb(es, f"stn{i}", [128, 512], BF16) for i in range(3)]
            r1 = [sb(es, f"r1{i}", [64, 512], F32) for i in range(2)]
            r2 = [sb(es, f"r2{i}", [64, 512], F32) for i in range(2)]
            pss = ps(es, "pss")
            pn = [ps(es, f"pn{i}") for i in range(2)]
            pu = [ps(es, f"pu{i}") for i in range(2)]
            pv = [ps(es, f"pv{i}") for i in range(2)]
            Bwf, Bwq, Bones = Buf("wf"), Buf("wq"), Buf("ones")
            Bcg = [Buf("cg0"), Buf("cg1")]
            Bcs = [Buf("cs0"), Buf("cs1")]
            Brst = [Buf("rst0"), Buf("rst1")]
            Bstn = [Buf(f"stn{i}") for i in range(3)]
            Br = [Buf("r0"), Buf("r1")]
            Bpss = PB("pss")
            Bpn, Bpu, Bpv = [PB("pn0"), PB("pn1")], [PB("pu0"), PB("pu1")], [PB("pv0"), PB("pv1")]
            P.op("sp", lambda E: E.dma_start(out=wf[:], in_=wq_l), writes=[Bwf], key="wf")
            P.op("dve", lambda E: E.tensor_copy(wq[:], wf[:]), reads=[Bwf], writes=[Bwq])
            wfv = wf[:].rearrange("p c (h f) -> p c h f", f=192)
            for cc in range(6):
                P.op("dve", lambda E, cc=cc: E.tensor_copy(wsw[:, cc, :, 0:32], wfv[:, cc, :, 160:192]), reads=[Bwf], writes=[Bwq])
                P.op("dve", lambda E, cc=cc: E.tensor_copy(wsw[:, cc, :, 32:64], wfv[:, cc, :, 128:160]), reads=[Bwf], writes=[Bwq])
            P.op("dve", lambda E: E.memset(ones[:], 1.0), writes=[Bones])
            cnt = {}

            def nxt(k, n):
                v = cnt.get(k, 0)
                cnt[k] = v + 1
                return v % n

            for (q0, qn) in blocks(T, 512):
                b = nxt("b", 2)
                P.op("sp", lambda E, b=b, q0=q0, qn=qn: E.dma_start(out=cg[b][:, :, :qn], in_=c.cqg[:, :, q0:q0 + qn].rearrange("c p t -> p c t")), writes=[Bcg[b]], key=f"cg{b}")
                P.op("sp", lambda E, b=b, q0=q0, qn=qn: E.dma_start(out=cq2[b][:, :, :qn], in_=c.cqs[:, :, q0:q0 + qn].rearrange("c p t -> p c t")), writes=[Bcg[b]], key=f"cs{b}")
                P.op("sp", lambda E, b=b, q0=q0, qn=qn: E.dma_start(out=cst[b][:, :qn], in_=cos2[si][:, q0:q0 + qn]), writes=[Bcs[b]], key=f"co{b}")
                P.op("sp", lambda E, b=b, q0=q0, qn=qn: E.dma_start(out=snt[b][:, :qn], in_=sin2[si][:, q0:q0 + qn]), writes=[Bcs[b]], key=f"si{b}")
                for cc in range(6):
                    P.op("pe", lambda E, b=b, cc=cc, qn=qn: E.matmul(pss[:, :qn], ones[:, :], cq2[b][:, cc, :qn], start=(cc == 0), stop=(cc == 5)),
                         reads=[Bones, Bcg[b]], writes=[Bpss])
                rstd_ops(P, lambda qn=qn: pss[:, :qn], lambda b=b, qn=qn: rst[b][:, :qn], 1.0 / 768, EPS, Bpss, Brst[b])
                for h in range(8):
                    i = nxt("pn", 2)
                    for cc in range(6):
                        P.op("pe", lambda E, i=i, b=b, cc=cc, qn=qn, h=h: E.matmul(pn[i][:, :qn], wq[:, cc, h * 192:h * 192 + 128], cg[b][:, cc, :qn], start=(cc == 0), stop=(cc == 5)),
                             reads=[Bwq, Bcg[b]], writes=[Bpn[i]])
                    for cc in range(6):
                        P.op("pe", lambda E, i=i, b=b, cc=cc, qn=qn, h=h: E.matmul(pu[i][:64, :qn], wq[:, cc, h * 192 + 128:h * 192 + 192], cg[b][:, cc, :qn], start=(cc == 0), stop=(cc == 5)),
                             reads=[Bwq, Bcg[b]], writes=[Bpu[i]])
                    for cc in range(6):
                        P.op("pe", lambda E, i=i, b=b, cc=cc, qn=qn, h=h: E.matmul(pv[i][:64, :qn], wsw[:, cc, h, :], cg[b][:, cc, :qn], start=(cc == 0), stop=(cc == 5)),
                             reads=[Bwq, Bcg[b]], writes=[Bpv[i]])
                    s1 = nxt("st", 3)
                    P.op("dve", lambda E, i=i, b=b, s1=s1, qn=qn: E.tensor_tensor(stn[s1][:, :qn], pn[i][:, :qn], rst[b][:, :qn], ALU.mult),
                         reads=[Bpn[i], Brst[b]], writes=[Bstn[s1]])
                    P.op("sp", lambda E, s1=s1, h=h, q0=q0, qn=qn: E.dma_start(out=c.QnT[h, :, q0:q0 + qn], in_=stn[s1][:, :qn]), reads=[Bstn[s1]], key=f"st{s1}")
                    rb_ = nxt("r", 2)
                    s2 = nxt("st", 3)
                    P.op("dve", lambda E, i=i, b=b, rb_=rb_, qn=qn: E.tensor_tensor(r1[rb_][:, :qn], pu[i][:64, :qn], cst[b][:, :qn], ALU.mult),
                         reads=[Bpu[i], Bcs[b]], writes=[Br[rb_]])
                    P.op("dve", lambda E, i=i, b=b, rb_=rb_, qn=qn: E.tensor_tensor(r2[rb_][:, :qn], pv[i][:64, :qn], snt[b][:, :qn], ALU.mult),
                         reads=[Bpv[i], Bcs[b]], writes=[Br[rb_]])
                    P.op("dve", lambda E, rb_=rb_, qn=qn: E.tensor_tensor(r1[rb_][:, :qn], r1[rb_][:, :qn], r2[rb_][:, :qn], ALU.add), reads=[Br[rb_]], writes=[Br[rb_]])
                    P.op("dve", lambda E, rb_=rb_, b=b, s2=s2, qn=qn: E.tensor_tensor(stn[s2][:64, :qn], r1[rb_][:, :qn], rst[b][:64, :qn], ALU.mult),
                         reads=[Br[rb_], Brst[b]], writes=[Bstn[s2]])
                    P.op("sp", lambda E, s2=s2, h=h, q0=q0, qn=qn: E.dma_start(out=c.QrT[h, :, q0:q0 + qn], in_=stn[s2][:64, :qn]), reads=[Bstn[s2]], key=f"st{s2}")
            P.emit(nc, f"q{l}{si}")

    def phase_kv(l, si):
        c = S[si]
        T = c.T
        wk_l = w_ukv[l].rearrange("(c p) n -> p c n", p=128)
        with ExitStack() as es:
            P = Prog()
            wf = sb(es, "wf", [128, 4, 2048], F32)
            wk = sb(es, "wk", [128, 4, 1024], BF16)
            wv = sb(es, "wv", [128, 4, 1024], BF16)
            ones = sb(es, "ones", [128, 128], BF16)
            cg = [sb(es, f"cg{i}", [128, 4, 512], BF16) for i in range(2)]
            cq2 = [sb(es, f"cq2{i}", [128, 4, 512], BF16) for i in range(2)]
            rst = [sb(es, f"rst{i}", [128, 512], F32) for i in range(2)]
            rcol = [sb(es, f"rcol{i}", [128, 1], F32) for i in range(2)]
            stn = [sb(es, f"stn{i}", [128, 512], BF16) for i in range(3)]
            pss = ps(es, "pss")
            pc = ps(es, "pc")
            pk = [ps(es, f"pk{i}") for i in range(3)]
            pvv = [ps(es, f"pvv{i}") for i in range(3)]
            Bwf, Bwk, Bones = Buf("wf"), Buf("wk"), Buf("ones")
            Bcg = [Buf("cg0"), Buf("cg1")]
            Brst = [Buf("rst0"), Buf("rst1")]
            Brc = [Buf("rc0"), Buf("rc1")]
            Bstn = [Buf(f"stn{i}") for i in range(3)]
            Bpss, Bpc = PB("pss"), PB("pc")
            Bpk = [PB(f"pk{i}") for i in range(3)]
            Bpvv = [PB(f"pvv{i}") for i in range(3)]
            P.op("sp", lambda E: E.dma_start(out=wf[:], in_=wk_l), writes=[Bwf], key="wf")
            wfv = wf[:].rearrange("p c (h two f) -> p c h two f", two=2, f=128)
            for cc in range(4):
                P.op("dve", lambda E, cc=cc: E.tensor_copy(wk[:, cc, :].rearrange("p (h f) -> p h f", f=128), wfv[:, cc, :, 0, :]), reads=[Bwf], writes=[Bwk])
                P.op("dve", lambda E, cc=cc: E.tensor_copy(wv[:, cc, :].rearrange("p (h f) -> p h f", f=128), wfv[:, cc, :, 1, :]), reads=[Bwf], writes=[Bwk])
            P.op("dve", lambda E: E.memset(ones[:], 1.0), writes=[Bones])
            cnt = {}

            def nxt(k, n):
                v = cnt.get(k, 0)
                cnt[k] = v + 1
                return v % n

            for (q0, qn) in blocks(T, 512):
                b = nxt("b", 2)
                P.op("sp", lambda E, b=b, q0=q0, qn=qn: E.dma_start(out=cg[b][:, :, :qn], in_=c.ckg[:, :, q0:q0 + qn].rearrange("c p t -> p c t")), writes=[Bcg[b]], key=f"cg{b}")
                P.op("sp", lambda E, b=b, q0=q0, qn=qn: E.dma_start(out=cq2[b][:, :, :qn], in_=c.cks[:, :, q0:q0 + qn].rearrange("c p t -> p c t")), writes=[Bcg[b]], key=f"cs{b}")
                for cc in range(4):
                    P.op("pe", lambda E, b=b, cc=cc, qn=qn: E.matmul(pss[:, :qn], ones[:, :], cq2[b][:, cc, :qn], start=(cc == 0), stop=(cc == 3)),
                         reads=[Bones, Bcg[b]], writes=[Bpss])
                rstd_ops(P, lambda qn=qn: pss[:, :qn], lambda b=b, qn=qn: rst[b][:, :qn], 1.0 / 512, EPS, Bpss, Brst[b])
                for h in range(8):
                    i = nxt("pk", 3)
                    for cc in range(4):
                        P.op("pe", lambda E, i=i, b=b, cc=cc, qn=qn, h=h: E.matmul(pk[i][:, :qn], wk[:, cc, h * 128:(h + 1) * 128], cg[b][:, cc, :qn], start=(cc == 0), stop=(cc == 3)),
                             reads=[Bwk, Bcg[b]], writes=[Bpk[i]])
                    s1 = nxt("st", 3)
                    P.op("dve", lambda E, i=i, b=b, s1=s1, qn=qn: E.tensor_tensor(stn[s1][:, :qn], pk[i][:, :qn], rst[b][:, :qn], ALU.mult),
                         reads=[Bpk[i], Brst[b]], writes=[Bstn[s1]])
                    P.op("sp", lambda E, s1=s1, h=h, q0=q0, qn=qn: E.dma_start(out=c.KnT[h, :, q0:q0 + qn], in_=stn[s1][:, :qn]), reads=[Bstn[s1]], key=f"st{s1}")
                for (t0, tn) in blocks(qn, 128):
                    rc = nxt("rc", 2)
                    for cc in range(4):
                        P.op("pe", lambda E, b=b, cc=cc, t0=t0, tn=tn: E.matmul(pc[:tn, 0:1], cq2[b][:, cc, t0:t0 + tn], ones[:, 0:1], start=(cc == 0), stop=(cc == 3)),
                             reads=[Bones, Bcg[b]], writes=[Bpc])
                    rstd_ops(P, lambda tn=tn: pc[:tn, 0:1], lambda rc=rc, tn=tn: rcol[rc][:tn, 0:1], 1.0 / 512, EPS, Bpc, Brc[rc])
                    for hh in range(2):
                        i = nxt("pvv", 3)
                        for cc in range(4):
                            P.op("pe", lambda E, i=i, b=b, cc=cc, t0=t0, tn=tn, hh=hh: E.matmul(pvv[i][:tn, :512], cg[b][:, cc, t0:t0 + tn], wv[:, cc, hh * 512:(hh + 1) * 512], start=(cc == 0), stop=(cc == 3)),
                                 reads=[Bwk, Bcg[b]], writes=[Bpvv[i]])
                        s1 = nxt("st", 3)
                        P.op("dve", lambda E, i=i, rc=rc, s1=s1, tn=tn: E.tensor_scalar(stn[s1][:tn, :], pvv[i][:tn, :], rcol[rc][:tn, 0:1], None, ALU.mult),
                             reads=[Bpvv[i], Brc[rc]], writes=[Bstn[s1]])
                        P.op("sp", lambda E, s1=s1, hh=hh, a=q0 + t0, tn=tn: E.dma_start(out=c.V[a:a + tn, hh * 512:(hh + 1) * 512], in_=stn[s1][:tn, :]), reads=[Bstn[s1]], key=f"st{s1}")
            P.emit(nc, f"kv{l}{si}")

    G.phase_q, G.phase_kv = phase_q, phase_kv

    MLA_SCALE = 1.0 / math.sqrt(192.0)

    def phase_mla(l, si):
        c = S[si]
        T = c.T
        kbs = blocks(T, 128)
        nkb = len(kbs)
        nfull = T // 128
        with ExitStack() as es:
            P = Prog()
            ones = sb(es, "ones", [128, 128], BF16)
            Kr = sb(es, "Kr", [64, T], BF16)
            Kn_ = [sb(es, f"Kn{i}", [128, T], BF16) for i in range(2)]
            Qn_ = [sb(es, f"Qn{i}", [128, T], BF16) for i in range(2)]
            Qr_ = [sb(es, f"Qr{i}", [64, T], BF16) for i in range(2)]
            Vh_ = [sb(es, f"Vh{i}", [128, nkb, 128], BF16) for i in range(2)]
            gt = [sb(es, f"gt{i}", [128, 512], BF16) for i in range(2)]
            PT = [sb(es, f"PT{i}", [128, 512], BF16) for i in range(4)]
            acc = [sb(es, f"acc{i}", [128, 512], F32) for i in range(2)]
            accb = sb(es, "accb", [128, 512], BF16)
            Bacc = [Buf("acc0"), Buf("acc1")]
            Baccb = Buf("accb")
            R = [sb(es, f"R{i}", [128, 512], F32) for i in range(2)]
            Y = [sb(es, f"Y{i}", [128, 512], F32) for i in range(2)]
            Yb = [sb(es, f"Yb{i}", [128, 512], BF16) for i in range(2)]
            pS = [ps(es, f"pS{i}") for i in range(3)]
            pO = [ps(es, f"pO{i}") for i in range(2)]
            pL = [ps(es, f"pL{i}") for i in range(2)]
            Bones, BKr = Buf("ones"), Buf("Kr")
            BK_, BQ_, BV_ = [Buf("K0"), Buf("K1")], [Buf("Q0"), Buf("Q1")], [Buf("V0"), Buf("V1")]
            Bgt = [Buf("gt0"), Buf("gt1")]
            BPT = [Buf(f"PT{i}") for i in range(4)]
            BR = [Buf("R0"), Buf("R1")]
            BY = [Buf("Y0"), Buf("Y1")]
            BYb = [Buf("Yb0"), Buf("Yb1")]
            BpS = [PB(f"pS{i}") for i in range(3)]
            BpO = [PB("pO0"), PB("pO1")]
            BpL = [PB("pL0"), PB("pL1")]
            P.op("dve", lambda E: E.memset(ones[:], 1.0), writes=[Bones])
            P.op("sp", lambda E: E.dma_start(out=Kr[:, :], in_=c.KrT[:, :]), writes=[BKr], key="kr")
            cnt = {}

            def nxt(k, n):
                v = cnt.get(k, 0)
                cnt[k] = v + 1
                return v % n

            def load_head(h):
                s = h % 2
                P.op("sp", lambda E: E.dma_start(out=Kn_[s][:, :], in_=c.KnT[h, :, :]), writes=[BK_[s]], key=f"kn{s}")
                P.op("sp", lambda E: E.dma_start(out=Qn_[s][:, :], in_=c.QnT[h, :, :]), writes=[BQ_[s]], key=f"qn{s}")
                P.op("sp", lambda E: E.dma_start(out=Qr_[s][:, :], in_=c.QrT[h, :, :]), writes=[BQ_[s]], key=f"qr{s}")
                for (b0, bn) in blocks(nfull, 8):
                    P.op("sp", lambda E, b0=b0, bn=bn: E.dma_start(out=Vh_[s][:, b0:b0 + bn, :], in_=c.V[b0 * 128:(b0 + bn) * 128, h * 128:(h + 1) * 128].rearrange("(k p) d -> p k d", p=128)),
                         writes=[BV_[s]], key=f"v{s}")
                if T % 128:
                    P.op("sp", lambda E: E.dma_start(out=Vh_[s][:T % 128, nfull, :], in_=c.V[nfull * 128:T, h * 128:(h + 1) * 128]), writes=[BV_[s]], key=f"w{s}")

            load_head(0)
            for h in range(8):
                if h + 1 < 8:
                    load_head(h + 1)
                Kn, Qn, Qr, Vh = Kn_[h % 2], Qn_[h % 2], Qr_[h % 2], Vh_[h % 2]
                BK, BQ, BV = BK_[h % 2], BQ_[h % 2], BV_[h % 2]
                for (q0, qn) in blocks(T, 512):
                    gb = nxt("g", 2)
                    ob = nxt("o", 2)
                    P.op("sp", lambda E, gb=gb, h=h, q0=q0, qn=qn: E.dma_start(out=gt[gb][:, :qn], in_=c.gT[4 + h, :, q0:q0 + qn]), writes=[Bgt[gb]], key=f"g{gb}")

                    def qk(kb, q0=q0, qn=qn, Kn=Kn, Qn=Qn, Qr=Qr, BK=BK, BQ=BQ):
                        k0, nk = kbs[kb]
                        sbk = kb % 3
                        P.op("pe", lambda E: E.matmul(pS[sbk][:nk, :qn], Kn[:, k0:k0 + nk], Qn[:, q0:q0 + qn], start=True, stop=False),
                             reads=[BK, BQ], writes=[BpS[sbk]])
                        P.op("pe", lambda E: E.matmul(pS[sbk][:nk, :qn], Kr[:, k0:k0 + nk], Qr[:, q0:q0 + qn], start=False, stop=True),
                             reads=[BKr, BQ], writes=[BpS[sbk]])

                    P.op("dve", lambda E, ob=ob, qn=qn: E.memset(acc[ob][:, :qn], 0.0), writes=[Bacc[ob]])
                    qk(0)
                    for kb in range(nkb):
                        k0, nk = kbs[kb]
                        if kb + 1 < nkb:
                            qk(kb + 1)
                        sbk = kb % 3
                        pb = nxt("pt", 4)
                        P.op("act", lambda E, sbk=sbk, pb=pb, nk=nk, qn=qn: E.activation(PT[pb][:nk, :qn], pS[sbk][:nk, :qn], AF.Exp, scale=MLA_SCALE),
                             reads=[BpS[sbk]], writes=[BPT[pb]])
                        P.op("pe", lambda E, pb=pb, ob=ob, kb=kb, nk=nk, qn=qn, Vh=Vh: E.matmul(pO[ob][:, :qn], Vh[:nk, kb, :], PT[pb][:nk, :qn], start=(kb == 0), stop=(kb == nkb - 1)),
                             reads=[BV, BPT[pb]], writes=[BpO[ob]])
                        if kb % 2 == 0:
                            P.op("pe", lambda E, pb=pb, ob=ob, kb=kb, nk=nk, qn=qn: E.matmul(pL[ob][:, :qn], ones[:nk, :], PT[pb][:nk, :qn], start=(kb == 0), stop=False),
                                 reads=[Bones, BPT[pb]], writes=[BpL[ob]])
                        else:
                            P.op("dve", lambda E, pb=pb, ob=ob, nk=nk, qn=qn: E.tensor_tensor(acc[ob][:nk, :qn], acc[ob][:nk, :qn], PT[pb][:nk, :qn], ALU.add),
                                 reads=[BPT[pb], Bacc[ob]], writes=[Bacc[ob]])
                    P.op("dve", lambda E, ob=ob, qn=qn: E.tensor_copy(accb[:, :qn], acc[ob][:, :qn]), reads=[Bacc[ob]], writes=[Baccb])
                    P.op("pe", lambda E, ob=ob, qn=qn: E.matmul(pL[ob][:, :qn], ones[:, :], accb[:, :qn], start=False, stop=True),
                         reads=[Bones, Baccb], writes=[BpL[ob]])
                    P.op("dve", lambda E, ob=ob, qn=qn: E.reciprocal(R[ob][:, :qn], pL[ob][:, :qn]), reads=[BpL[ob]], writes=[BR[ob]])
                    P.op("dve", lambda E, ob=ob, qn=qn: E.tensor_tensor(Y[ob][:, :qn], pO[ob][:, :qn], R[ob][:, :qn], ALU.mult), reads=[BpO[ob], BR[ob]], writes=[BY[ob]])
                    P.op("dve", lambda E, ob=ob, gb=gb, qn=qn: E.tensor_tensor(Yb[ob][:, :qn], Y[ob][:, :qn], gt[gb][:, :qn], ALU.mult), reads=[BY[ob], Bgt[gb]], writes=[BYb[ob]])
                    P.op("sp", lambda E, ob=ob, h=h, q0=q0, qn=qn: E.dma_start(out=c.yT[4 + h, :, q0:q0 + qn], in_=Yb[ob][:, :qn]), reads=[BYb[ob]], key=f"y{ob}")
            P.emit(nc, f"mla{l}{si}")

    def phase_diff(l, si):
        c = S[si]
        T = c.T
        kbs = blocks(T, 128)
        nkb = len(kbs)
        nfull = T // 128
        lam_init = 0.8 - 0.6 * math.exp(-0.3 * l)
        with ExitStack() as es:
            P = Prog()
            ones = sb(es, "ones", [128, 128], BF16)
            onesf = sb(es, "onesf", [128, 128], F32)
            identf = sb(es, "identf", [128, 128], F32)
            ident = sb(es, "ident", [128, 128], BF16)
            lv = sb(es, "lv", [64, 4], F32)
            lp = sb(es, "lp", [64, 2], BF16)
            ee = sb(es, "ee", [128, 2], F32)
            neglam = sb(es, "neglam", [128, 1], F32)
            gd = sb(es, "gd", [128, 1], F32)
            Kd_ = [[sb(es, f"Kd{s}{a}", [64, T], BF16) for a in range(2)] for s in range(2)]
            Qd_ = [[sb(es, f"Qd{s}{a}", [64, T], BF16) for a in range(2)] for s in range(2)]
            Vh_ = [sb(es, f"Vh{s}", [128, nkb, 128], BF16) for s in range(2)]
            Bt_ = [sb(es, f"Bt{s}", [128, 6, 512], BF16) for s in range(2)]
            cfb_ = [sb(es, f"cfb{s}", [128, 2], BF16) for s in range(2)]
            cf_ = [sb(es, f"cf{s}", [128, 2], F32) for s in range(2)]
            gt = [sb(es, f"gt{i}", [128, 512], BF16) for i in range(2)]
            PT = [sb(es, f"PT{i}", [128, 512], BF16) for i in range(4)]
            acc = [sb(es, f"acc{i}", [128, 512], F32) for i in range(2)]
            accb = [sb(es, f"accb{i}", [128, 512], BF16) for i in range(2)]
            Bacc = [Buf("acc0"), Buf("acc1")]
            Baccb = [Buf("accb0"), Buf("accb1")]
            R = [sb(es, f"R{a}", [128, 512], F32) for a in range(2)]
            On = [sb(es, f"On{a}", [128, 512], F32) for a in range(2)]
            Dm = sb(es, "Dm", [128, 512], F32)
            D2 = sb(es, "D2", [128, 512], BF16)
            rs = sb(es, "rs", [128, 512], F32)
            Yb = [sb(es, f"Yb{i}", [128, 512], BF16) for i in range(2)]
            pS = [ps(es, f"pS{i}") for i in range(3)]
            pO = [ps(es, f"pO{i}") for i in range(2)]
            pL = [ps(es, f"pL{i}") for i in range(2)]
            pX = ps(es, "pX")
            Bones, Bid, Blv, Bee, Bnl, Bgd, Bcf = Buf("ones"), Buf("id"), Buf("lv"), Buf("ee"), Buf("nl"), Buf("gd"), Buf("cf")
            BK_, BQ_, BV_, BBt_, Bcf_ = ([Buf("K0"), Buf("K1")], [Buf("Q0"), Buf("Q1")], [Buf("V0"), Buf("V1")],
                                         [Buf("Bt0"), Buf("Bt1")], [Buf("cf0"), Buf("cf1")])
            Bgt = [Buf("gt0"), Buf("gt1")]
            BPT = [Buf(f"PT{i}") for i in range(4)]
            BR = [Buf("R0"), Buf("R1")]
            BOn = [Buf("On0"), Buf("On1")]
            BDm, BD2, Brs = Buf("Dm"), Buf("D2"), Buf("rs")
            BYb = [Buf("Yb0"), Buf("Yb1")]
            BpS = [PB(f"pS{i}") for i in range(3)]
            BpO = [PB("pO0"), PB("pO1")]
            BpL = [PB("pL0"), PB("pL1")]
            BpX = PB("pX")
            P.op("dve", lambda E: E.memset(ones[:], 1.0), writes=[Bones])
            P.op("dve", lambda E: E.memset(onesf[:], 1.0), writes=[Bones])
            P.op("sp", lambda E: E.dma_start(out=identf[:], in_=ident_d[:, :]), writes=[Bid], key="id")
            P.op("dve", lambda E: E.tensor_copy(ident[:], identf[:]), reads=[Bid], writes=[Bid])
            for i in range(4):
                P.op("sp", lambda E, i=i: E.dma_start(out=lv[:, i:i + 1], in_=lam_in[i][l].rearrange("(p o) -> p o", o=1)), writes=[Blv], key=f"lv{i}")
            P.op("sp", lambda E: E.dma_start(out=gd[:, :], in_=diff_norm[l].rearrange("(p o) -> p o", o=1)), writes=[Bgd], key="gd")
            P.op("dve", lambda E: E.tensor_scalar(gd[:, :], gd[:, :], 1.0 - lam_init, None, ALU.mult), reads=[Bgd], writes=[Bgd])
            P.op("dve", lambda E: E.tensor_tensor(lp[:, 0:1], lv[:, 0:1], lv[:, 1:2], ALU.mult), reads=[Blv], writes=[Blv])
            P.op("dve", lambda E: E.tensor_tensor(lp[:, 1:2], lv[:, 2:3], lv[:, 3:4], ALU.mult), reads=[Blv], writes=[Blv])
            P.op("pe", lambda E: E.matmul(pX[:, 0:2], ones[:64, :], lp[:, 0:2], start=True, stop=True), reads=[Bones, Blv], writes=[BpX])
            P.op("act", lambda E: E.activation(ee[:, :], pX[:, 0:2], AF.Exp), reads=[BpX], writes=[Bee])
            P.op("dve", lambda E: E.tensor_tensor(neglam[:, :], ee[:, 1:2], ee[:, 0:1], ALU.subtract), reads=[Bee], writes=[Bnl])
            P.op("dve", lambda E: E.tensor_scalar(neglam[:, :], neglam[:, :], -lam_init, None, ALU.add), reads=[Bnl], writes=[Bnl])
            cnt = {}

            def nxt(k, n):
                v = cnt.get(k, 0)
                cnt[k] = v + 1
                return v % n

            def load_head(h):
                s = h % 2
                for a in range(2):
                    P.op("sp", lambda E, a=a: E.dma_start(out=Kd_[s][a][:, :], in_=c.kdT[2 * h + a, :, :]), writes=[BK_[s]], key=f"k{s}{a}")
                    P.op("sp", lambda E, a=a: E.dma_start(out=Qd_[s][a][:, :], in_=c.qdT[2 * h + a, :, :]), writes=[BQ_[s]], key=f"q{s}{a}")
                for (b0, bn) in blocks(nfull, 8):
                    P.op("sp", lambda E, b0=b0, bn=bn: E.dma_start(out=Vh_[s][:, b0:b0 + bn, :], in_=c.Vd[b0 * 128:(b0 + bn) * 128, h * 128:(h + 1) * 128].rearrange("(k p) d -> p k d", p=128)),
                         writes=[BV_[s]], key=f"v{s}")
                if T % 128:
                    P.op("sp", lambda E: E.dma_start(out=Vh_[s][:T % 128, nfull, :], in_=c.Vd[nfull * 128:T, h * 128:(h + 1) * 128]), writes=[BV_[s]], key=f"w{s}")
                gflat = Gb[h].rearrange("r w -> (r w)")
                for i in range(6):
                    off = RMAX - (-128 + 128 * i)
                    P.op("sp", lambda E, i=i, off=off: E.dma_start(out=Bt_[s][:, i, :], in_=gflat[off:off + 128 * (WFULL - 1)].rearrange("(p w) -> p w", w=WFULL - 1)[:, 0:512]),
                         writes=[BBt_[s]], key=f"bt{s}")
                P.op("sp", lambda E: E.dma_start(out=cfb_[s][:, 0:1], in_=Gb[h, 0:128, 0:1], allow_slow_non_contiguous=True), writes=[Bcf_[s]], key=f"cfa{s}")
                P.op("sp", lambda E: E.dma_start(out=cfb_[s][:, 1:2], in_=Gb[h, 0:128, WFULL - 1:WFULL], allow_slow_non_contiguous=True), writes=[Bcf_[s]], key=f"cfb{s}")
                P.op("dve", lambda E: E.tensor_scalar(cf_[s][:, :], cfb_[s][:, :], 0.125, None, ALU.mult), reads=[Bcf_[s]], writes=[Bcf_[s]])

            load_head(0)
            for h in range(4):
                if h + 1 < 4:
                    load_head(h + 1)
                Kd, Qd, Vh, Bt, cf = Kd_[h % 2], Qd_[h % 2], Vh_[h % 2], Bt_[h % 2], cf_[h % 2]
                BK, BQ, BV, BBt, Bcf = BK_[h % 2], BQ_[h % 2], BV_[h % 2], BBt_[h % 2], Bcf_[h % 2]
                for (q0, qn) in blocks(T, 512):
                    gb = nxt("g", 2)
                    P.op("sp", lambda E, gb=gb, h=h, q0=q0, qn=qn: E.dma_start(out=gt[gb][:, :qn], in_=c.gT[12 + h, :, q0:q0 + qn]), writes=[Bgt[gb]], key=f"g{gb}")
                    steps = [(kb, a) for kb in range(nkb) for a in range(2)]

                    def klass(kb, q0=q0, qn=qn):
                        k0, nk = kbs[kb]
                        relmin, relmax = k0 - (q0 + qn - 1), k0 + nk - 1 - q0
                        if relmin >= 91:
                            return ("far", 0)
                        if relmax <= -91:
                            return ("far", 1)
                        d = k0 - q0
                        i = (d + 128) // 128
                        assert (d + 128) % 128 == 0 and 0 <= i < 6, (d, i)
                        return ("near", i)

                    def qk(st, q0=q0, qn=qn, Kd=Kd, Qd=Qd, Bt=Bt, BK=BK, BQ=BQ, BBt=BBt):
                        kb, a = steps[st]
                        k0, nk = kbs[kb]
                        sbk = st % 3
                        kl = klass(kb)
                        near = kl[0] == "near" and not DBG.get("nobias")
                        P.op("pe", lambda E: E.matmul(pS[sbk][:nk, :qn], Kd[a][:, k0:k0 + nk], Qd[a][:, q0:q0 + qn], start=True, stop=not near),
                             reads=[BK, BQ], writes=[BpS[sbk]])
                        if near:
                            P.op("pe", lambda E: E.matmul(pS[sbk][:nk, :qn], ident[:nk, :nk], Bt[:nk, kl[1], :qn], start=False, stop=True),
                                 reads=[Bid, BBt], writes=[BpS[sbk]])

                    for a in range(2):
                        P.op("dve", lambda E, a=a, qn=qn: E.memset(acc[a][:, :qn], 0.0), writes=[Bacc[a]])
                    qk(0)
                    if len(steps) > 1:
                        qk(1)
                    for st in range(len(steps)):
                        kb, a = steps[st]
                        k0, nk = kbs[kb]
                        if st + 2 < len(steps):
                            qk(st + 2)
                        sbk = st % 3
                        pb = nxt("pt", 4)
                        kl = klass(kb)
                        if kl[0] == "far":
                            P.op("act", lambda E, sbk=sbk, pb=pb, nk=nk, qn=qn, j=kl[1], cf=cf: E.activation(PT[pb][:nk, :qn], pS[sbk][:nk, :qn], AF.Exp, bias=cf[:nk, j:j + 1], scale=0.125),
                                 reads=[BpS[sbk], Bcf], writes=[BPT[pb]])
                        else:
                            P.op("act", lambda E, sbk=sbk, pb=pb, nk=nk, qn=qn: E.activation(PT[pb][:nk, :qn], pS[sbk][:nk, :qn], AF.Exp, scale=0.125),
                                 reads=[BpS[sbk]], writes=[BPT[pb]])
                        P.op("pe", lambda E, pb=pb, a=a, kb=kb, nk=nk, qn=qn, Vh=Vh: E.matmul(pO[a][:, :qn], Vh[:nk, kb, :], PT[pb][:nk, :qn], start=(kb == 0), stop=(kb == nkb - 1)),
                             reads=[BV, BPT[pb]], writes=[BpO[a]])
                        if kb % 2 == 0:
                            P.op("pe", lambda E, pb=pb, a=a, kb=kb, nk=nk, qn=qn: E.matmul(pL[a][:, :qn], ones[:nk, :], PT[pb][:nk, :qn], start=(kb == 0), stop=False),
                                 reads=[Bones, BPT[pb]], writes=[BpL[a]])
                        else:
                            P.op("dve", lambda E, pb=pb, a=a, nk=nk, qn=qn: E.tensor_tensor(acc[a][:nk, :qn], acc[a][:nk, :qn], PT[pb][:nk, :qn], ALU.add),
                                 reads=[BPT[pb], Bacc[a]], writes=[Bacc[a]])
                    for a in range(2):
                        P.op("dve", lambda E, a=a, qn=qn: E.tensor_copy(accb[a][:, :qn], acc[a][:, :qn]), reads=[Bacc[a]], writes=[Baccb[a]])
                        P.op("pe", lambda E, a=a, qn=qn: E.matmul(pL[a][:, :qn], ones[:, :], accb[a][:, :qn], start=False, stop=True),
                             reads=[Bones, Baccb[a]], writes=[BpL[a]])
                    for a in range(2):
                        pass
                    for a in range(2):
                        P.op("dve", lambda E, a=a, qn=qn: E.tensor_copy(R[a][:, :qn], pL[a][:, :qn]), reads=[BpL[a]], writes=[BR[a]])
                        P.op("dve", lambda E, a=a, qn=qn: E.tensor_copy(On[a][:, :qn], pO[a][:, :qn]), reads=[BpO[a]], writes=[BOn[a]])
                    for a in range(2):
                        P.op("dve", lambda E, a=a, qn=qn: E.reciprocal(R[a][:, :qn], R[a][:, :qn]), reads=[BR[a]], writes=[BR[a]])
                        P.op("dve", lambda E, a=a, qn=qn: E.tensor_tensor(On[a][:, :qn], On[a][:, :qn], R[a][:, :qn], ALU.mult), reads=[BOn[a], BR[a]], writes=[BOn[a]])
                    if debug and h == 3 and q0 == 0:
                        d0 = dsc(f"dbg_On0_{l}{si}", [128, 512], F32)
                        d1 = dsc(f"dbg_On1_{l}{si}", [128, 512], F32)
                        d2 = dsc(f"dbg_nl_{l}{si}", [128, 1], F32)
                        P.op("sp", lambda E, qn=qn: E.dma_start(out=d0[:, :qn], in_=On[0][:, :qn]), reads=[BOn[0]], key="dbg0")
                        P.op("sp", lambda E, qn=qn: E.dma_start(out=d1[:, :qn], in_=On[1][:, :qn]), reads=[BOn[1]], key="dbg1")
                        P.op("sp", lambda E: E.dma_start(out=d2[:, :], in_=neglam[:, :]), reads=[Bnl], key="dbg2")
                    P.op("dve", lambda E, qn=qn: E.tensor_scalar(On[1][:, :qn], On[1][:, :qn], neglam[:, 0:1], None, ALU.mult), reads=[BOn[1], Bnl], writes=[BOn[1]])
                    P.op("dve", lambda E, qn=qn: E.tensor_tensor(Dm[:, :qn], On[0][:, :qn], On[1][:, :qn], ALU.add), reads=[BOn[0], BOn[1]], writes=[BDm])
                    P.op("dve", lambda E, qn=qn: E.tensor_tensor(D2[:, :qn], Dm[:, :qn], Dm[:, :qn], ALU.mult), reads=[BDm], writes=[BD2])
                    P.op("pe", lambda E, qn=qn: E.matmul(pX[:, :qn], ones[:, :], D2[:, :qn], start=True, stop=True), reads=[Bones, BD2], writes=[BpX])
                    rstd_ops(P, lambda qn=qn: pX[:, :qn], lambda qn=qn: rs[:, :qn], 1.0 / 128, 1e-5, BpX, Brs)
                    P.op("dve", lambda E, qn=qn: E.tensor_tensor(Dm[:, :qn], Dm[:, :qn], rs[:, :qn], ALU.mult), reads=[BDm, Brs], writes=[BDm])
                    P.op("dve", lambda E, qn=qn: E.tensor_scalar(Dm[:, :qn], Dm[:, :qn], gd[:, 0:1], None, ALU.mult), reads=[BDm, Bgd], writes=[BDm])
                    if debug and h == 3 and q0 == 0:
                        d3 = dsc(f"dbg_lv_{l}{si}", [64, 4], F32)
                        d4 = dsc(f"dbg_ee_{l}{si}", [128, 2], F32)
                        d5 = dsc(f"dbg_rs_{l}{si}", [128, 512], F32)
                        d6 = dsc(f"dbg_Dm_{l}{si}", [128, 512], F32)
                        P.op("sp", lambda E: E.dma_start(out=d3[:, :], in_=lv[:, :]), reads=[Blv], key="dbg0")
                        P.op("sp", lambda E: E.dma_start(out=d4[:, :], in_=ee[:, :]), reads=[Bee], key="dbg1")
                        P.op("sp", lambda E, qn=qn: E.dma_start(out=d5[:, :qn], in_=rs[:, :qn]), reads=[Brs], key="dbg2")
                        P.op("sp", lambda E, qn=qn: E.dma_start(out=d6[:, :qn], in_=Dm[:, :qn]), reads=[BDm], key="dbg0")
                    yb = nxt("yb", 2)
                    P.op("dve", lambda E, yb=yb, gb=gb, qn=qn: E.tensor_tensor(Yb[yb][:, :qn], Dm[:, :qn], gt[gb][:, :qn], ALU.mult), reads=[BDm, Bgt[gb]], writes=[BYb[yb]])
                    P.op("sp", lambda E, yb=yb, h=h, q0=q0, qn=qn: E.dma_start(out=c.yT[12 + h, :, q0:q0 + qn], in_=Yb[yb][:, :qn]), reads=[BYb[yb]], key=f"y{yb}")
            P.emit(nc, f"df{l}{si}")

    def phase_dft(l, si):
        c = S[si]
        T = c.T
        sbs = blocks(T, 128)
        fscale = 1.0 / math.sqrt(T * 128.0)
        with ExitStack() as es:
            P = Prog()
            wff = sb(es, "wff", [128, 4, 128], F32)
            wfm = sb(es, "wfm", [128, 4, 128], BF16)
            ABt = [sb(es, f"ABt{i}", [128, 4, 256], BF16) for i in range(3)]
            Ct = [sb(es, f"Ct{i}", [128, 512], BF16) for i in range(3)]
            Nt = [sb(es, f"Nt{i}", [128, 512], BF16) for i in range(3)]
            fT = [sb(es, f"fT{i}", [128, 512], BF16) for i in range(2)]
            gt = [sb(es, f"gt{i}", [128, 512], BF16) for i in range(2)]
            Yb = [sb(es, f"Yb{i}", [128, 512], BF16) for i in range(2)]
            pF = [ps(es, f"pF{i}") for i in range(4)]
            pY = [ps(es, f"pY{i}") for i in range(2)]
            Bw = Buf("w")
            BAB = [Buf(f"AB{i}") for i in range(3)]
            BC = [Buf(f"C{i}") for i in range(3)]
            BfT = [Buf("fT0"), Buf("fT1")]
            Bgt = [Buf("gt0"), Buf("gt1")]
            BYb = [Buf("Yb0"), Buf("Yb1")]
            BpF = [PB(f"pF{i}") for i in range(4)]
            BpY = [PB("pY0"), PB("pY1")]
            P.op("sp", lambda E: E.dma_start(out=wff[:], in_=w_fmix[l].rearrange("g c d -> c g d")), writes=[Bw], key="w")
            P.op("dve", lambda E: E.tensor_copy(wfm[:], wff[:]), reads=[Bw], writes=[Bw])
            cnt = {}

            def nxt(k, n):
                v = cnt.get(k, 0)
                cnt[k] = v + 1
                return v % n

            for (k0, nk) in blocks(T, 512):
                for si_, (s0, ns) in enumerate(sbs):
                    b = nxt("in", 3)
                    P.op("sp", lambda E, b=b, s0=s0, ns=ns: E.dma_start(out=ABt[b][:ns], in_=c.AB[s0:s0 + ns]), writes=[BAB[b]], key=f"ab{b}")
                    P.op("sp", lambda E, b=b, s0=s0, ns=ns, k0=k0, nk=nk: E.dma_start(out=Ct[b][:ns, :nk], in_=c.CT[s0:s0 + ns, k0:k0 + nk]), writes=[BC[b]], key=f"ct{b}")
                    P.op("sp", lambda E, b=b, s0=s0, ns=ns, k0=k0, nk=nk: E.dma_start(out=Nt[b][:ns, :nk], in_=c.NST[s0:s0 + ns, k0:k0 + nk]), writes=[BC[b]], key=f"nt{b}")
                    for g in range(4):
                        P.op("pe", lambda E, b=b, g=g, ns=ns, nk=nk, si_=si_: E.matmul(pF[g][:, :nk], ABt[b][:ns, g, 0:128], Ct[b][:ns, :nk], start=(si_ == 0), stop=False),
                             reads=[BAB[b], BC[b]], writes=[BpF[g]])
                        P.op("pe", lambda E, b=b, g=g, ns=ns, nk=nk, si_=si_: E.matmul(pF[g][:, :nk], ABt[b][:ns, g, 128:256], Nt[b][:ns, :nk], start=False, stop=(si_ == len(sbs) - 1)),
                             reads=[BAB[b], BC[b]], writes=[BpF[g]])
                for g in range(4):
                    fb = nxt("f", 2)
                    P.op("act", lambda E, fb=fb, g=g, nk=nk: E.activation(fT[fb][:, :nk], pF[g][:, :nk], AF.Copy, scale=fscale), reads=[BpF[g]], writes=[BfT[fb]])
                    yb = nxt("y", 2)
                    P.op("pe", lambda E, fb=fb, yb=yb, g=g, nk=nk: E.matmul(pY[yb][:, :nk], wfm[:, g, :], fT[fb][:, :nk], start=True, stop=True),
                         reads=[Bw, BfT[fb]], writes=[BpY[yb]])
                    gb = nxt("g", 2)
                    P.op("sp", lambda E, gb=gb, g=g, k0=k0, nk=nk: E.dma_start(out=gt[gb][:, :nk], in_=c.gT[g, :, k0:k0 + nk]), writes=[Bgt[gb]], key=f"g{gb}")
                    P.op("dve", lambda E, yb=yb, gb=gb, nk=nk: E.tensor_tensor(Yb[yb][:, :nk], pY[yb][:, :nk], gt[gb][:, :nk], ALU.mult), reads=[BpY[yb], Bgt[gb]], writes=[BYb[yb]])
                    P.op("sp", lambda E, yb=yb, g=g, k0=k0, nk=nk: E.dma_start(out=c.yT[g, :, k0:k0 + nk], in_=Yb[yb][:, :nk]), reads=[BYb[yb]], key=f"y{yb}")
            P.emit(nc, f"dft{l}{si}")

    def phase_out(l, si, xsrc, xdst, final):
        c = S[si]
        T = c.T
        wo_l = w_o[l].rearrange("(c p) n -> p c n", p=128)
        with ExitStack() as es:
            P = Prog()
            wf = sb(es, "wf", [128, 16, 256], F32)
            wo = sb(es, "wo", [128, 16, 2048], BF16)
            yt = sb(es, "yt", [128, 16, 512], BF16)
            xt = [sb(es, f"xt{i}", [128, D], F32) for i in range(2)]
            xo = [sb(es, f"xo{i}", [128, D], F32) for i in range(2)]
            junk = sb(es, "junk", [128, D], BF16)
            gt = sb(es, "gt", [128, D], F32)
            grow = sb(es, "grow", [1, D], F32)
            one1 = sb(es, "one1", [1, 128], F32)
            ss = [sb(es, f"ss{i}", [128, 2], F32) for i in range(2)]
            po = [ps(es, f"po{i}") for i in range(4)]
            Bwf, Bwo, Byt = Buf("wf"), Buf("wo"), Buf("yt")
            Bxt = [Buf("xt0"), Buf("xt1")]
            Bxo = [Buf("xo0"), Buf("xo1")]
            Bjunk, Bgt, Bgrow, Bone1 = Buf("junk"), Buf("gt"), Buf("grow"), Buf("one1")
            Bss = [Buf("ss0"), Buf("ss1")]
            Bpo = [PB(f"po{i}") for i in range(4)]
            for i in range(8):
                for hf in range(2):
                    P.op("sp", lambda E, i=i, hf=hf: E.dma_start(out=wf[:, hf * 8:hf * 8 + 8, :], in_=wo_l[:, hf * 8:hf * 8 + 8, i * 256:(i + 1) * 256]), writes=[Bwf], key="wf")
                P.op("dve", lambda E, i=i: E.tensor_copy(wo[:, :, i * 256:(i + 1) * 256], wf[:]), reads=[Bwf], writes=[Bwo])
            if final:
                P.op("sp", lambda E: E.dma_start(out=grow[:], in_=final_norm.rearrange("(o n) -> o n", o=1)), writes=[Bgrow], key="c4")
                P.op("dve", lambda E: E.memset(one1[:], 1.0), writes=[Bone1])
                for i in range(4):
                    P.op("pe", lambda E, i=i: E.matmul(po[i][:, :], one1[:, :], grow[:, i * 512:(i + 1) * 512], start=True, stop=True),
                         reads=[Bone1, Bgrow], writes=[Bpo[i]])
                    P.op("dve", lambda E, i=i: E.tensor_copy(gt[:, i * 512:(i + 1) * 512], po[i][:, :]), reads=[Bpo[i]], writes=[Bgt])
            cnt = {}

            def nxt(k, n):
                v = cnt.get(k, 0)
                cnt[k] = v + 1
                return v % n

            for (q0, qn) in blocks(T, 512):
                for hf in range(2):
                    P.op("sp", lambda E, q0=q0, qn=qn, hf=hf: E.dma_start(out=yt[:, hf * 8:hf * 8 + 8, :qn], in_=c.yT[hf * 8:hf * 8 + 8, :, q0:q0 + qn].rearrange("c p t -> p c t")), writes=[Byt], key="yt")
                for (t0, tn) in blocks(qn, 128):
                    a = q0 + t0
                    b = nxt("x", 2)
                    P.op("sp", lambda E, b=b, a=a, tn=tn: E.dma_start(out=xt[b][:tn, :], in_=xsrc[a:a + tn, :]), writes=[Bxt[b]], key=f"x{b}")
                    for dc in range(4):
                        for cc in range(16):
                            P.op("pe", lambda E, dc=dc, cc=cc, t0=t0, tn=tn: E.matmul(po[dc][:tn, :], yt[:, cc, t0:t0 + tn], wo[:, cc, dc * 512:(dc + 1) * 512], start=(cc == 0), stop=(cc == 15)),
                                 reads=[Byt, Bwo], writes=[Bpo[dc]])
                        P.op("dve", lambda E, dc=dc, b=b, tn=tn: E.tensor_tensor(xo[b][:tn, dc * 512:(dc + 1) * 512], po[dc][:tn, :], xt[b][:tn, dc * 512:(dc + 1) * 512], ALU.add),
                             reads=[Bpo[dc], Bxt[b]], writes=[Bxo[b]])
                    if not final:
                        P.op("sp", lambda E, b=b, a=a, tn=tn: E.dma_start(out=xdst[a:a + tn, :], in_=xo[b][:tn, :]), reads=[Bxo[b]], key=f"o{b}")
                    else:
                        P.op("act", lambda E, b=b, tn=tn: E.activation(junk[:tn, :], xo[b][:tn, :], AF.Square, accum_out=ss[b][:tn, 0:1]),
                             reads=[Bxo[b]], writes=[Bjunk, Bss[b]])
                        rstd_ops(P, lambda b=b, tn=tn: ss[b][:tn, 0:1], lambda b=b, tn=tn: ss[b][:tn, 1:2], 1.0 / D, EPS, Bss[b], Bss[b])
                        P.op("dve", lambda E, b=b, tn=tn: E.tensor_scalar(xo[b][:tn, :], xo[b][:tn, :], ss[b][:tn, 1:2], None, ALU.mult), reads=[Bxo[b], Bss[b]], writes=[Bxo[b]])
                        P.op("dve", lambda E, b=b, tn=tn: E.tensor_tensor(xo[b][:tn, :], xo[b][:tn, :], gt[:tn, :], ALU.mult), reads=[Bxo[b], Bgt], writes=[Bxo[b]])
                        lo = max(a, N_META)
                        if lo < a + tn:
                            P.op("sp", lambda E, b=b, a=a, tn=tn, lo=lo: E.dma_start(out=xdst[lo - N_META:a + tn - N_META, :], in_=xo[b][lo - a:tn, :]), reads=[Bxo[b]], key=f"o{b}")
            P.emit(nc, f"out{l}{si}")

    def full():
        for si in range(nseg):
            phase_tables(si)
        phase_bias()
        for l in range(NL):
            for si in range(nseg):
                xs = x_in[si] if l == 0 else S[si].x1
                phase_inproj(l, si, xs)
                phase_q(l, si)
                phase_kv(l, si)
                phase_dft(l, si)
                phase_mla(l, si)
                phase_diff(l, si)
                if l == NL - 1:
                    phase_out(l, si, xs, y_out[si], True)
                else:
                    phase_out(l, si, xs, S[si].x1, False)

    G.phase_mla, G.phase_diff, G.phase_dft, G.phase_out, G.full = phase_mla, phase_diff, phase_dft, phase_out, full
    G.phase_tables, G.phase_bias, G.phase_inproj = phase_tables, phase_bias, phase_inproj
    G.nc, G.S, G.x_in, G.y_out, G.Gb = nc, S, x_in, y_out, Gb
    G.din = dict(w_uq=w_uq, w_ukv=w_ukv, q_norm=q_norm, kv_norm=kv_norm, cos2=cos2, sin2=sin2, w_fmix=w_fmix, w_o=w_o,
                 lam=lam_in, diff_norm=diff_norm, final_norm=final_norm, rel_bias=rel_bias, ident=ident_d)
    G.sb, G.ps = sb, ps
    return G


SEG_T = [4096 + N_META, 8192 + N_META]


def kernel(x_prompt, x_sample, meta_tokens, rel_bias, final_norm, norm_w, w_in, w_fmix, q_norm, w_uq,
           kv_norm, w_ukv, lam_q1, lam_k1, lam_q2, lam_k2, diff_norm, w_o):
    f32 = lambda a: np.ascontiguousarray(np.asarray(a, dtype=np.float32))
    x_prompt, x_sample, meta = f32(x_prompt), f32(x_sample), f32(meta_tokens)
    G = build(SEG_T, debug=False)
    G.full()
    ident, cs, oh = misc_consts()
    shared = dict(ident=ident, cs=cs, oh=oh, rel_bias=f32(rel_bias), final_norm=f32(final_norm), norm_w=f32(norm_w),
                  w_in=f32(w_in), w_fmix=f32(w_fmix), q_norm=f32(q_norm), w_uq=f32(w_uq), kv_norm=f32(kv_norm),
                  w_ukv=f32(w_ukv), lam_q1=f32(lam_q1), lam_k1=f32(lam_k1), lam_q2=f32(lam_q2), lam_k2=f32(lam_k2),
                  diff_norm=f32(diff_norm), w_o=f32(w_o))
    for s, T in enumerate(SEG_T):
        shared[f"cos2_{s}"], shared[f"sin2_{s}"] = host_consts(T)
        shared[f"tj_{s}"], shared[f"tk_{s}"] = dft_consts(T)
    xp = [np.concatenate([meta, x_prompt[g]], 0) for g in range(x_prompt.shape[0])]
    in_maps = []
    for c in range(8):
        m = dict(shared)
        m["x0"] = np.concatenate([meta, x_sample[c]], 0)
        m["x1"] = xp[c // 4]
        in_maps.append(m)
    res = run_bass_kernel_spmd(G.nc, in_maps, core_ids=list(range(8)))
    y_sample = np.stack([np.asarray(res.results[c]["y0"], dtype=np.float32) for c in range(8)], 0)
    y_prompt = np.stack([np.asarray(res.results[4 * g]["y1"], dtype=np.float32) for g in range(2)], 0)
    return (y_prompt, y_sample)
```

```python
import math
from contextlib import ExitStack
import numpy as np
import concourse.bass as bass
import concourse.mybir as mybir
from concourse.bass_utils import run_bass_kernel_spmd

F32 = mybir.dt.float32
BF16 = mybir.dt.bfloat16
I32 = mybir.dt.int32
AF = mybir.ActivationFunctionType
ALU = mybir.AluOpType

D = 2048
N_META = 16
NL = 2
INW = 5440
C_UF, C_CQ, C_CKV, C_KR, C_QD, C_KD, C_VD, C_G = 0, 512, 1280, 1792, 1856, 2368, 2880, 3392
EPS = 1e-6
RMAX = 768
WFULL = 1600
PI_S = 3.14159


class Buf:
    def __init__(self, name, excl=False):
        self.name = name
        self.excl = excl
        self.writers = {}
        self.readers = {}


def PB(name):
    return Buf(name, True)


class Op:
    __slots__ = ("eng", "fn", "deps", "sig", "seq", "key", "cum")


SEM_SKIP = [0]
EMIT_N = [0]
DBG = {}
MAXQ = [4]
ENGS = ["pe", "act", "dve", "pool", "sp"]


class Prog:
    def __init__(self):
        self.ops = {e: [] for e in ENGS}
        self.keycum = {}

    def op(self, eng, fn, reads=(), writes=(), key=None):
        o = Op()
        o.eng, o.fn, o.deps, o.sig, o.key, o.seq, o.cum = eng, fn, [], False, key, 0, 0
        if key is not None:
            self.keycum[key] = self.keycum.get(key, 0) + 16
            o.cum = self.keycum[key]
        ex = [b for b in reads if b.excl]
        if ex:
            reads = [b for b in reads if not b.excl]
            writes = list(writes) + [b for b in ex if b not in writes]
        for b in reads:
            for w in b.writers.values():
                self._dep(o, w)
        for b in writes:
            for r in b.readers.values():
                self._dep(o, r)
            for w in b.writers.values():
                self._dep(o, w)
        for b in reads:
            b.readers[(eng, key)] = o
        for b in writes:
            b.writers[(eng, key)] = o
            b.readers = {}
        self.ops[eng].append(o)
        return o

    def _dep(self, o, p):
        if p is o:
            return
        if p.key is None and o.key is None and p.eng == o.eng and p.eng == "pe":
            return
        if p.key is None and p.eng == o.eng:
            pass
        o.deps.append(p)
        p.sig = True

    def emit(self, nc, name):
        EMIT_N[0] += 1
        name = f"{name}_{EMIT_N[0]}"
        with ExitStack() as es:
            engsem = {e: nc.alloc_semaphore(name=f"{name}_s_{e}") for e in ENGS}
            keysem = {k: nc.alloc_semaphore(name=f"{name}_k{i}") for i, k in enumerate(self.keycum)}
            allsems = list(engsem.values()) + list(keysem.values())
            for e in ENGS:
                c = 0
                for o in self.ops[e]:
                    if o.key is None and o.sig:
                        c += 1
                        o.seq = c
            block = es.enter_context(nc.Block())

            def run(E, e):
                waited = {}
                inflight = []
                for o in self.ops[e]:
                    for p in o.deps:
                        if p.key is not None:
                            sid, sem, val = ("k", p.key), keysem[p.key], p.cum
                        else:
                            sid, sem, val = ("e", p.eng), engsem[p.eng], p.seq
                        if waited.get(sid, 0) < val:
                            E.wait_ge(sem, val)
                            waited[sid] = val
                    if o.key is not None and len(inflight) >= MAXQ[0]:
                        k0_, c0_ = inflight.pop(0)
                        if waited.get(("k", k0_), 0) < c0_:
                            E.wait_ge(keysem[k0_], c0_)
                            waited[("k", k0_)] = c0_
                    inst = o.fn(E)
                    if o.key is not None:
                        inflight.append((o.key, o.cum))
                        inst.then_inc(keysem[o.key], 16)
                    elif o.sig:
                        inst.then_inc(engsem[e], 1)
                if e == "sp":
                    for k, tot in self.keycum.items():
                        if waited.get(("k", k), 0) < tot:
                            E.wait_ge(keysem[k], tot)

            @block.tensor
            def _(E):
                run(E, "pe")

            @block.scalar
            def _(E):
                run(E, "act")

            @block.vector
            def _(E):
                run(E, "dve")

            @block.gpsimd
            def _(E):
                run(E, "pool")

            @block.sync
            def _(E):
                run(E, "sp")
        nc.clear_and_free_semaphores(allsems)
        nc.all_engine_barrier()


def blocks(n, size):
    return [(s, min(size, n - s)) for s in range(0, n, size)]


class Ctx:
    pass


def t5_bucket_np(rel):
    nb = 16
    max_exact = 8
    ret = (rel > 0).astype(np.int64) * nb
    n = np.abs(rel)
    nf = np.maximum(n, 1).astype(np.float32)
    large = max_exact + (np.log(nf / np.float32(max_exact)) / np.float32(math.log(128 / max_exact))
                         * np.float32(nb - max_exact)).astype(np.int32)
    large = np.minimum(large, nb - 1)
    return ret + np.where(n < max_exact, n, large)


def host_consts(T):
    pos = np.arange(T, dtype=np.float32)
    inv = (10000.0 ** (-np.arange(0, 64, 2, dtype=np.float32) / 64)).astype(np.float32)
    ang = pos[:, None] * inv[None, :]
    cos, sin = np.cos(ang).astype(np.float32).T, np.sin(ang).astype(np.float32).T
    cos2 = np.concatenate([cos, cos], 0)
    sin2 = np.concatenate([-sin, sin], 0)
    return np.ascontiguousarray(cos2), np.ascontiguousarray(sin2)


def dft_consts(T):
    s = np.arange(T, dtype=np.int64)
    j = np.arange(512, dtype=np.int64)
    a = 2 * np.pi * ((s[:, None] * j[None, :]) % T).astype(np.float64) / T
    tj = np.stack([np.cos(a), np.sin(a)], 1).astype(np.float32)
    k0 = np.arange(0, T, 512, dtype=np.int64)
    a = 2 * np.pi * ((s[:, None] * k0[None, :]) % T).astype(np.float64) / T
    tk = np.stack([np.cos(a), np.sin(a)], 2).astype(np.float32)
    return np.ascontiguousarray(tj), np.ascontiguousarray(tk)


def misc_consts():
    ident = np.eye(128, dtype=np.float32)
    c = np.arange(128)
    ang = 2 * np.pi * np.outer(c, c) / 128.0
    cs = np.concatenate([np.cos(ang), np.sin(ang)], 1).astype(np.float32)
    n = np.arange(WFULL)
    rel = RMAX - n
    bk = t5_bucket_np(rel)
    oh = np.zeros((32, WFULL), np.float32)
    oh[bk, n] = 8.0
    return ident, cs, oh


def build(segT, debug=False):
    nc = bass.Bass("TRN2", target_bir_lowering=False)
    G = Ctx()
    nseg = len(segT)

    def din(name, shape, dt=F32):
        return nc.dram_tensor(name, list(shape), dt, kind="ExternalInput").ap()

    def dsc(name, shape, dt=BF16):
        t = nc.dram_tensor(name, list(shape), dt, kind=("ExternalOutput" if debug else "Internal")).ap()
        dbg[name] = t
        return t

    x_in = [din(f"x{s}", [segT[s], D]) for s in range(nseg)]
    y_out = [nc.dram_tensor(f"y{s}", [segT[s] - N_META, D], F32, kind="ExternalOutput").ap() for s in range(nseg)]
    cos2 = [din(f"cos2_{s}", [64, segT[s]]) for s in range(nseg)]
    sin2 = [din(f"sin2_{s}", [64, segT[s]]) for s in range(nseg)]
    tjd = [din(f"tj_{s}", [segT[s], 2, 512]) for s in range(nseg)]
    tkd = [din(f"tk_{s}", [segT[s], len(blocks(segT[s], 512)), 2]) for s in range(nseg)]
    ident_d = din("ident", [128, 128])
    cs_d = din("cs", [128, 256])
    oh_d = din("oh", [32, WFULL])
    rel_bias = din("rel_bias", [32, 4])
    final_norm = din("final_norm", [D])
    norm_w = din("norm_w", [NL, D])
    w_in = din("w_in", [NL, D, INW])
    w_fmix = din("w_fmix", [NL, 4, 128, 128])
    q_norm = din("q_norm", [NL, 768])
    w_uq = din("w_uq", [NL, 768, 1536])
    kv_norm = din("kv_norm", [NL, 512])
    w_ukv = din("w_ukv", [NL, 512, 2048])
    lam_in = [din(n, [NL, 64]) for n in ("lam_q1", "lam_k1", "lam_q2", "lam_k2")]
    diff_norm = din("diff_norm", [NL, 128])
    w_o = din("w_o", [NL, D, D])

    dbg = {}
    S = []
    for s in range(nseg):
        T = segT[s]
        c = Ctx()
        c.T = T
        c.x1 = dsc(f"x1_{s}", [T, D], F32)
        c.AB = dsc(f"AB_{s}", [T, 4, 256])
        c.cqg = dsc(f"cqg_{s}", [6, 128, T])
        c.cqs = dsc(f"cqs_{s}", [6, 128, T])
        c.ckg = dsc(f"ckg_{s}", [4, 128, T])
        c.cks = dsc(f"cks_{s}", [4, 128, T])
        c.KrT = dsc(f"KrT_{s}", [64, T])
        c.qdT = dsc(f"qdT_{s}", [8, 64, T])
        c.kdT = dsc(f"kdT_{s}", [8, 64, T])
        c.Vd = dsc(f"Vd_{s}", [T, 512])
        c.gT = dsc(f"gT_{s}", [16, 128, T])
        c.QnT = dsc(f"QnT_{s}", [8, 128, T])
        c.QrT = dsc(f"QrT_{s}", [8, 64, T])
        c.KnT = dsc(f"KnT_{s}", [8, 128, T])
        c.V = dsc(f"V_{s}", [T, 1024])
        c.yT = dsc(f"yT_{s}", [16, 128, T])
        c.CT = dsc(f"CT_{s}", [T, T])
        c.NST = dsc(f"NST_{s}", [T, T])
        S.append(c)
    Gb = dsc("Gb", [4, 132, WFULL])

    uid = [0]

    def sb(es, name, shape, dt):
        uid[0] += 1
        return es.enter_context(nc.sbuf_tensor(f"sb{uid[0]}_{name}", list(shape), dt))

    def ps(es, name, shape=(128, 512), dt=F32):
        uid[0] += 1
        return es.enter_context(nc.psum_tensor(f"ps{uid[0]}_{name}", list(shape), dt))

    nc_allow = nc.allow_non_contiguous_dma("small strided const loads")
    nc_allow.__enter__()
    nc_lp = nc.allow_low_precision("bf16 matmul operands by design")
    nc_lp.__enter__()

    def phase_tables(si, variant=""):
        c = S[si]
        T = c.T
        nkb = len(blocks(T, 512))
        with ExitStack() as es:
            P = Prog()
            tj = [sb(es, f"tj{i}", [128, 2, 512], F32) for i in range(2)]
            tk = [sb(es, f"tk{i}", [128, nkb, 2], F32) for i in range(2)]
            ntk = [sb(es, f"ntk{i}", [128, nkb, 2], F32) for i in range(2)]
            tmp = [sb(es, f"tmp{i}", [128, 2, 512], F32) for i in range(2)]
            tm2 = [sb(es, f"tm2{i}", [128, 2, 512], F32) for i in range(2)]
            ot = [sb(es, f"ot{i}", [128, 2, 512], BF16) for i in range(2)]
            Btj = [Buf("tj0"), Buf("tj1")]
            Btk = [Buf("tk0"), Buf("tk1")]
            Btmp = [Buf("tmp0"), Buf("tmp1")]
            Bot = [Buf("ot0"), Buf("ot1")]
            it = 0
            for si_, (s0, ns) in enumerate(blocks(T, 128)):
                a_ = si_ % 2
                P.op("sp", lambda E, a_=a_, s0=s0, ns=ns: E.dma_start(out=tj[a_][:ns], in_=tjd[si][s0:s0 + ns]), writes=[Btj[a_]], key=f"tj{a_}")
                if "B" in variant:
                    P.op("dve", lambda E, a_=a_, ns=ns: E.memset(tk[a_][:ns], 0.5), writes=[Btk[a_]])
                else:
                    P.op("sp", lambda E, a_=a_, s0=s0, ns=ns: E.dma_start(out=tk[a_][:ns], in_=tkd[si][s0:s0 + ns]), writes=[Btk[a_]], key=f"tk{a_}")
                P.op("dve", lambda E, a_=a_, ns=ns: E.tensor_scalar(ntk[a_][:ns], tk[a_][:ns], -1.0, None, ALU.mult), reads=[Btk[a_]], writes=[Btk[a_]])
                for kb, (k0, nk) in enumerate(blocks(T, 512)):
                    b = it % 2
                    it += 1
                    ck, sk = tk[a_][:ns, kb, 0:1], tk[a_][:ns, kb, 1:2]
                    nck, nsk = ntk[a_][:ns, kb, 0:1], ntk[a_][:ns, kb, 1:2]
                    P.op("dve", lambda E, a_=a_, b=b, ns=ns, nk=nk, ck=ck: E.tensor_scalar(tmp[b][:ns, 0, :nk], tj[a_][:ns, 0, :nk], ck, None, ALU.mult),
                         reads=[Btj[a_], Btk[a_]], writes=[Btmp[b]])
                    P.op("dve", lambda E, a_=a_, b=b, ns=ns, nk=nk, nsk=nsk: E.tensor_scalar(tm2[b][:ns, 0, :nk], tj[a_][:ns, 1, :nk], nsk, None, ALU.mult),
                         reads=[Btj[a_], Btk[a_]], writes=[Btmp[b]])
                    P.op("dve", lambda E, a_=a_, b=b, ns=ns, nk=nk: E.tensor_tensor(ot[b][:ns, 0, :nk], tm2[b][:ns, 0, :nk], tmp[b][:ns, 0, :nk], ALU.add),
                         reads=[Btmp[b]], writes=[Bot[b]])
                    P.op("dve", lambda E, a_=a_, b=b, ns=ns, nk=nk, nck=nck: E.tensor_scalar(tmp[b][:ns, 1, :nk], tj[a_][:ns, 1, :nk], nck, None, ALU.mult),
                         reads=[Btj[a_], Btk[a_]], writes=[Btmp[b]])
                    P.op("dve", lambda E, a_=a_, b=b, ns=ns, nk=nk, nsk=nsk: E.tensor_scalar(tm2[b][:ns, 1, :nk], tj[a_][:ns, 0, :nk], nsk, None, ALU.mult),
                         reads=[Btj[a_], Btk[a_]], writes=[Btmp[b]])
                    P.op("dve", lambda E, a_=a_, b=b, ns=ns, nk=nk: E.tensor_tensor(ot[b][:ns, 1, :nk], tm2[b][:ns, 1, :nk], tmp[b][:ns, 1, :nk], ALU.add),
                         reads=[Btmp[b]], writes=[Bot[b]])
                    P.op("sp", lambda E, b=b, s0=s0, ns=ns, k0=k0, nk=nk: E.dma_start(out=c.CT[s0:s0 + ns, k0:k0 + nk], in_=ot[b][:ns, 0, :nk]),
                         reads=[Bot[b]], key=f"c{b}")
                    if "A" not in variant:
                        P.op("sp", lambda E, b=b, s0=s0, ns=ns, k0=k0, nk=nk: E.dma_start(out=c.NST[s0:s0 + ns, k0:k0 + nk], in_=ot[b][:ns, 1, :nk]),
                             reads=[Bot[b]], key=f"n{b}")
            P.emit(nc, f"tb{si}")

    def phase_bias():
        with ExitStack() as es:
            P = Prog()
            oh = sb(es, "oh", [32, WFULL], F32)
            rb = sb(es, "rb", [32, 4], F32)
            ohh = sb(es, "ohh", [32, WFULL], F32)
            one = sb(es, "one32", [32, 128], F32)
            gsb = sb(es, "gsb", [128, WFULL], BF16)
            pss = [ps(es, f"pb{i}") for i in range(4)]
            Boh, Brb, Bohh, Bone, Bg = Buf("oh"), Buf("rb"), Buf("ohh"), Buf("one"), Buf("g")
            Bps = [PB(f"pb{i}") for i in range(4)]
            P.op("sp", lambda E: E.dma_start(out=oh[:], in_=oh_d[:, :]), writes=[Boh], key="l0")
            P.op("sp", lambda E: E.dma_start(out=rb[:], in_=rel_bias[:, :]), writes=[Brb], key="l1")
            P.op("dve", lambda E: E.memset(one[:], 1.0), writes=[Bone])
            for h in range(4):
                P.op("dve", lambda E, h=h: E.tensor_scalar(ohh[:], oh[:], rb[:, h:h + 1], None, ALU.mult), reads=[Boh, Brb], writes=[Bohh])
                for i, (n0, nn) in enumerate(blocks(WFULL, 512)):
                    P.op("pe", lambda E, i=i, n0=n0, nn=nn: E.matmul(pss[i][:, :nn], one[:, :], ohh[:, n0:n0 + nn], start=True, stop=True),
                         reads=[Bohh, Bone], writes=[Bps[i]])
                    P.op("act", lambda E, i=i, n0=n0, nn=nn: E.activation(gsb[:, n0:n0 + nn], pss[i][:, :nn], AF.Copy), reads=[Bps[i]], writes=[Bg])
                P.op("sp", lambda E, h=h: E.dma_start(out=Gb[h, 0:128, :], in_=gsb[:]), reads=[Bg], key="st")
            P.emit(nc, "bias")

    SG = 2064

    def phase_inproj(l, si, xsrc, variant=""):
        c = S[si]
        T = c.T
        w_l = w_in[l].rearrange("(c p) n -> p c n", p=128)
        chunks = [("uf", C_UF, 512), ("cq", C_CQ, 512), ("cq", C_CQ + 512, 256), ("ckv", C_CKV, 512), ("kr", C_KR, 64),
                  ("qd", C_QD, 512), ("kd", C_KD, 512), ("vd", C_VD, 512)] + [("gate", C_G + 512 * i, 512) for i in range(4)]
        with ExitStack() as es:
            P = Prog()
            hT = sb(es, "hT", [128, 16, SG], BF16)
            wb = [sb(es, f"wb{i}", [128, 16, 512], BF16) for i in range(2)]
            wsw = sb(es, "wsw", [128, 16, 64], BF16)
            wf = sb(es, "wf", [128, 16, 512], F32)
            Bwf = Buf("wf")
            xt = [sb(es, f"xt{i}", [128, D], F32) for i in range(2)]
            hb = [sb(es, f"hb{i}", [128, D], BF16) for i in range(2)]
            junk = sb(es, "junk", [128, D], BF16)
            gt = sb(es, "gt", [128, D], F32)
            grow = sb(es, "grow", [1, D], F32)
            one1 = sb(es, "one1", [1, 128], F32)
            ss = [sb(es, f"ss{i}", [128, 2], F32) for i in range(2)]
            identf = sb(es, "identf", [128, 128], F32)
            ident = sb(es, "ident", [128, 128], BF16)
            csf = sb(es, "csf", [128, 256], F32)
            csb = sb(es, "csb", [128, 256], BF16)
            gq = sb(es, "gq", [128, 6], F32)
            gk = sb(es, "gk", [128, 4], F32)
            cst = [sb(es, f"cst{i}", [64, 512], F32) for i in range(2)]
            snt = [sb(es, f"snt{i}", [64, 512], F32) for i in range(2)]
            uT = [sb(es, f"uT{i}", [128, 512], BF16) for i in range(2)]
            stA = [sb(es, f"stA{i}", [128, 512], BF16) for i in range(3)]
            stB = [sb(es, f"stB{i}", [128, 512], BF16) for i in range(3)]
            r1 = [sb(es, f"r1_{i}", [64, 512], F32) for i in range(2)]
            r2 = [sb(es, f"r2_{i}", [64, 512], F32) for i in range(2)]
            pT = [ps(es, f"pT{i}", (128, 1024), BF16) for i in range(2)]
            pm = [ps(es, f"pm{i}") for i in range(4)]
            p2 = [ps(es, f"p2{i}") for i in range(2)]
            BhT = Buf("hT")
            Bwb = [Buf("wb0"), Buf("wb1")]
            Bwsw = Buf("wsw")
            Bxt = [Buf("xt0"), Buf("xt1")]
            Bhb = [Buf("hb0"), Buf("hb1")]
            Bjunk, Bgt, Bgrow, Bone1 = Buf("junk"), Buf("gt"), Buf("grow"), Buf("one1")
            Bss = [Buf("ss0"), Buf("ss1")]
            Bid, Bcs, Bgq = Buf("id"), Buf("cs"), Buf("gq")
            Bcst = [Buf("cst0"), Buf("cst1")]
            BuT = [Buf("uT0"), Buf("uT1")]
            BstA = [Buf(f"stA{i}") for i in range(3)]
            BstB = [Buf(f"stB{i}") for i in range(3)]
            Br = [Buf("r0"), Buf("r1")]
            BpT = [PB("pT0"), PB("pT1")]
            Bpm = [PB(f"pm{i}") for i in range(4)]
            Bp2 = [PB("p20"), PB("p21")]
            P.op("sp", lambda E: E.dma_start(out=identf[:], in_=ident_d[:, :]), writes=[Bid], key="c0")
            P.op("dve", lambda E: E.tensor_copy(ident[:], identf[:]), reads=[Bid], writes=[Bid])
            P.op("sp", lambda E: E.dma_start(out=csf[:], in_=cs_d[:, :]), writes=[Bcs], key="c1")
            P.op("dve", lambda E: E.tensor_copy(csb[:], csf[:]), reads=[Bcs], writes=[Bcs])
            P.op("sp", lambda E: E.dma_start(out=gq[:], in_=q_norm[l].rearrange("(c p) -> p c", p=128), allow_slow_non_contiguous=True), writes=[Bgq], key="c2")
            P.op("sp", lambda E: E.dma_start(out=gk[:], in_=kv_norm[l].rearrange("(c p) -> p c", p=128), allow_slow_non_contiguous=True), writes=[Bgq], key="c3")
            P.op("sp", lambda E: E.dma_start(out=grow[:], in_=norm_w[l:l + 1, :]), writes=[Bgrow], key="c4")
            P.op("dve", lambda E: E.memset(one1[:], 1.0), writes=[Bone1])
            for i in range(4):
                P.op("pe", lambda E, i=i: E.matmul(pm[i][:, :], one1[:, :], grow[:, i * 512:(i + 1) * 512], start=True, stop=True),
                     reads=[Bone1, Bgrow], writes=[Bpm[i]])
                P.op("dve", lambda E, i=i: E.tensor_copy(gt[:, i * 512:(i + 1) * 512], pm[i][:, :]), reads=[Bpm[i]], writes=[Bgt])
            cnt = {"x": 0, "w": 0, "pm": 0, "p2": 0, "sa": 0, "sb": 0, "u": 0, "r": 0, "cs": 0, "pT": 0}

            def nxt(k, n):
                v = cnt[k] % n
                cnt[k] += 1
                return v

            for (g0, gn) in blocks(T, SG):
                for (t0, tn) in blocks(gn, 128):
                    b = nxt("x", 2)
                    P.op("sp", lambda E, b=b, t0=t0, tn=tn, g0=g0: E.dma_start(out=xt[b][:tn, :], in_=xsrc[g0 + t0:g0 + t0 + tn, :]),
                         writes=[Bxt[b]], key=f"x{b}")
                    P.op("act", lambda E, b=b, tn=tn: E.activation(junk[:tn, :], xt[b][:tn, :], AF.Square, accum_out=ss[b][:tn, 0:1]),
                         reads=[Bxt[b]], writes=[Bjunk, Bss[b]])
                    P.op("dve", lambda E, b=b, tn=tn: E.tensor_scalar(ss[b][:tn, 1:2], ss[b][:tn, 0:1], 1.0 / D, EPS, ALU.mult, ALU.add),
                         reads=[Bss[b]], writes=[Bss[b]])
                    P.op("act", lambda E, b=b, tn=tn: E.activation(ss[b][:tn, 1:2], ss[b][:tn, 1:2], AF.Sqrt), reads=[Bss[b]], writes=[Bss[b]])
                    P.op("dve", lambda E, b=b, tn=tn: E.reciprocal(ss[b][:tn, 1:2], ss[b][:tn, 1:2]), reads=[Bss[b]], writes=[Bss[b]])
                    P.op("dve", lambda E, b=b, tn=tn: E.tensor_scalar(xt[b][:tn, :], xt[b][:tn, :], ss[b][:tn, 1:2], None, ALU.mult),
                         reads=[Bxt[b], Bss[b]], writes=[Bxt[b]])
                    P.op("dve", lambda E, b=b, tn=tn: E.tensor_tensor(hb[b][:tn, :], xt[b][:tn, :], gt[:tn, :], ALU.mult),
                         reads=[Bxt[b], Bgt], writes=[Bhb[b]])
                    for half in range(2):
                        pb = nxt("pT", 2)
                        for cc in range(8):
                            ch = half * 8 + cc
                            P.op("pe", lambda E, b=b, pb=pb, cc=cc, ch=ch, tn=tn: E.transpose(pT[pb][:, cc * 128:cc * 128 + tn], hb[b][:tn, ch * 128:(ch + 1) * 128], ident[:tn, :tn]),
                                 reads=[Bhb[b], Bid], writes=[BpT[pb]])
                        src = lambda pb=pb, tn=tn: pT[pb][:, :].rearrange("p (c t) -> p c t", t=128)[:, :, :tn]
                        dst = lambda half=half, t0=t0, tn=tn: hT[:, half * 8:half * 8 + 8, t0:t0 + tn]
                        if False:
                            P.op("act", lambda E, src=src, dst=dst: E.activation(dst(), src(), AF.Copy), reads=[BpT[pb]], writes=[BhT])
                        else:
                            P.op("dve", lambda E, src=src, dst=dst: E.tensor_copy(dst(), src()), reads=[BpT[pb]], writes=[BhT])
                for (kind, col0, ncols) in chunks:
                    if variant and kind not in variant.split(","):
                        continue
                    wbi = nxt("w", 2)
                    for hf in range(2):
                        P.op("sp", lambda E, col0=col0, ncols=ncols, hf=hf: E.dma_start(out=wf[:, hf * 8:hf * 8 + 8, :ncols], in_=w_l[:, hf * 8:hf * 8 + 8, col0:col0 + ncols]),
                             writes=[Bwf], key="wf")
                    P.op("dve", lambda E, wbi=wbi, ncols=ncols: E.tensor_copy(wb[wbi][:, :, :ncols], wf[:, :, :ncols]), reads=[Bwf], writes=[Bwb[wbi]])
                    if kind == "kr":
                        P.op("dve", lambda E: E.tensor_copy(wsw[:, :, 0:32], wf[:, :, 32:64]), reads=[Bwf], writes=[Bwsw])
                        P.op("dve", lambda E: E.tensor_copy(wsw[:, :, 32:64], wf[:, :, 0:32]), reads=[Bwf], writes=[Bwsw])
                    for (q0, qn) in blocks(gn, 512):
                        tg = g0 + q0
                        if kind == "vd":
                            for (t0, tn) in blocks(qn, 128):
                                pb = nxt("pm", 4)
                                for ch in range(16):
                                    P.op("pe", lambda E, pb=pb, ch=ch, tn=tn, a=q0 + t0, wbi=wbi: E.matmul(pm[pb][:tn, :512], hT[:, ch, a:a + tn], wb[wbi][:, ch, :512], start=(ch == 0), stop=(ch == 15)),
                                         reads=[BhT, Bwb[wbi]], writes=[Bpm[pb]])
                                sa = nxt("sa", 3)
                                P.op("act", lambda E, pb=pb, sa=sa, tn=tn: E.activation(stA[sa][:tn, :], pm[pb][:tn, :], AF.Copy), reads=[Bpm[pb]], writes=[BstA[sa]])
                                P.op("sp", lambda E, sa=sa, tn=tn, a=tg + t0: E.dma_start(out=c.Vd[a:a + tn, :], in_=stA[sa][:tn, :]), reads=[BstA[sa]], key=f"sa{sa}")
                            continue
                        if kind == "kr":
                            cb = nxt("cs", 2)
                            P.op("sp", lambda E, cb=cb, qn=qn, tg=tg: E.dma_start(out=cst[cb][:, :qn], in_=cos2[si][:, tg:tg + qn]), writes=[Bcst[cb]], key=f"cs{cb}")
                            P.op("sp", lambda E, cb=cb, qn=qn, tg=tg: E.dma_start(out=snt[cb][:, :qn], in_=sin2[si][:, tg:tg + qn]), writes=[Bcst[cb]], key=f"sn{cb}")
                            pu, pv = nxt("pm", 4), nxt("pm", 4)
                            for ch in range(16):
                                P.op("pe", lambda E, pu=pu, ch=ch, qn=qn, q0=q0, wbi=wbi: E.matmul(pm[pu][:64, :qn], wb[wbi][:, ch, 0:64], hT[:, ch, q0:q0 + qn], start=(ch == 0), stop=(ch == 15)),
                                     reads=[BhT, Bwb[wbi]], writes=[Bpm[pu]])
                            for ch in range(16):
                                P.op("pe", lambda E, pv=pv, ch=ch, qn=qn, q0=q0: E.matmul(pm[pv][:64, :qn], wsw[:, ch, 0:64], hT[:, ch, q0:q0 + qn], start=(ch == 0), stop=(ch == 15)),
                                     reads=[BhT, Bwsw], writes=[Bpm[pv]])
                            rb_ = nxt("r", 2)
                            sa = nxt("sa", 3)
                            P.op("dve", lambda E, pu=pu, rb_=rb_, cb=cb, qn=qn: E.tensor_tensor(r1[rb_][:, :qn], pm[pu][:64, :qn], cst[cb][:, :qn], ALU.mult),
                                 reads=[Bpm[pu], Bcst[cb]], writes=[Br[rb_]])
                            P.op("dve", lambda E, pv=pv, rb_=rb_, cb=cb, qn=qn: E.tensor_tensor(r2[rb_][:, :qn], pm[pv][:64, :qn], snt[cb][:, :qn], ALU.mult),
                                 reads=[Bpm[pv], Bcst[cb]], writes=[Br[rb_]])
                            P.op("dve", lambda E, rb_=rb_, sa=sa, qn=qn: E.tensor_tensor(stA[sa][:64, :qn], r1[rb_][:, :qn], r2[rb_][:, :qn], ALU.add),
                                 reads=[Br[rb_]], writes=[BstA[sa]])
                            P.op("sp", lambda E, sa=sa, qn=qn, tg=tg: E.dma_start(out=c.KrT[:, tg:tg + qn], in_=stA[sa][:64, :qn]), reads=[BstA[sa]], key=f"sa{sa}")
                            continue
                        for (m0, mn) in blocks(ncols, 128):
                            pb = nxt("pm", 4)
                            for ch in range(16):
                                P.op("pe", lambda E, pb=pb, ch=ch, qn=qn, q0=q0, m0=m0, mn=mn, wbi=wbi: E.matmul(pm[pb][:mn, :qn], wb[wbi][:, ch, m0:m0 + mn], hT[:, ch, q0:q0 + qn], start=(ch == 0), stop=(ch == 15)),
                                     reads=[BhT, Bwb[wbi]], writes=[Bpm[pb]])
                            gi = (col0 + m0)
                            if kind == "uf":
                                ub = nxt("u", 2)
                                g = m0 // 128
                                P.op("act", lambda E, pb=pb, ub=ub, qn=qn: E.activation(uT[ub][:, :qn], pm[pb][:, :qn], AF.Copy), reads=[Bpm[pb]], writes=[BuT[ub]])
                                for (t0, tn) in blocks(qn, 128):
                                    p2b = nxt("p2", 2)
                                    P.op("pe", lambda E, p2b=p2b, ub=ub, t0=t0, tn=tn: E.matmul(p2[p2b][:tn, :256], uT[ub][:, t0:t0 + tn], csb[:, :], start=True, stop=True),
                                         reads=[BuT[ub], Bcs], writes=[Bp2[p2b]])
                                    sb_ = nxt("sb", 3)
                                    P.op("dve", lambda E, p2b=p2b, sb_=sb_, tn=tn: E.tensor_copy(stB[sb_][:tn, :256], p2[p2b][:tn, :256]), reads=[Bp2[p2b]], writes=[BstB[sb_]])
                                    P.op("sp", lambda E, sb_=sb_, tn=tn, a=tg + t0, g=g: E.dma_start(out=c.AB[a:a + tn, g, :], in_=stB[sb_][:tn, :256]), reads=[BstB[sb_]], key=f"sb{sb_}")
                            elif kind in ("cq", "ckv"):
                                ci = (gi - (C_CQ if kind == "cq" else C_CKV)) // 128
                                gv = gq if kind == "cq" else gk
                                dg, dsq = (c.cqg, c.cqs) if kind == "cq" else (c.ckg, c.cks)
                                sa, sb_ = nxt("sa", 3), nxt("sb", 3)
                                P.op("dve", lambda E, pb=pb, sa=sa, qn=qn, gv=gv, ci=ci: E.tensor_scalar(stA[sa][:, :qn], pm[pb][:, :qn], gv[:, ci:ci + 1], None, ALU.mult),
                                     reads=[Bpm[pb], Bgq], writes=[BstA[sa]])
                                P.op("act", lambda E, pb=pb, sb_=sb_, qn=qn: E.activation(stB[sb_][:, :qn], pm[pb][:, :qn], AF.Square), reads=[Bpm[pb]], writes=[BstB[sb_]])
                                P.op("sp", lambda E, sa=sa, qn=qn, tg=tg, dg=dg, ci=ci: E.dma_start(out=dg[ci, :, tg:tg + qn], in_=stA[sa][:, :qn]), reads=[BstA[sa]], key=f"sa{sa}")
                                P.op("sp", lambda E, sb_=sb_, qn=qn, tg=tg, dsq=dsq, ci=ci: E.dma_start(out=dsq[ci, :, tg:tg + qn], in_=stB[sb_][:, :qn]), reads=[BstB[sb_]], key=f"sb{sb_}")
                            elif kind in ("qd", "kd"):
                                h = m0 // 128
                                dd = c.qdT if kind == "qd" else c.kdT
                                sa = nxt("sa", 3)
                                P.op("act", lambda E, pb=pb, sa=sa, qn=qn: E.activation(stA[sa][:, :qn], pm[pb][:, :qn], AF.Copy), reads=[Bpm[pb]], writes=[BstA[sa]])
                                P.op("sp", lambda E, sa=sa, qn=qn, tg=tg, dd=dd, h=h: E.dma_start(out=dd[2 * h, :, tg:tg + qn], in_=stA[sa][0:64, :qn]), reads=[BstA[sa]], key=f"sa{sa}")
                                P.op("sp", lambda E, sa=sa, qn=qn, tg=tg, dd=dd, h=h: E.dma_start(out=dd[2 * h + 1, :, tg:tg + qn], in_=stA[sa][64:128, :qn]), reads=[BstA[sa]], key=f"sa{sa}")
                            elif kind == "gate":
                                ci = (gi - C_G) // 128
                                sb_ = nxt("sb", 3)
                                P.op("act", lambda E, pb=pb, sb_=sb_, qn=qn: E.activation(stB[sb_][:, :qn], pm[pb][:, :qn], AF.Silu), reads=[Bpm[pb]], writes=[BstB[sb_]])
                                P.op("sp", lambda E, sb_=sb_, qn=qn, tg=tg, ci=ci: E.dma_start(out=c.gT[ci, :, tg:tg + qn], in_=stB[sb_][:, :qn]), reads=[BstB[sb_]], key=f"sb{sb_}")
            if debug:
                dh = dsc(f"dbg_hT{l}{si}", [128, 128])
                dw = dsc(f"dbg_wb{l}{si}", [128, 128])
                dp = dsc(f"dbg_hb{l}{si}", [128, 128])
                P.op("sp", lambda E: E.dma_start(out=dh[:, :], in_=hT[:, 0, 0:128]), reads=[BhT], key="dbg0")
                P.op("sp", lambda E: E.dma_start(out=dw[:, :], in_=wb[0][:, 0, 0:128]), reads=[Bwb[0]], key="dbg1")
                P.op("sp", lambda E: E.dma_start(out=dp[:, :], in_=hb[0][:, 0:128]), reads=[Bhb[0]], key="dbg2")
            P.emit(nc, f"ip{l}{si}")


    def rstd_ops(P, src_ap, dst_ap, mult, eps, Bsrc, Bdst):
        P.op("dve", lambda E: E.tensor_scalar(dst_ap(), src_ap(), mult, eps, ALU.mult, ALU.add), reads=[Bsrc], writes=[Bdst])
        P.op("act", lambda E: E.activation(dst_ap(), dst_ap(), AF.Sqrt), reads=[Bdst], writes=[Bdst])
        P.op("dve", lambda E: E.reciprocal(dst_ap(), dst_ap()), reads=[Bdst], writes=[Bdst])

    def phase_q(l, si):
        c = S[si]
        T = c.T
        wq_l = w_uq[l].rearrange("(c p) n -> p c n", p=128)
        with ExitStack() as es:
            P = Prog()
            wf = sb(es, "wf", [128, 6, 1536], F32)
            wq = sb(es, "wq", [128, 6, 1536], BF16)
            wsw = sb(es, "wsw", [128, 6, 8, 64], BF16)
            ones = sb(es, "ones", [128, 128], BF16)
            cg = [sb(es, f"cg{i}", [128, 6, 512], BF16) for i in range(2)]
            cq2 = [sb(es, f"cq2{i}", [128, 6, 512], BF16) for i in range(2)]
            cst = [sb(es, f"cst{i}", [64, 512], F32) for i in range(2)]
            snt = [sb(es, f"snt{i}", [64, 512], F32) for i in range(2)]
            rst = [sb(es, f"rst{i}", [128, 512], F32) for i in range(2)]
            stn = [sb(es, f"stn{i}", [128, 512], BF16) for i in range(3)]
            r1 = [sb(es, f"r1{i}", [64, 512], F32) for i in range(2)]
            r2 = [sb(es, f"r2{i}", [64, 512], F32) for i in range(2)]
            pss = ps(es, "pss")
            pn = [ps(es, f"pn{i}") for i in range(2)]
            pu = [ps(es, f"pu{i}") for i in range(2)]
            pv = [ps(es, f"pv{i}") for i in range(2)]
            Bwf, Bwq, Bones = Buf("wf"), Buf("wq"), Buf("ones")
            Bcg = [Buf("cg0"), Buf("cg1")]
            Bcs = [Buf("cs0"), Buf("cs1")]
            Brst = [Buf("rst0"), Buf("rst1")]
            Bstn = [Buf(f"stn{i}") for i in range(3)]
            Br = [Buf("r0"), Buf("r1")]
            Bpss = PB("pss")
            Bpn, Bpu, Bpv = [PB("pn0"), PB("pn1")], [PB("pu0"), PB("pu1")], [PB("pv0"), PB("pv1")]
            P.op("sp", lambda E: E.dma_start(out=wf[:], in_=wq_l), writes=[Bwf], key="wf")
            P.op("dve", lambda E: E.tensor_copy(wq[:], wf[:]), reads=[Bwf], writes=[Bwq])
            wfv = wf[:].rearrange("p c (h f) -> p c h f", f=192)
            for cc in range(6):
                P.op("dve", lambda E, cc=cc: E.tensor_copy(wsw[:, cc, :, 0:32], wfv[:, cc, :, 160:192]), reads=[Bwf], writes=[Bwq])
                P.op("dve", lambda E, cc=cc: E.tensor_copy(wsw[:, cc, :, 32:64], wfv[:, cc, :, 128:160]), reads=[Bwf], writes=[Bwq])
            P.op("dve", lambda E: E.memset(ones[:], 1.0), writes=[Bones])
            cnt = {}

            def nxt(k, n):
                v = cnt.get(k, 0)
                cnt[k] = v + 1
                return v % n

            for (q0, qn) in blocks(T, 512):
                b = nxt("b", 2)
                P.op("sp", lambda E, b=b, q0=q0, qn=qn: E.dma_start(out=cg[b][:, :, :qn], in_=c.cqg[:, :, q0:q0 + qn].rearrange("c p t -> p c t")), writes=[Bcg[b]], key=f"cg{b}")
                P.op("sp", lambda E, b=b, q0=q0, qn=qn: E.dma_start(out=cq2[b][:, :, :qn], in_=c.cqs[:, :, q0:q0 + qn].rearrange("c p t -> p c t")), writes=[Bcg[b]], key=f"cs{b}")
                P.op("sp", lambda E, b=b, q0=q0, qn=qn: E.dma_start(out=cst[b][:, :qn], in_=cos2[si][:, q0:q0 + qn]), writes=[Bcs[b]], key=f"co{b}")
                P.op("sp", lambda E, b=b, q0=q0, qn=qn: E.dma_start(out=snt[b][:, :qn], in_=sin2[si][:, q0:q0 + qn]), writes=[Bcs[b]], key=f"si{b}")
                for cc in range(6):
                    P.op("pe", lambda E, b=b, cc=cc, qn=qn: E.matmul(pss[:, :qn], ones[:, :], cq2[b][:, cc, :qn], start=(cc == 0), stop=(cc == 5)),
                         reads=[Bones, Bcg[b]], writes=[Bpss])
                rstd_ops(P, lambda qn=qn: pss[:, :qn], lambda b=b, qn=qn: rst[b][:, :qn], 1.0 / 768, EPS, Bpss, Brst[b])
                for h in range(8):
                    i = nxt("pn", 2)
                    for cc in range(6):
                        P.op("pe", lambda E, i=i, b=b, cc=cc, qn=qn, h=h: E.matmul(pn[i][:, :qn], wq[:, cc, h * 192:h * 192 + 128], cg[b][:, cc, :qn], start=(cc == 0), stop=(cc == 5)),
                             reads=[Bwq, Bcg[b]], writes=[Bpn[i]])
                    for cc in range(6):
                        P.op("pe", lambda E, i=i, b=b, cc=cc, qn=qn, h=h: E.matmul(pu[i][:64, :qn], wq[:, cc, h * 192 + 128:h * 192 + 192], cg[b][:, cc, :qn], start=(cc == 0), stop=(cc == 5)),
                             reads=[Bwq, Bcg[b]], writes=[Bpu[i]])
                    for cc in range(6):
                        P.op("pe", lambda E, i=i, b=b, cc=cc, qn=qn, h=h: E.matmul(pv[i][:64, :qn], wsw[:, cc, h, :], cg[b][:, cc, :qn], start=(cc == 0), stop=(cc == 5)),
                             reads=[Bwq, Bcg[b]], writes=[Bpv[i]])
                    s1 = nxt("st", 3)
                    P.op("dve", lambda E, i=i, b=b, s1=s1, qn=qn: E.tensor_tensor(stn[s1][:, :qn], pn[i][:, :qn], rst[b][:, :qn], ALU.mult),
                         reads=[Bpn[i], Brst[b]], writes=[Bstn[s1]])
                    P.op("sp", lambda E, s1=s1, h=h, q0=q0, qn=qn: E.dma_start(out=c.QnT[h, :, q0:q0 + qn], in_=stn[s1][:, :qn]), reads=[Bstn[s1]], key=f"st{s1}")
                    rb_ = nxt("r", 2)
                    s2 = nxt("st", 3)
                    P.op("dve", lambda E, i=i, b=b, rb_=rb_, qn=qn: E.tensor_tensor(r1[rb_][:, :qn], pu[i][:64, :qn], cst[b][:, :qn], ALU.mult),
                         reads=[Bpu[i], Bcs[b]], writes=[Br[rb_]])
                    P.op("dve", lambda E, i=i, b=b, rb_=rb_, qn=qn: E.tensor_tensor(r2[rb_][:, :qn], pv[i][:64, :qn], snt[b][:, :qn], ALU.mult),
                         reads=[Bpv[i], Bcs[b]], writes=[Br[rb_]])
                    P.op("dve", lambda E, rb_=rb_, qn=qn: E.tensor_tensor(r1[rb_][:, :qn], r1[rb_][:, :qn], r2[rb_][:, :qn], ALU.add), reads=[Br[rb_]], writes=[Br[rb_]])
                    P.op("dve", lambda E, rb_=rb_, b=b, s2=s2, qn=qn: E.tensor_tensor(stn[s2][:64, :qn], r1[rb_][:, :qn], rst[b][:64, :qn], ALU.mult),
                         reads=[Br[rb_], Brst[b]], writes=[Bstn[s2]])
                    P.op("sp", lambda E, s2=s2, h=h, q0=q0, qn=qn: E.dma_start(out=c.QrT[h, :, q0:q0 + qn], in_=stn[s2][:64, :qn]), reads=[Bstn[s2]], key=f"st{s2}")
            P.emit(nc, f"q{l}{si}")

    def phase_kv(l, si):
        c = S[si]
        T = c.T
        wk_l = w_ukv[l].rearrange("(c p) n -> p c n", p=128)
        with ExitStack() as es:
            P = Prog()
            wf = sb(es, "wf", [128, 4, 2048], F32)
            wk = sb(es, "wk", [128, 4, 1024], BF16)
            wv = sb(es, "wv", [128, 4, 1024], BF16)
            ones = sb(es, "ones", [128, 128], BF16)
            cg = [sb(es, f"cg{i}", [128, 4, 512], BF16) for i in range(2)]
            cq2 = [sb(es, f"cq2{i}", [128, 4, 512], BF16) for i in range(2)]
            rst = [sb(es, f"rst{i}", [128, 512], F32) for i in range(2)]
            rcol = [sb(es, f"rcol{i}", [128, 1], F32) for i in range(2)]
            stn = [sb(es, f"stn{i}", [128, 512], BF16) for i in range(3)]
            pss = ps(es, "pss")
            pc = ps(es, "pc")
            pk = [ps(es, f"pk{i}") for i in range(3)]
            pvv = [ps(es, f"pvv{i}") for i in range(3)]
            Bwf, Bwk, Bones = Buf("wf"), Buf("wk"), Buf("ones")
            Bcg = [Buf("cg0"), Buf("cg1")]
            Brst = [Buf("rst0"), Buf("rst1")]
            Brc = [Buf("rc0"), Buf("rc1")]
            Bstn = [Buf(f"stn{i}") for i in range(3)]
            Bpss, Bpc = PB("pss"), PB("pc")
            Bpk = [PB(f"pk{i}") for i in range(3)]
            Bpvv = [PB(f"pvv{i}") for i in range(3)]
            P.op("sp", lambda E: E.dma_start(out=wf[:], in_=wk_l), writes=[Bwf], key="wf")
            wfv = wf[:].rearrange("p c (h two f) -> p c h two f", two=2, f=128)
            for cc in range(4):
                P.op("dve", lambda E, cc=cc: E.tensor_copy(wk[:, cc, :].rearrange("p (h f) -> p h f", f=128), wfv[:, cc, :, 0, :]), reads=[Bwf], writes=[Bwk])
                P.op("dve", lambda E, cc=cc: E.tensor_copy(wv[:, cc, :].rearrange("p (h f) -> p h f", f=128), wfv[:, cc, :, 1, :]), reads=[Bwf], writes=[Bwk])
            P.op("dve", lambda E: E.memset(ones[:], 1.0), writes=[Bones])
            cnt = {}

            def nxt(k, n):
                v = cnt.get(k, 0)
                cnt[k] = v + 1
                return v % n

            for (q0, qn) in blocks(T, 512):
                b = nxt("b", 2)
                P.op("sp", lambda E, b=b, q0=q0, qn=qn: E.dma_start(out=cg[b][:, :, :qn], in_=c.ckg[:, :, q0:q0 + qn].rearrange("c p t -> p c t")), writes=[Bcg[b]], key=f"cg{b}")
                P.op("sp", lambda E, b=b, q0=q0, qn=qn: E.dma_start(out=cq2[b][:, :, :qn], in_=c.cks[:, :, q0:q0 + qn].rearrange("c p t -> p c t")), writes=[Bcg[b]], key=f"cs{b}")
                for cc in range(4):
                    P.op("pe", lambda E, b=b, cc=cc, qn=qn: E.matmul(pss[:, :qn], ones[:, :], cq2[b][:, cc, :qn], start=(cc == 0), stop=(cc == 3)),
                         reads=[Bones, Bcg[b]], writes=[Bpss])
                rstd_ops(P, lambda qn=qn: pss[:, :qn], lambda b=b, qn=qn: rst[b][:, :qn], 1.0 / 512, EPS, Bpss, Brst[b])
                for h in range(8):
                    i = nxt("pk", 3)
                    for cc in range(4):
                        P.op("pe", lambda E, i=i, b=b, cc=cc, qn=qn, h=h: E.matmul(pk[i][:, :qn], wk[:, cc, h * 128:(h + 1) * 128], cg[b][:, cc, :qn], start=(cc == 0), stop=(cc == 3)),
                             reads=[Bwk, Bcg[b]], writes=[Bpk[i]])
                    s1 = nxt("st", 3)
                    P.op("dve", lambda E, i=i, b=b, s1=s1, qn=qn: E.tensor_tensor(stn[s1][:, :qn], pk[i][:, :qn], rst[b][:, :qn], ALU.mult),
                         reads=[Bpk[i], Brst[b]], writes=[Bstn[s1]])
                    P.op("sp", lambda E, s1=s1, h=h, q0=q0, qn=qn: E.dma_start(out=c.KnT[h, :, q0:q0 + qn], in_=stn[s1][:, :qn]), reads=[Bstn[s1]], key=f"st{s1}")
                for (t0, tn) in blocks(qn, 128):
                    rc = nxt("rc", 2)
                    for cc in range(4):
                        P.op("pe", lambda E, b=b, cc=cc, t0=t0, tn=tn: E.matmul(pc[:tn, 0:1], cq2[b][:, cc, t0:t0 + tn], ones[:, 0:1], start=(cc == 0), stop=(cc == 3)),
                             reads=[Bones, Bcg[b]], writes=[Bpc])
                    rstd_ops(P, lambda tn=tn: pc[:tn, 0:1], lambda rc=rc, tn=tn: rcol[rc][:tn, 0:1], 1.0 / 512, EPS, Bpc, Brc[rc])
                    for hh in range(2):
                        i = nxt("pvv", 3)
                        for cc in range(4):
                            P.op("pe", lambda E, i=i, b=b, cc=cc, t0=t0, tn=tn, hh=hh: E.matmul(pvv[i][:tn, :512], cg[b][:, cc, t0:t0 + tn], wv[:, cc, hh * 512:(hh + 1) * 512], start=(cc == 0), stop=(cc == 3)),
                                 reads=[Bwk, Bcg[b]], writes=[Bpvv[i]])
                        s1 = nxt("st", 3)
                        P.op("dve", lambda E, i=i, rc=rc, s1=s1, tn=tn: E.tensor_scalar(stn[s1][:tn, :], pvv[i][:tn, :], rcol[rc][:tn, 0:1], None, ALU.mult),
                             reads=[Bpvv[i], Brc[rc]], writes=[Bstn[s1]])
                        P.op("sp", lambda E, s1=s1, hh=hh, a=q0 + t0, tn=tn: E.dma_start(out=c.V[a:a + tn, hh * 512:(hh + 1) * 512], in_=stn[s1][:tn, :]), reads=[Bstn[s1]], key=f"st{s1}")
            P.emit(nc, f"kv{l}{si}")

    G.phase_q, G.phase_kv = phase_q, phase_kv

    MLA_SCALE = 1.0 / math.sqrt(192.0)

    def phase_mla(l, si):
        c = S[si]
        T = c.T
        kbs = blocks(T, 128)
        nkb = len(kbs)
        nfull = T // 128
        with ExitStack() as es:
            P = Prog()
            ones = sb(es, "ones", [128, 128], BF16)
            Kr = sb(es, "Kr", [64, T], BF16)
            Kn_ = [sb(es, f"Kn{i}", [128, T], BF16) for i in range(2)]
            Qn_ = [sb(es, f"Qn{i}", [128, T], BF16) for i in range(2)]
            Qr_ = [sb(es, f"Qr{i}", [64, T], BF16) for i in range(2)]
            Vh_ = [sb(es, f"Vh{i}", [128, nkb, 128], BF16) for i in range(2)]
            gt = [sb(es, f"gt{i}", [128, 512], BF16) for i in range(2)]
            PT = [sb(es, f"PT{i}", [128, 512], BF16) for i in range(4)]
            acc = [sb(es, f"acc{i}", [128, 512], F32) for i in range(2)]
            accb = sb(es, "accb", [128, 512], BF16)
            Bacc = [Buf("acc0"), Buf("acc1")]
            Baccb = Buf("accb")
            R = [sb(es, f"R{i}", [128, 512], F32) for i in range(2)]
            Y = [sb(es, f"Y{i}", [128, 512], F32) for i in range(2)]
            Yb = [sb(es, f"Yb{i}", [128, 512], BF16) for i in range(2)]
            pS = [ps(es, f"pS{i}") for i in range(3)]
            pO = [ps(es, f"pO{i}") for i in range(2)]
            pL = [ps(es, f"pL{i}") for i in range(2)]
            Bones, BKr = Buf("ones"), Buf("Kr")
            BK_, BQ_, BV_ = [Buf("K0"), Buf("K1")], [Buf("Q0"), Buf("Q1")], [Buf("V0"), Buf("V1")]
            Bgt = [Buf("gt0"), Buf("gt1")]
            BPT = [Buf(f"PT{i}") for i in range(4)]
            BR = [Buf("R0"), Buf("R1")]
            BY = [Buf("Y0"), Buf("Y1")]
            BYb = [Buf("Yb0"), Buf("Yb1")]
            BpS = [PB(f"pS{i}") for i in range(3)]
            BpO = [PB("pO0"), PB("pO1")]
            BpL = [PB("pL0"), PB("pL1")]
            P.op("dve", lambda E: E.memset(ones[:], 1.0), writes=[Bones])
            P.op("sp", lambda E: E.dma_start(out=Kr[:, :], in_=c.KrT[:, :]), writes=[BKr], key="kr")
            cnt = {}

            def nxt(k, n):
                v = cnt.get(k, 0)
                cnt[k] = v + 1
                return v % n

            def load_head(h):
                s = h % 2
                P.op("sp", lambda E: E.dma_start(out=Kn_[s][:, :], in_=c.KnT[h, :, :]), writes=[BK_[s]], key=f"kn{s}")
                P.op("sp", lambda E: E.dma_start(out=Qn_[s][:, :], in_=c.QnT[h, :, :]), writes=[BQ_[s]], key=f"qn{s}")
                P.op("sp", lambda E: E.dma_start(out=Qr_[s][:, :], in_=c.QrT[h, :, :]), writes=[BQ_[s]], key=f"qr{s}")
                for (b0, bn) in blocks(nfull, 8):
                    P.op("sp", lambda E, b0=b0, bn=bn: E.dma_start(out=Vh_[s][:, b0:b0 + bn, :], in_=c.V[b0 * 128:(b0 + bn) * 128, h * 128:(h + 1) * 128].rearrange("(k p) d -> p k d", p=128)),
                         writes=[BV_[s]], key=f"v{s}")
                if T % 128:
                    P.op("sp", lambda E: E.dma_start(out=Vh_[s][:T % 128, nfull, :], in_=c.V[nfull * 128:T, h * 128:(h + 1) * 128]), writes=[BV_[s]], key=f"w{s}")

            load_head(0)
            for h in range(8):
                if h + 1 < 8:
                    load_head(h + 1)
                Kn, Qn, Qr, Vh = Kn_[h % 2], Qn_[h % 2], Qr_[h % 2], Vh_[h % 2]
                BK, BQ, BV = BK_[h % 2], BQ_[h % 2], BV_[h % 2]
                for (q0, qn) in blocks(T, 512):
                    gb = nxt("g", 2)
                    ob = nxt("o", 2)
                    P.op("sp", lambda E, gb=gb, h=h, q0=q0, qn=qn: E.dma_start(out=gt[gb][:, :qn], in_=c.gT[4 + h, :, q0:q0 + qn]), writes=[Bgt[gb]], key=f"g{gb}")

                    def qk(kb, q0=q0, qn=qn, Kn=Kn, Qn=Qn, Qr=Qr, BK=BK, BQ=BQ):
                        k0, nk = kbs[kb]
                        sbk = kb % 3
                        P.op("pe", lambda E: E.matmul(pS[sbk][:nk, :qn], Kn[:, k0:k0 + nk], Qn[:, q0:q0 + qn], start=True, stop=False),
                             reads=[BK, BQ], writes=[BpS[sbk]])
                        P.op("pe", lambda E: E.matmul(pS[sbk][:nk, :qn], Kr[:, k0:k0 + nk], Qr[:, q0:q0 + qn], start=False, stop=True),
                             reads=[BKr, BQ], writes=[BpS[sbk]])

                    P.op("dve", lambda E, ob=ob, qn=qn: E.memset(acc[ob][:, :qn], 0.0), writes=[Bacc[ob]])
                    qk(0)
                    if nkb > 1:
                        qk(1)
                    for kb in range(nkb):
                        k0, nk = kbs[kb]
                        if kb + 2 < nkb:
                            qk(kb + 2)
                        sbk = kb % 3
                        pb = nxt("pt", 4)
                        P.op("act", lambda E, sbk=sbk, pb=pb, nk=nk, qn=qn: E.activation(PT[pb][:nk, :qn], pS[sbk][:nk, :qn], AF.Exp, scale=MLA_SCALE),
                             reads=[BpS[sbk]], writes=[BPT[pb]])
                        P.op("pe", lambda E, pb=pb, ob=ob, kb=kb, nk=nk, qn=qn, Vh=Vh: E.matmul(pO[ob][:, :qn], Vh[:nk, kb, :], PT[pb][:nk, :qn], start=(kb == 0), stop=(kb == nkb - 1)),
                             reads=[BV, BPT[pb]], writes=[BpO[ob]])
                        if kb % 2 == 0:
                            P.op("pe", lambda E, pb=pb, ob=ob, kb=kb, nk=nk, qn=qn: E.matmul(pL[ob][:, :qn], ones[:nk, :], PT[pb][:nk, :qn], start=(kb == 0), stop=False),
                                 reads=[Bones, BPT[pb]], writes=[BpL[ob]])
                        else:
                            P.op("dve", lambda E, pb=pb, ob=ob, nk=nk, qn=qn: E.tensor_tensor(acc[ob][:nk, :qn], acc[ob][:nk, :qn], PT[pb][:nk, :qn], ALU.add),
                                 reads=[BPT[pb], Bacc[ob]], writes=[Bacc[ob]])
                    P.op("dve", lambda E, ob=ob, qn=qn: E.tensor_copy(accb[:, :qn], acc[ob][:, :qn]), reads=[Bacc[ob]], writes=[Baccb])
                    P.op("pe", lambda E, ob=ob, qn=qn: E.matmul(pL[ob][:, :qn], ones[:, :], accb[:, :qn], start=False, stop=True),
                         reads=[Bones, Baccb], writes=[BpL[ob]])
                    P.op("dve", lambda E, ob=ob, qn=qn: E.reciprocal(R[ob][:, :qn], pL[ob][:, :qn]), reads=[BpL[ob]], writes=[BR[ob]])
                    P.op("dve", lambda E, ob=ob, qn=qn: E.tensor_tensor(Y[ob][:, :qn], pO[ob][:, :qn], R[ob][:, :qn], ALU.mult), reads=[BpO[ob], BR[ob]], writes=[BY[ob]])
                    P.op("dve", lambda E, ob=ob, gb=gb, qn=qn: E.tensor_tensor(Yb[ob][:, :qn], Y[ob][:, :qn], gt[gb][:, :qn], ALU.mult), reads=[BY[ob], Bgt[gb]], writes=[BYb[ob]])
                    P.op("sp", lambda E, ob=ob, h=h, q0=q0, qn=qn: E.dma_start(out=c.yT[4 + h, :, q0:q0 + qn], in_=Yb[ob][:, :qn]), reads=[BYb[ob]], key=f"y{ob}")
            P.emit(nc, f"mla{l}{si}")

    def phase_diff(l, si):
        c = S[si]
        T = c.T
        kbs = blocks(T, 128)
        nkb = len(kbs)
        nfull = T // 128
        lam_init = 0.8 - 0.6 * math.exp(-0.3 * l)
        with ExitStack() as es:
            P = Prog()
            ones = sb(es, "ones", [128, 128], BF16)
            onesf = sb(es, "onesf", [128, 128], F32)
            identf = sb(es, "identf", [128, 128], F32)
            ident = sb(es, "ident", [128, 128], BF16)
            lv = sb(es, "lv", [64, 4], F32)
            lp = sb(es, "lp", [64, 2], BF16)
            ee = sb(es, "ee", [128, 2], F32)
            neglam = sb(es, "neglam", [128, 1], F32)
            gd = sb(es, "gd", [128, 1], F32)
            Kd_ = [[sb(es, f"Kd{s}{a}", [64, T], BF16) for a in range(2)] for s in range(2)]
            Qd_ = [[sb(es, f"Qd{s}{a}", [64, T], BF16) for a in range(2)] for s in range(2)]
            Vh_ = [sb(es, f"Vh{s}", [128, nkb, 128], BF16) for s in range(2)]
            Bt_ = [sb(es, f"Bt{s}", [128, 6, 512], BF16) for s in range(2)]
            cfb_ = [sb(es, f"cfb{s}", [128, 2], BF16) for s in range(2)]
            cf_ = [sb(es, f"cf{s}", [128, 2], F32) for s in range(2)]
            gt = [sb(es, f"gt{i}", [128, 512], BF16) for i in range(2)]
            PT = [sb(es, f"PT{i}", [128, 512], BF16) for i in range(4)]
            acc = [sb(es, f"acc{i}", [128, 512], F32) for i in range(2)]
            accb = [sb(es, f"accb{i}", [128, 512], BF16) for i in range(2)]
            Bacc = [Buf("acc0"), Buf("acc1")]
            Baccb = [Buf("accb0"), Buf("accb1")]
            R = [sb(es, f"R{a}", [128, 512], F32) for a in range(2)]
            On = [sb(es, f"On{a}", [128, 512], F32) for a in range(2)]
            Dm = sb(es, "Dm", [128, 512], F32)
            D2 = sb(es, "D2", [128, 512], BF16)
            rs = sb(es, "rs", [128, 512], F32)
            Yb = [sb(es, f"Yb{i}", [128, 512], BF16) for i in range(2)]
            pS = [ps(es, f"pS{i}") for i in range(3)]
            pO = [ps(es, f"pO{i}") for i in range(2)]
            pL = [ps(es, f"pL{i}") for i in range(2)]
            pX = ps(es, "pX")
            Bones, Bid, Blv, Bee, Bnl, Bgd, Bcf = Buf("ones"), Buf("id"), Buf("lv"), Buf("ee"), Buf("nl"), Buf("gd"), Buf("cf")
            BK_, BQ_, BV_, BBt_, Bcf_ = ([Buf("K0"), Buf("K1")], [Buf("Q0"), Buf("Q1")], [Buf("V0"), Buf("V1")],
                                         [Buf("Bt0"), Buf("Bt1")], [Buf("cf0"), Buf("cf1")])
            Bgt = [Buf("gt0"), Buf("gt1")]
            BPT = [Buf(f"PT{i}") for i in range(4)]
            BR = [Buf("R0"), Buf("R1")]
            BOn = [Buf("On0"), Buf("On1")]
            BDm, BD2, Brs = Buf("Dm"), Buf("D2"), Buf("rs")
            BYb = [Buf("Yb0"), Buf("Yb1")]
            BpS = [PB(f"pS{i}") for i in range(3)]
            BpO = [PB("pO0"), PB("pO1")]
            BpL = [PB("pL0"), PB("pL1")]
            BpX = PB("pX")
            P.op("dve", lambda E: E.memset(ones[:], 1.0), writes=[Bones])
            P.op("dve", lambda E: E.memset(onesf[:], 1.0), writes=[Bones])
            P.op("sp", lambda E: E.dma_start(out=identf[:], in_=ident_d[:, :]), writes=[Bid], key="id")
            P.op("dve", lambda E: E.tensor_copy(ident[:], identf[:]), reads=[Bid], writes=[Bid])
            for i in range(4):
                P.op("sp", lambda E, i=i: E.dma_start(out=lv[:, i:i + 1], in_=lam_in[i][l].rearrange("(p o) -> p o", o=1)), writes=[Blv], key=f"lv{i}")
            P.op("sp", lambda E: E.dma_start(out=gd[:, :], in_=diff_norm[l].rearrange("(p o) -> p o", o=1)), writes=[Bgd], key="gd")
            P.op("dve", lambda E: E.tensor_scalar(gd[:, :], gd[:, :], 1.0 - lam_init, None, ALU.mult), reads=[Bgd], writes=[Bgd])
            P.op("dve", lambda E: E.tensor_tensor(lp[:, 0:1], lv[:, 0:1], lv[:, 1:2], ALU.mult), reads=[Blv], writes=[Blv])
            P.op("dve", lambda E: E.tensor_tensor(lp[:, 1:2], lv[:, 2:3], lv[:, 3:4], ALU.mult), reads=[Blv], writes=[Blv])
            P.op("pe", lambda E: E.matmul(pX[:, 0:2], ones[:64, :], lp[:, 0:2], start=True, stop=True), reads=[Bones, Blv], writes=[BpX])
            P.op("act", lambda E: E.activation(ee[:, :], pX[:, 0:2], AF.Exp), reads=[BpX], writes=[Bee])
            P.op("dve", lambda E: E.tensor_tensor(neglam[:, :], ee[:, 1:2], ee[:, 0:1], ALU.subtract), reads=[Bee], writes=[Bnl])
            P.op("dve", lambda E: E.tensor_scalar(neglam[:, :], neglam[:, :], -lam_init, None, ALU.add), reads=[Bnl], writes=[Bnl])
            cnt = {}

            def nxt(k, n):
                v = cnt.get(k, 0)
                cnt[k] = v + 1
                return v % n

            def load_head(h):
                s = h % 2
                for a in range(2):
                    P.op("sp", lambda E, a=a: E.dma_start(out=Kd_[s][a][:, :], in_=c.kdT[2 * h + a, :, :]), writes=[BK_[s]], key=f"k{s}{a}")
                    P.op("sp", lambda E, a=a: E.dma_start(out=Qd_[s][a][:, :], in_=c.qdT[2 * h + a, :, :]), writes=[BQ_[s]], key=f"q{s}{a}")
                for (b0, bn) in blocks(nfull, 8):
                    P.op("sp", lambda E, b0=b0, bn=bn: E.dma_start(out=Vh_[s][:, b0:b0 + bn, :], in_=c.Vd[b0 * 128:(b0 + bn) * 128, h * 128:(h + 1) * 128].rearrange("(k p) d -> p k d", p=128)),
                         writes=[BV_[s]], key=f"v{s}")
                if T % 128:
                    P.op("sp", lambda E: E.dma_start(out=Vh_[s][:T % 128, nfull, :], in_=c.Vd[nfull * 128:T, h * 128:(h + 1) * 128]), writes=[BV_[s]], key=f"w{s}")
                gflat = Gb[h].rearrange("r w -> (r w)")
                for i in range(6):
                    off = RMAX - (-128 + 128 * i)
                    P.op("sp", lambda E, i=i, off=off: E.dma_start(out=Bt_[s][:, i, :], in_=gflat[off:off + 128 * (WFULL - 1)].rearrange("(p w) -> p w", w=WFULL - 1)[:, 0:512]),
                         writes=[BBt_[s]], key=f"bt{s}")
                P.op("sp", lambda E: E.dma_start(out=cfb_[s][:, 0:1], in_=Gb[h, 0:128, 0:1], allow_slow_non_contiguous=True), writes=[Bcf_[s]], key=f"cfa{s}")
                P.op("sp", lambda E: E.dma_start(out=cfb_[s][:, 1:2], in_=Gb[h, 0:128, WFULL - 1:WFULL], allow_slow_non_contiguous=True), writes=[Bcf_[s]], key=f"cfb{s}")
                P.op("dve", lambda E: E.tensor_scalar(cf_[s][:, :], cfb_[s][:, :], 0.125, None, ALU.mult), reads=[Bcf_[s]], writes=[Bcf_[s]])

            load_head(0)
            for h in range(4):
                if h + 1 < 4:
                    load_head(h + 1)
                Kd, Qd, Vh, Bt, cf = Kd_[h % 2], Qd_[h % 2], Vh_[h % 2], Bt_[h % 2], cf_[h % 2]
                BK, BQ, BV, BBt, Bcf = BK_[h % 2], BQ_[h % 2], BV_[h % 2], BBt_[h % 2], Bcf_[h % 2]
                for (q0, qn) in blocks(T, 512):
                    gb = nxt("g", 2)
                    P.op("sp", lambda E, gb=gb, h=h, q0=q0, qn=qn: E.dma_start(out=gt[gb][:, :qn], in_=c.gT[12 + h, :, q0:q0 + qn]), writes=[Bgt[gb]], key=f"g{gb}")
                    steps = [(kb, a) for kb in range(nkb) for a in range(2)]

                    def klass(kb, q0=q0, qn=qn):
                        k0, nk = kbs[kb]
                        relmin, relmax = k0 - (q0 + qn - 1), k0 + nk - 1 - q0
                        if relmin >= 91:
                            return ("far", 0)
                        if relmax <= -91:
                            return ("far", 1)
                        d = k0 - q0
                        i = (d + 128) // 128
                        assert (d + 128) % 128 == 0 and 0 <= i < 6, (d, i)
                        return ("near", i)

                    def qk(st, q0=q0, qn=qn, Kd=Kd, Qd=Qd, Bt=Bt, BK=BK, BQ=BQ, BBt=BBt):
                        kb, a = steps[st]
                        k0, nk = kbs[kb]
                        sbk = st % 3
                        kl = klass(kb)
                        near = kl[0] == "near" and not DBG.get("nobias")
                        P.op("pe", lambda E: E.matmul(pS[sbk][:nk, :qn], Kd[a][:, k0:k0 + nk], Qd[a][:, q0:q0 + qn], start=True, stop=not near),
                             reads=[BK, BQ], writes=[BpS[sbk]])
                        if near:
                            P.op("pe", lambda E: E.matmul(pS[sbk][:nk, :qn], ident[:nk, :nk], Bt[:nk, kl[1], :qn], start=False, stop=True),
                                 reads=[Bid, BBt], writes=[BpS[sbk]])

                    for a in range(2):
                        P.op("dve", lambda E, a=a, qn=qn: E.memset(acc[a][:, :qn], 0.0), writes=[Bacc[a]])
                    qk(0)
                    if len(steps) > 1:
                        qk(1)
                    for st in range(len(steps)):
                        kb, a = steps[st]
                        k0, nk = kbs[kb]
                        if st + 2 < len(steps):
                            qk(st + 2)
                        sbk = st % 3
                        pb = nxt("pt", 4)
                        kl = klass(kb)
                        if kl[0] == "far":
                            P.op("act", lambda E, sbk=sbk, pb=pb, nk=nk, qn=qn, j=kl[1], cf=cf: E.activation(PT[pb][:nk, :qn], pS[sbk][:nk, :qn], AF.Exp, bias=cf[:nk, j:j + 1], scale=0.125),
                                 reads=[BpS[sbk], Bcf], writes=[BPT[pb]])
                        else:
                            P.op("act", lambda E, sbk=sbk, pb=pb, nk=nk, qn=qn: E.activation(PT[pb][:nk, :qn], pS[sbk][:nk, :qn], AF.Exp, scale=0.125),
                                 reads=[BpS[sbk]], writes=[BPT[pb]])
                        P.op("pe", lambda E, pb=pb, a=a, kb=kb, nk=nk, qn=qn, Vh=Vh: E.matmul(pO[a][:, :qn], Vh[:nk, kb, :], PT[pb][:nk, :qn], start=(kb == 0), stop=(kb == nkb - 1)),
                             reads=[BV, BPT[pb]], writes=[BpO[a]])
                        if kb % 2 == 0:
                            P.op("pe", lambda E, pb=pb, a=a, kb=kb, nk=nk, qn=qn: E.matmul(pL[a][:, :qn], ones[:nk, :], PT[pb][:nk, :qn], start=(kb == 0), stop=False),
                                 reads=[Bones, BPT[pb]], writes=[BpL[a]])
                        else:
                            P.op("dve", lambda E, pb=pb, a=a, nk=nk, qn=qn: E.tensor_tensor(acc[a][:nk, :qn], acc[a][:nk, :qn], PT[pb][:nk, :qn], ALU.add),
                                 reads=[BPT[pb], Bacc[a]], writes=[Bacc[a]])
                    for a in range(2):
                        P.op("dve", lambda E, a=a, qn=qn: E.tensor_copy(accb[a][:, :qn], acc[a][:, :qn]), reads=[Bacc[a]], writes=[Baccb[a]])
                        P.op("pe", lambda E, a=a, qn=qn: E.matmul(pL[a][:, :qn], ones[:, :], accb[a][:, :qn], start=False, stop=True),
                             reads=[Bones, Baccb[a]], writes=[BpL[a]])
                    for a in range(2):
                        P.op("dve", lambda E, a=a, qn=qn: E.reciprocal(R[a][:, :qn], pL[a][:, :qn]), reads=[BpL[a]], writes=[BR[a]])
                        P.op("dve", lambda E, a=a, qn=qn: E.tensor_tensor(On[a][:, :qn], pO[a][:, :qn], R[a][:, :qn], ALU.mult), reads=[BpO[a], BR[a]], writes=[BOn[a]])
                    if debug and h == 3 and q0 == 0:
                        d0 = dsc(f"dbg_On0_{l}{si}", [128, 512], F32)
                        d1 = dsc(f"dbg_On1_{l}{si}", [128, 512], F32)
                        d2 = dsc(f"dbg_nl_{l}{si}", [128, 1], F32)
                        P.op("sp", lambda E, qn=qn: E.dma_start(out=d0[:, :qn], in_=On[0][:, :qn]), reads=[BOn[0]], key="dbg0")
                        P.op("sp", lambda E, qn=qn: E.dma_start(out=d1[:, :qn], in_=On[1][:, :qn]), reads=[BOn[1]], key="dbg1")
                        P.op("sp", lambda E: E.dma_start(out=d2[:, :], in_=neglam[:, :]), reads=[Bnl], key="dbg2")
                    P.op("dve", lambda E, qn=qn: E.tensor_scalar(On[1][:, :qn], On[1][:, :qn], neglam[:, 0:1], None, ALU.mult), reads=[BOn[1], Bnl], writes=[BOn[1]])
                    P.op("dve", lambda E, qn=qn: E.tensor_tensor(Dm[:, :qn], On[0][:, :qn], On[1][:, :qn], ALU.add), reads=[BOn[0], BOn[1]], writes=[BDm])
                    P.op("dve", lambda E, qn=qn: E.tensor_tensor(D2[:, :qn], Dm[:, :qn], Dm[:, :qn], ALU.mult), reads=[BDm], writes=[BD2])
                    P.op("pe", lambda E, qn=qn: E.matmul(pX[:, :qn], ones[:, :], D2[:, :qn], start=True, stop=True), reads=[Bones, BD2], writes=[BpX])
                    rstd_ops(P, lambda qn=qn: pX[:, :qn], lambda qn=qn: rs[:, :qn], 1.0 / 128, 1e-5, BpX, Brs)
                    P.op("dve", lambda E, qn=qn: E.tensor_tensor(Dm[:, :qn], Dm[:, :qn], rs[:, :qn], ALU.mult), reads=[BDm, Brs], writes=[BDm])
                    P.op("dve", lambda E, qn=qn: E.tensor_scalar(Dm[:, :qn], Dm[:, :qn], gd[:, 0:1], None, ALU.mult), reads=[BDm, Bgd], writes=[BDm])
                    if debug and h == 3 and q0 == 0:
                        d3 = dsc(f"dbg_lv_{l}{si}", [64, 4], F32)
                        d4 = dsc(f"dbg_ee_{l}{si}", [128, 2], F32)
                        d5 = dsc(f"dbg_rs_{l}{si}", [128, 512], F32)
                        d6 = dsc(f"dbg_Dm_{l}{si}", [128, 512], F32)
                        P.op("sp", lambda E: E.dma_start(out=d3[:, :], in_=lv[:, :]), reads=[Blv], key="dbg0")
                        P.op("sp", lambda E: E.dma_start(out=d4[:, :], in_=ee[:, :]), reads=[Bee], key="dbg1")
                        P.op("sp", lambda E, qn=qn: E.dma_start(out=d5[:, :qn], in_=rs[:, :qn]), reads=[Brs], key="dbg2")
                        P.op("sp", lambda E, qn=qn: E.dma_start(out=d6[:, :qn], in_=Dm[:, :qn]), reads=[BDm], key="dbg0")
                    yb = nxt("yb", 2)
                    P.op("dve", lambda E, yb=yb, gb=gb, qn=qn: E.tensor_tensor(Yb[yb][:, :qn], Dm[:, :qn], gt[gb][:, :qn], ALU.mult), reads=[BDm, Bgt[gb]], writes=[BYb[yb]])
                    P.op("sp", lambda E, yb=yb, h=h, q0=q0, qn=qn: E.dma_start(out=c.yT[12 + h, :, q0:q0 + qn], in_=Yb[yb][:, :qn]), reads=[BYb[yb]], key=f"y{yb}")
            P.emit(nc, f"df{l}{si}")

    def phase_dft(l, si):
        c = S[si]
        T = c.T
        sbs = blocks(T, 128)
        fscale = 1.0 / math.sqrt(T * 128.0)
        with ExitStack() as es:
            P = Prog()
            wff = sb(es, "wff", [128, 4, 128], F32)
            wfm = sb(es, "wfm", [128, 4, 128], BF16)
            ABt = [sb(es, f"ABt{i}", [128, 4, 256], BF16) for i in range(3)]
            Ct = [sb(es, f"Ct{i}", [128, 512], BF16) for i in range(3)]
            Nt = [sb(es, f"Nt{i}", [128, 512], BF16) for i in range(3)]
            fT = [sb(es, f"fT{i}", [128, 512], BF16) for i in range(2)]
            gt = [sb(es, f"gt{i}", [128, 512], BF16) for i in range(2)]
            Yb = [sb(es, f"Yb{i}", [128, 512], BF16) for i in range(2)]
            pF = [ps(es, f"pF{i}") for i in range(4)]
            pY = [ps(es, f"pY{i}") for i in range(2)]
            Bw = Buf("w")
            BAB = [Buf(f"AB{i}") for i in range(3)]
            BC = [Buf(f"C{i}") for i in range(3)]
            BfT = [Buf("fT0"), Buf("fT1")]
            Bgt = [Buf("gt0"), Buf("gt1")]
            BYb = [Buf("Yb0"), Buf("Yb1")]
            BpF = [PB(f"pF{i}") for i in range(4)]
            BpY = [PB("pY0"), PB("pY1")]
            P.op("sp", lambda E: E.dma_start(out=wff[:], in_=w_fmix[l].rearrange("g c d -> c g d")), writes=[Bw], key="w")
            P.op("dve", lambda E: E.tensor_copy(wfm[:], wff[:]), reads=[Bw], writes=[Bw])
            cnt = {}

            def nxt(k, n):
                v = cnt.get(k, 0)
                cnt[k] = v + 1
                return v % n

            for (k0, nk) in blocks(T, 512):
                for si_, (s0, ns) in enumerate(sbs):
                    b = nxt("in", 3)
                    P.op("sp", lambda E, b=b, s0=s0, ns=ns: E.dma_start(out=ABt[b][:ns], in_=c.AB[s0:s0 + ns]), writes=[BAB[b]], key=f"ab{b}")
                    P.op("sp", lambda E, b=b, s0=s0, ns=ns, k0=k0, nk=nk: E.dma_start(out=Ct[b][:ns, :nk], in_=c.CT[s0:s0 + ns, k0:k0 + nk]), writes=[BC[b]], key=f"ct{b}")
                    P.op("sp", lambda E, b=b, s0=s0, ns=ns, k0=k0, nk=nk: E.dma_start(out=Nt[b][:ns, :nk], in_=c.NST[s0:s0 + ns, k0:k0 + nk]), writes=[BC[b]], key=f"nt{b}")
                    for g in range(4):
                        P.op("pe", lambda E, b=b, g=g, ns=ns, nk=nk, si_=si_: E.matmul(pF[g][:, :nk], ABt[b][:ns, g, 0:128], Ct[b][:ns, :nk], start=(si_ == 0), stop=False),
                             reads=[BAB[b], BC[b]], writes=[BpF[g]])
                        P.op("pe", lambda E, b=b, g=g, ns=ns, nk=nk, si_=si_: E.matmul(pF[g][:, :nk], ABt[b][:ns, g, 128:256], Nt[b][:ns, :nk], start=False, stop=(si_ == len(sbs) - 1)),
                             reads=[BAB[b], BC[b]], writes=[BpF[g]])
                for g in range(4):
                    fb = nxt("f", 2)
                    P.op("act", lambda E, fb=fb, g=g, nk=nk: E.activation(fT[fb][:, :nk], pF[g][:, :nk], AF.Copy, scale=fscale), reads=[BpF[g]], writes=[BfT[fb]])
                    yb = nxt("y", 2)
                    P.op("pe", lambda E, fb=fb, yb=yb, g=g, nk=nk: E.matmul(pY[yb][:, :nk], wfm[:, g, :], fT[fb][:, :nk], start=True, stop=True),
                         reads=[Bw, BfT[fb]], writes=[BpY[yb]])
                    gb = nxt("g", 2)
                    P.op("sp", lambda E, gb=gb, g=g, k0=k0, nk=nk: E.dma_start(out=gt[gb][:, :nk], in_=c.gT[g, :, k0:k0 + nk]), writes=[Bgt[gb]], key=f"g{gb}")
                    P.op("dve", lambda E, yb=yb, gb=gb, nk=nk: E.tensor_tensor(Yb[yb][:, :nk], pY[yb][:, :nk], gt[gb][:, :nk], ALU.mult), reads=[BpY[yb], Bgt[gb]], writes=[BYb[yb]])
                    P.op("sp", lambda E, yb=yb, g=g, k0=k0, nk=nk: E.dma_start(out=c.yT[g, :, k0:k0 + nk], in_=Yb[yb][:, :nk]), reads=[BYb[yb]], key=f"y{yb}")
            P.emit(nc, f"dft{l}{si}")

    def phase_out(l, si, xsrc, xdst, final):
        c = S[si]
        T = c.T
        wo_l = w_o[l].rearrange("(c p) n -> p c n", p=128)
        with ExitStack() as es:
            P = Prog()
            wf = sb(es, "wf", [128, 16, 256], F32)
            wo = sb(es, "wo", [128, 16, 2048], BF16)
            yt = sb(es, "yt", [128, 16, 512], BF16)
            xt = [sb(es, f"xt{i}", [128, D], F32) for i in range(2)]
            xo = [sb(es, f"xo{i}", [128, D], F32) for i in range(2)]
            junk = sb(es, "junk", [128, D], BF16)
            gt = sb(es, "gt", [128, D], F32)
            grow = sb(es, "grow", [1, D], F32)
            one1 = sb(es, "one1", [1, 128], F32)
            ss = [sb(es, f"ss{i}", [128, 2], F32) for i in range(2)]
            po = [ps(es, f"po{i}") for i in range(4)]
            Bwf, Bwo, Byt = Buf("wf"), Buf("wo"), Buf("yt")
            Bxt = [Buf("xt0"), Buf("xt1")]
            Bxo = [Buf("xo0"), Buf("xo1")]
            Bjunk, Bgt, Bgrow, Bone1 = Buf("junk"), Buf("gt"), Buf("grow"), Buf("one1")
            Bss = [Buf("ss0"), Buf("ss1")]
            Bpo = [PB(f"po{i}") for i in range(4)]
            for i in range(8):
                for hf in range(2):
                    P.op("sp", lambda E, i=i, hf=hf: E.dma_start(out=wf[:, hf * 8:hf * 8 + 8, :], in_=wo_l[:, hf * 8:hf * 8 + 8, i * 256:(i + 1) * 256]), writes=[Bwf], key="wf")
                P.op("dve", lambda E, i=i: E.tensor_copy(wo[:, :, i * 256:(i + 1) * 256], wf[:]), reads=[Bwf], writes=[Bwo])
            if final:
                P.op("sp", lambda E: E.dma_start(out=grow[:], in_=final_norm.rearrange("(o n) -> o n", o=1)), writes=[Bgrow], key="c4")
                P.op("dve", lambda E: E.memset(one1[:], 1.0), writes=[Bone1])
                for i in range(4):
                    P.op("pe", lambda E, i=i: E.matmul(po[i][:, :], one1[:, :], grow[:, i * 512:(i + 1) * 512], start=True, stop=True),
                         reads=[Bone1, Bgrow], writes=[Bpo[i]])
                    P.op("dve", lambda E, i=i: E.tensor_copy(gt[:, i * 512:(i + 1) * 512], po[i][:, :]), reads=[Bpo[i]], writes=[Bgt])
            cnt = {}

            def nxt(k, n):
                v = cnt.get(k, 0)
                cnt[k] = v + 1
                return v % n

            for (q0, qn) in blocks(T, 512):
                for hf in range(2):
                    P.op("sp", lambda E, q0=q0, qn=qn, hf=hf: E.dma_start(out=yt[:, hf * 8:hf * 8 + 8, :qn], in_=c.yT[hf * 8:hf * 8 + 8, :, q0:q0 + qn].rearrange("c p t -> p c t")), writes=[Byt], key="yt")
                for (t0, tn) in blocks(qn, 128):
                    a = q0 + t0
                    b = nxt("x", 2)
                    P.op("sp", lambda E, b=b, a=a, tn=tn: E.dma_start(out=xt[b][:tn, :], in_=xsrc[a:a + tn, :]), writes=[Bxt[b]], key=f"x{b}")
                    for dc in range(4):
                        for cc in range(16):
                            P.op("pe", lambda E, dc=dc, cc=cc, t0=t0, tn=tn: E.matmul(po[dc][:tn, :], yt[:, cc, t0:t0 + tn], wo[:, cc, dc * 512:(dc + 1) * 512], start=(cc == 0), stop=(cc == 15)),
                                 reads=[Byt, Bwo], writes=[Bpo[dc]])
                        P.op("dve", lambda E, dc=dc, b=b, tn=tn: E.tensor_tensor(xo[b][:tn, dc * 512:(dc + 1) * 512], po[dc][:tn, :], xt[b][:tn, dc * 512:(dc + 1) * 512], ALU.add),
                             reads=[Bpo[dc], Bxt[b]], writes=[Bxo[b]])
                    if not final:
                        P.op("sp", lambda E, b=b, a=a, tn=tn: E.dma_start(out=xdst[a:a + tn, :], in_=xo[b][:tn, :]), reads=[Bxo[b]], key=f"o{b}")
                    else:
                        P.op("act", lambda E, b=b, tn=tn: E.activation(junk[:tn, :], xo[b][:tn, :], AF.Square, accum_out=ss[b][:tn, 0:1]),
                             reads=[Bxo[b]], writes=[Bjunk, Bss[b]])
                        rstd_ops(P, lambda b=b, tn=tn: ss[b][:tn, 0:1], lambda b=b, tn=tn: ss[b][:tn, 1:2], 1.0 / D, EPS, Bss[b], Bss[b])
                        P.op("dve", lambda E, b=b, tn=tn: E.tensor_scalar(xo[b][:tn, :], xo[b][:tn, :], ss[b][:tn, 1:2], None, ALU.mult), reads=[Bxo[b], Bss[b]], writes=[Bxo[b]])
                        P.op("dve", lambda E, b=b, tn=tn: E.tensor_tensor(xo[b][:tn, :], xo[b][:tn, :], gt[:tn, :], ALU.mult), reads=[Bxo[b], Bgt], writes=[Bxo[b]])
                        lo = max(a, N_META)
                        if lo < a + tn:
                            P.op("sp", lambda E, b=b, a=a, tn=tn, lo=lo: E.dma_start(out=xdst[lo - N_META:a + tn - N_META, :], in_=xo[b][lo - a:tn, :]), reads=[Bxo[b]], key=f"o{b}")
            P.emit(nc, f"out{l}{si}")

    def full():
        for si in range(nseg):
            phase_tables(si)
        phase_bias()
        for l in range(NL):
            for si in range(nseg):
                xs = x_in[si] if l == 0 else S[si].x1
                phase_inproj(l, si, xs)
                phase_q(l, si)
                phase_kv(l, si)
                phase_dft(l, si)
                phase_mla(l, si)
                phase_diff(l, si)
                if l == NL - 1:
                    phase_out(l, si, xs, y_out[si], True)
                else:
                    phase_out(l, si, xs, S[si].x1, False)

    G.phase_mla, G.phase_diff, G.phase_dft, G.phase_out, G.full = phase_mla, phase_diff, phase_dft, phase_out, full
    G.phase_tables, G.phase_bias, G.phase_inproj = phase_tables, phase_bias, phase_inproj
    G.nc, G.S, G.x_in, G.y_out, G.Gb = nc, S, x_in, y_out, Gb
    G.din = dict(w_uq=w_uq, w_ukv=w_ukv, q_norm=q_norm, kv_norm=kv_norm, cos2=cos2, sin2=sin2, w_fmix=w_fmix, w_o=w_o,
                 lam=lam_in, diff_norm=diff_norm, final_norm=final_norm, rel_bias=rel_bias, ident=ident_d)
    G.sb, G.ps = sb, ps
    return G


SEG_T = [4096 + N_META, 8192 + N_META]


def kernel(x_prompt, x_sample, meta_tokens, rel_bias, final_norm, norm_w, w_in, w_fmix, q_norm, w_uq,
           kv_norm, w_ukv, lam_q1, lam_k1, lam_q2, lam_k2, diff_norm, w_o):
    f32 = lambda a: np.ascontiguousarray(np.asarray(a, dtype=np.float32))
    x_prompt, x_sample, meta = f32(x_prompt), f32(x_sample), f32(meta_tokens)
    G = build(SEG_T, debug=False)
    G.full()
    ident, cs, oh = misc_consts()
    shared = dict(ident=ident, cs=cs, oh=oh, rel_bias=f32(rel_bias), final_norm=f32(final_norm), norm_w=f32(norm_w),
                  w_in=f32(w_in), w_fmix=f32(w_fmix), q_norm=f32(q_norm), w_uq=f32(w_uq), kv_norm=f32(kv_norm),
                  w_ukv=f32(w_ukv), lam_q1=f32(lam_q1), lam_k1=f32(lam_k1), lam_q2=f32(lam_q2), lam_k2=f32(lam_k2),
                  diff_norm=f32(diff_norm), w_o=f32(w_o))
    for s, T in enumerate(SEG_T):
        shared[f"cos2_{s}"], shared[f"sin2_{s}"] = host_consts(T)
        shared[f"tj_{s}"], shared[f"tk_{s}"] = dft_consts(T)
    xp = [np.concatenate([meta, x_prompt[g]], 0) for g in range(x_prompt.shape[0])]
    in_maps = []
    for c in range(8):
        m = dict(shared)
        m["x0"] = np.concatenate([meta, x_sample[c]], 0)
        m["x1"] = xp[c // 4]
        in_maps.append(m)
    res = run_bass_kernel_spmd(G.nc, in_maps, core_ids=list(range(8)))
    y_sample = np.stack([np.asarray(res.results[c]["y0"], dtype=np.float32) for c in range(8)], 0)
    y_prompt = np.stack([np.asarray(res.results[4 * g]["y1"], dtype=np.float32) for g in range(2)], 0)
    return (y_prompt, y_sample)
```

```python
import math
from contextlib import ExitStack
import numpy as np
import concourse.bass as bass
import concourse.mybir as mybir
from concourse.bass_utils import run_bass_kernel_spmd

F32 = mybir.dt.float32
BF16 = mybir.dt.bfloat16
I32 = mybir.dt.int32
AF = mybir.ActivationFunctionType
ALU = mybir.AluOpType

D = 2048
N_META = 16
NL = 2
INW = 5440
C_UF, C_CQ, C_CKV, C_KR, C_QD, C_KD, C_VD, C_G = 0, 512, 1280, 1792, 1856, 2368, 2880, 3392
EPS = 1e-6
RMAX = 768
WFULL = 1600
PI_S = 3.14159


class Buf:
    def __init__(self, name, excl=False):
        self.name = name
        self.excl = excl
        self.writers = {}
        self.readers = {}


def PB(name):
    return Buf(name, True)


class Op:
    __slots__ = ("eng", "fn", "deps", "sig", "seq", "key", "cum")


SEM_SKIP = [0]
EMIT_N = [0]
DBG = {}
MAXQ = [4]
ENGS = ["pe", "act", "dve", "pool", "sp"]


class Prog:
    def __init__(self):
        self.ops = {e: [] for e in ENGS}
        self.keycum = {}

    def op(self, eng, fn, reads=(), writes=(), key=None):
        o = Op()
        o.eng, o.fn, o.deps, o.sig, o.key, o.seq, o.cum = eng, fn, [], False, key, 0, 0
        if key is not None:
            self.keycum[key] = self.keycum.get(key, 0) + 16
            o.cum = self.keycum[key]
        ex = [b for b in reads if b.excl]
        if ex:
            reads = [b for b in reads if not b.excl]
            writes = list(writes) + [b for b in ex if b not in writes]
        for b in reads:
            for w in b.writers.values():
                self._dep(o, w)
        for b in writes:
            for r in b.readers.values():
                self._dep(o, r)
            for w in b.writers.values():
                self._dep(o, w)
        for b in reads:
            b.readers[(eng, key)] = o
        for b in writes:
            b.writers[(eng, key)] = o
            b.readers = {}
        self.ops[eng].append(o)
        return o

    def _dep(self, o, p):
        if p is o:
            return
        if p.key is None and o.key is None and p.eng == o.eng and p.eng == "pe":
            return
        if p.key is None and p.eng == o.eng:
            pass
        o.deps.append(p)
        p.sig = True

    def emit(self, nc, name):
        EMIT_N[0] += 1
        name = f"{name}_{EMIT_N[0]}"
        with ExitStack() as es:
            engsem = {e: nc.alloc_semaphore(name=f"{name}_s_{e}") for e in ENGS}
            keysem = {k: nc.alloc_semaphore(name=f"{name}_k{i}") for i, k in enumerate(self.keycum)}
            allsems = list(engsem.values()) + list(keysem.values())
            for e in ENGS:
                c = 0
                for o in self.ops[e]:
                    if o.key is None and o.sig:
                        c += 1
                        o.seq = c
            block = es.enter_context(nc.Block())

            def run(E, e):
                waited = {}
                inflight = []
                for o in self.ops[e]:
                    for p in o.deps:
                        if p.key is not None:
                            sid, sem, val = ("k", p.key), keysem[p.key], p.cum
                        else:
                            sid, sem, val = ("e", p.eng), engsem[p.eng], p.seq
                        if waited.get(sid, 0) < val:
                            E.wait_ge(sem, val)
                            waited[sid] = val
                    if o.key is not None and len(inflight) >= MAXQ[0]:
                        k0_, c0_ = inflight.pop(0)
                        if waited.get(("k", k0_), 0) < c0_:
                            E.wait_ge(keysem[k0_], c0_)
                            waited[("k", k0_)] = c0_
                    inst = o.fn(E)
                    if o.key is not None:
                        inflight.append((o.key, o.cum))
                        inst.then_inc(keysem[o.key], 16)
                    elif o.sig:
                        inst.then_inc(engsem[e], 1)
                if e == "sp":
                    for k, tot in self.keycum.items():
                        if waited.get(("k", k), 0) < tot:
                            E.wait_ge(keysem[k], tot)

            @block.tensor
            def _(E):
                run(E, "pe")

            @block.scalar
            def _(E):
                run(E, "act")

            @block.vector
            def _(E):
                run(E, "dve")

            @block.gpsimd
            def _(E):
                run(E, "pool")

            @block.sync
            def _(E):
                run(E, "sp")
        nc.clear_and_free_semaphores(allsems)
        nc.all_engine_barrier()


def blocks(n, size):
    return [(s, min(size, n - s)) for s in range(0, n, size)]


class Ctx:
    pass


def t5_bucket_np(rel):
    nb = 16
    max_exact = 8
    ret = (rel > 0).astype(np.int64) * nb
    n = np.abs(rel)
    nf = np.maximum(n, 1).astype(np.float32)
    large = max_exact + (np.log(nf / np.float32(max_exact)) / np.float32(math.log(128 / max_exact))
                         * np.float32(nb - max_exact)).astype(np.int32)
    large = np.minimum(large, nb - 1)
    return ret + np.where(n < max_exact, n, large)


def host_consts(T):
    pos = np.arange(T, dtype=np.float32)
    inv = (10000.0 ** (-np.arange(0, 64, 2, dtype=np.float32) / 64)).astype(np.float32)
    ang = pos[:, None] * inv[None, :]
    cos, sin = np.cos(ang).astype(np.float32).T, np.sin(ang).astype(np.float32).T
    cos2 = np.concatenate([cos, cos], 0)
    sin2 = np.concatenate([-sin, sin], 0)
    return np.ascontiguousarray(cos2), np.ascontiguousarray(sin2)


def dft_consts(T):
    s = np.arange(T, dtype=np.int64)
    j = np.arange(512, dtype=np.int64)
    a = 2 * np.pi * ((s[:, None] * j[None, :]) % T).astype(np.float64) / T
    tj = np.stack([np.cos(a), np.sin(a)], 1).astype(np.float32)
    k0 = np.arange(0, T, 512, dtype=np.int64)
    a = 2 * np.pi * ((s[:, None] * k0[None, :]) % T).astype(np.float64) / T
    tk = np.stack([np.cos(a), np.sin(a)], 2).astype(np.float32)
    return np.ascontiguousarray(tj), np.ascontiguousarray(tk)


def misc_consts():
    ident = np.eye(128, dtype=np.float32)
    c = np.arange(128)
    ang = 2 * np.pi * np.outer(c, c) / 128.0
    cs = np.concatenate([np.cos(ang), np.sin(ang)], 1).astype(np.float32)
    n = np.arange(WFULL)
    rel = RMAX - n
    bk = t5_bucket_np(rel)
    oh = np.zeros((32, WFULL), np.float32)
    oh[bk, n] = 8.0
    return ident, cs, oh


def build(segT, debug=False):
    nc = bass.Bass("TRN2", target_bir_lowering=False)
    G = Ctx()
    nseg = len(segT)

    def din(name, shape, dt=F32):
        return nc.dram_tensor(name, list(shape), dt, kind="ExternalInput").ap()

    def dsc(name, shape, dt=BF16):
        t = nc.dram_tensor(name, list(shape), dt, kind=("ExternalOutput" if debug else "Internal")).ap()
        dbg[name] = t
        return t

    x_in = [din(f"x{s}", [segT[s], D]) for s in range(nseg)]
    y_out = [nc.dram_tensor(f"y{s}", [segT[s] - N_META, D], F32, kind="ExternalOutput").ap() for s in range(nseg)]
    cos2 = [din(f"cos2_{s}", [64, segT[s]]) for s in range(nseg)]
    sin2 = [din(f"sin2_{s}", [64, segT[s]]) for s in range(nseg)]
    tjd = [din(f"tj_{s}", [segT[s], 2, 512]) for s in range(nseg)]
    tkd = [din(f"tk_{s}", [segT[s], len(blocks(segT[s], 512)), 2]) for s in range(nseg)]
    ident_d = din("ident", [128, 128])
    cs_d = din("cs", [128, 256])
    oh_d = din("oh", [32, WFULL])
    rel_bias = din("rel_bias", [32, 4])
    final_norm = din("final_norm", [D])
    norm_w = din("norm_w", [NL, D])
    w_in = din("w_in", [NL, D, INW])
    w_fmix = din("w_fmix", [NL, 4, 128, 128])
    q_norm = din("q_norm", [NL, 768])
    w_uq = din("w_uq", [NL, 768, 1536])
    kv_norm = din("kv_norm", [NL, 512])
    w_ukv = din("w_ukv", [NL, 512, 2048])
    lam_in = [din(n, [NL, 64]) for n in ("lam_q1", "lam_k1", "lam_q2", "lam_k2")]
    diff_norm = din("diff_norm", [NL, 128])
    w_o = din("w_o", [NL, D, D])

    dbg = {}
    S = []
    for s in range(nseg):
        T = segT[s]
        c = Ctx()
        c.T = T
        c.x1 = dsc(f"x1_{s}", [T, D], F32)
        c.AB = dsc(f"AB_{s}", [T, 4, 256])
        c.cqg = dsc(f"cqg_{s}", [6, 128, T])
        c.cqs = dsc(f"cqs_{s}", [6, 128, T])
        c.ckg = dsc(f"ckg_{s}", [4, 128, T])
        c.cks = dsc(f"cks_{s}", [4, 128, T])
        c.KrT = dsc(f"KrT_{s}", [64, T])
        c.qdT = dsc(f"qdT_{s}", [8, 64, T])
        c.kdT = dsc(f"kdT_{s}", [8, 64, T])
        c.Vd = dsc(f"Vd_{s}", [T, 512])
        c.gT = dsc(f"gT_{s}", [16, 128, T])
        c.QnT = dsc(f"QnT_{s}", [8, 128, T])
        c.QrT = dsc(f"QrT_{s}", [8, 64, T])
        c.KnT = dsc(f"KnT_{s}", [8, 128, T])
        c.V = dsc(f"V_{s}", [T, 1024])
        c.yT = dsc(f"yT_{s}", [16, 128, T])
        c.CT = dsc(f"CT_{s}", [T, T])
        c.NST = dsc(f"NST_{s}", [T, T])
        S.append(c)
    Gb = dsc("Gb", [4, 132, WFULL])

    uid = [0]

    def sb(es, name, shape, dt):
        uid[0] += 1
        return es.enter_context(nc.sbuf_tensor(f"sb{uid[0]}_{name}", list(shape), dt))

    def ps(es, name, shape=(128, 512), dt=F32):
        uid[0] += 1
        return es.enter_context(nc.psum_tensor(f"ps{uid[0]}_{name}", list(shape), dt))

    nc_allow = nc.allow_non_contiguous_dma("small strided const loads")
    nc_allow.__enter__()
    nc_lp = nc.allow_low_precision("bf16 matmul operands by design")
    nc_lp.__enter__()

    def phase_tables(si, variant=""):
        c = S[si]
        T = c.T
        nkb = len(blocks(T, 512))
        with ExitStack() as es:
            P = Prog()
            tj = [sb(es, f"tj{i}", [128, 2, 512], F32) for i in range(2)]
            tk = [sb(es, f"tk{i}", [128, nkb, 2], F32) for i in range(2)]
            ntk = [sb(es, f"ntk{i}", [128, nkb, 2], F32) for i in range(2)]
            tmp = [sb(es, f"tmp{i}", [128, 2, 512], F32) for i in range(2)]
            tm2 = [sb(es, f"tm2{i}", [128, 2, 512], F32) for i in range(2)]
            ot = [sb(es, f"ot{i}", [128, 2, 512], BF16) for i in range(2)]
            Btj = [Buf("tj0"), Buf("tj1")]
            Btk = [Buf("tk0"), Buf("tk1")]
            Btmp = [Buf("tmp0"), Buf("tmp1")]
            Bot = [Buf("ot0"), Buf("ot1")]
            it = 0
            for si_, (s0, ns) in enumerate(blocks(T, 128)):
                a_ = si_ % 2
                P.op("sp", lambda E, a_=a_, s0=s0, ns=ns: E.dma_start(out=tj[a_][:ns], in_=tjd[si][s0:s0 + ns]), writes=[Btj[a_]], key=f"tj{a_}")
                if "B" in variant:
                    P.op("dve", lambda E, a_=a_, ns=ns: E.memset(tk[a_][:ns], 0.5), writes=[Btk[a_]])
                else:
                    P.op("sp", lambda E, a_=a_, s0=s0, ns=ns: E.dma_start(out=tk[a_][:ns], in_=tkd[si][s0:s0 + ns]), writes=[Btk[a_]], key=f"tk{a_}")
                P.op("dve", lambda E, a_=a_, ns=ns: E.tensor_scalar(ntk[a_][:ns], tk[a_][:ns], -1.0, None, ALU.mult), reads=[Btk[a_]], writes=[Btk[a_]])
                for kb, (k0, nk) in enumerate(blocks(T, 512)):
                    b = it % 2
                    it += 1
                    ck, sk = tk[a_][:ns, kb, 0:1], tk[a_][:ns, kb, 1:2]
                    nck, nsk = ntk[a_][:ns, kb, 0:1], ntk[a_][:ns, kb, 1:2]
                    P.op("dve", lambda E, a_=a_, b=b, ns=ns, nk=nk, ck=ck: E.tensor_scalar(tmp[b][:ns, 0, :nk], tj[a_][:ns, 0, :nk], ck, None, ALU.mult),
                         reads=[Btj[a_], Btk[a_]], writes=[Btmp[b]])
                    P.op("dve", lambda E, a_=a_, b=b, ns=ns, nk=nk, nsk=nsk: E.tensor_scalar(tm2[b][:ns, 0, :nk], tj[a_][:ns, 1, :nk], nsk, None, ALU.mult),
                         reads=[Btj[a_], Btk[a_]], writes=[Btmp[b]])
                    P.op("dve", lambda E, a_=a_, b=b, ns=ns, nk=nk: E.tensor_tensor(ot[b][:ns, 0, :nk], tm2[b][:ns, 0, :nk], tmp[b][:ns, 0, :nk], ALU.add),
                         reads=[Btmp[b]], writes=[Bot[b]])
                    P.op("dve", lambda E, a_=a_, b=b, ns=ns, nk=nk, nck=nck: E.tensor_scalar(tmp[b][:ns, 1, :nk], tj[a_][:ns, 1, :nk], nck, None, ALU.mult),
                         reads=[Btj[a_], Btk[a_]], writes=[Btmp[b]])
                    P.op("dve", lambda E, a_=a_, b=b, ns=ns, nk=nk, nsk=nsk: E.tensor_scalar(tm2[b][:ns, 1, :nk], tj[a_][:ns, 0, :nk], nsk, None, ALU.mult),
                         reads=[Btj[a_], Btk[a_]], writes=[Btmp[b]])
                    P.op("dve", lambda E, a_=a_, b=b, ns=ns, nk=nk: E.tensor_tensor(ot[b][:ns, 1, :nk], tm2[b][:ns, 1, :nk], tmp[b][:ns, 1, :nk], ALU.add),
                         reads=[Btmp[b]], writes=[Bot[b]])
                    P.op("sp", lambda E, b=b, s0=s0, ns=ns, k0=k0, nk=nk: E.dma_start(out=c.CT[s0:s0 + ns, k0:k0 + nk], in_=ot[b][:ns, 0, :nk]),
                         reads=[Bot[b]], key=f"c{b}")
                    if "A" not in variant:
                        P.op("sp", lambda E, b=b, s0=s0, ns=ns, k0=k0, nk=nk: E.dma_start(out=c.NST[s0:s0 + ns, k0:k0 + nk], in_=ot[b][:ns, 1, :nk]),
                             reads=[Bot[b]], key=f"n{b}")
            P.emit(nc, f"tb{si}")

    def phase_bias():
        with ExitStack() as es:
            P = Prog()
            oh = sb(es, "oh", [32, WFULL], F32)
            rb = sb(es, "rb", [32, 4], F32)
            ohh = sb(es, "ohh", [32, WFULL], F32)
            one = sb(es, "one32", [32, 128], F32)
            gsb = sb(es, "gsb", [128, WFULL], BF16)
            pss = [ps(es, f"pb{i}") for i in range(4)]
            Boh, Brb, Bohh, Bone, Bg = Buf("oh"), Buf("rb"), Buf("ohh"), Buf("one"), Buf("g")
            Bps = [PB(f"pb{i}") for i in range(4)]
            P.op("sp", lambda E: E.dma_start(out=oh[:], in_=oh_d[:, :]), writes=[Boh], key="l0")
            P.op("sp", lambda E: E.dma_start(out=rb[:], in_=rel_bias[:, :]), writes=[Brb], key="l1")
            P.op("dve", lambda E: E.memset(one[:], 1.0), writes=[Bone])
            for h in range(4):
                P.op("dve", lambda E, h=h: E.tensor_scalar(ohh[:], oh[:], rb[:, h:h + 1], None, ALU.mult), reads=[Boh, Brb], writes=[Bohh])
                for i, (n0, nn) in enumerate(blocks(WFULL, 512)):
                    P.op("pe", lambda E, i=i, n0=n0, nn=nn: E.matmul(pss[i][:, :nn], one[:, :], ohh[:, n0:n0 + nn], start=True, stop=True),
                         reads=[Bohh, Bone], writes=[Bps[i]])
                    P.op("act", lambda E, i=i, n0=n0, nn=nn: E.activation(gsb[:, n0:n0 + nn], pss[i][:, :nn], AF.Copy), reads=[Bps[i]], writes=[Bg])
                P.op("sp", lambda E, h=h: E.dma_start(out=Gb[h, 0:128, :], in_=gsb[:]), reads=[Bg], key="st")
            P.emit(nc, "bias")

    SG = 2064

    def phase_inproj(l, si, xsrc, variant=""):
        c = S[si]
        T = c.T
        w_l = w_in[l].rearrange("(c p) n -> p c n", p=128)
        chunks = [("uf", C_UF, 512), ("cq", C_CQ, 512), ("cq", C_CQ + 512, 256), ("ckv", C_CKV, 512), ("kr", C_KR, 64),
                  ("qd", C_QD, 512), ("kd", C_KD, 512), ("vd", C_VD, 512)] + [("gate", C_G + 512 * i, 512) for i in range(4)]
        with ExitStack() as es:
            P = Prog()
            hT = sb(es, "hT", [128, 16, SG], BF16)
            wb = [sb(es, f"wb{i}", [128, 16, 512], BF16) for i in range(2)]
            wsw = sb(es, "wsw", [128, 16, 64], BF16)
            wf = sb(es, "wf", [128, 16, 512], F32)
            Bwf = Buf("wf")
            xt = [sb(es, f"xt{i}", [128, D], F32) for i in range(2)]
            hb = [sb(es, f"hb{i}", [128, D], BF16) for i in range(2)]
            junk = sb(es, "junk", [128, D], BF16)
            gt = sb(es, "gt", [128, D], F32)
            grow = sb(es, "grow", [1, D], F32)
            one1 = sb(es, "one1", [1, 128], F32)
            ss = [sb(es, f"ss{i}", [128, 2], F32) for i in range(2)]
            identf = sb(es, "identf", [128, 128], F32)
            ident = sb(es, "ident", [128, 128], BF16)
            csf = sb(es, "csf", [128, 256], F32)
            csb = sb(es, "csb", [128, 256], BF16)
            gq = sb(es, "gq", [128, 6], F32)
            gk = sb(es, "gk", [128, 4], F32)
            cst = [sb(es, f"cst{i}", [64, 512], F32) for i in range(2)]
            snt = [sb(es, f"snt{i}", [64, 512], F32) for i in range(2)]
            uT = [sb(es, f"uT{i}", [128, 512], BF16) for i in range(2)]
            stA = [sb(es, f"stA{i}", [128, 512], BF16) for i in range(3)]
            stB = [sb(es, f"stB{i}", [128, 512], BF16) for i in range(3)]
            r1 = [sb(es, f"r1_{i}", [64, 512], F32) for i in range(2)]
            r2 = [sb(es, f"r2_{i}", [64, 512], F32) for i in range(2)]
            pT = [ps(es, f"pT{i}", (128, 1024), BF16) for i in range(2)]
            pm = [ps(es, f"pm{i}") for i in range(4)]
            p2 = [ps(es, f"p2{i}") for i in range(2)]
            BhT = Buf("hT")
            Bwb = [Buf("wb0"), Buf("wb1")]
            Bwsw = Buf("wsw")
            Bxt = [Buf("xt0"), Buf("xt1")]
            Bhb = [Buf("hb0"), Buf("hb1")]
            Bjunk, Bgt, Bgrow, Bone1 = Buf("junk"), Buf("gt"), Buf("grow"), Buf("one1")
            Bss = [Buf("ss0"), Buf("ss1")]
            Bid, Bcs, Bgq = Buf("id"), Buf("cs"), Buf("gq")
            Bcst = [Buf("cst0"), Buf("cst1")]
            BuT = [Buf("uT0"), Buf("uT1")]
            BstA = [Buf(f"stA{i}") for i in range(3)]
            BstB = [Buf(f"stB{i}") for i in range(3)]
            Br = [Buf("r0"), Buf("r1")]
            BpT = [PB("pT0"), PB("pT1")]
            Bpm = [PB(f"pm{i}") for i in range(4)]
            Bp2 = [PB("p20"), PB("p21")]
            P.op("sp", lambda E: E.dma_start(out=identf[:], in_=ident_d[:, :]), writes=[Bid], key="c0")
            P.op("dve", lambda E: E.tensor_copy(ident[:], identf[:]), reads=[Bid], writes=[Bid])
            P.op("sp", lambda E: E.dma_start(out=csf[:], in_=cs_d[:, :]), writes=[Bcs], key="c1")
            P.op("dve", lambda E: E.tensor_copy(csb[:], csf[:]), reads=[Bcs], writes=[Bcs])
            P.op("sp", lambda E: E.dma_start(out=gq[:], in_=q_norm[l].rearrange("(c p) -> p c", p=128), allow_slow_non_contiguous=True), writes=[Bgq], key="c2")
            P.op("sp", lambda E: E.dma_start(out=gk[:], in_=kv_norm[l].rearrange("(c p) -> p c", p=128), allow_slow_non_contiguous=True), writes=[Bgq], key="c3")
            P.op("sp", lambda E: E.dma_start(out=grow[:], in_=norm_w[l:l + 1, :]), writes=[Bgrow], key="c4")
            P.op("dve", lambda E: E.memset(one1[:], 1.0), writes=[Bone1])
            for i in range(4):
                P.op("pe", lambda E, i=i: E.matmul(pm[i][:, :], one1[:, :], grow[:, i * 512:(i + 1) * 512], start=True, stop=True),
                     reads=[Bone1, Bgrow], writes=[Bpm[i]])
                P.op("dve", lambda E, i=i: E.tensor_copy(gt[:, i * 512:(i + 1) * 512], pm[i][:, :]), reads=[Bpm[i]], writes=[Bgt])
            cnt = {"x": 0, "w": 0, "pm": 0, "p2": 0, "sa": 0, "sb": 0, "u": 0, "r": 0, "cs": 0, "pT": 0}

            def nxt(k, n):
                v = cnt[k] % n
                cnt[k] += 1
                return v

            for (g0, gn) in blocks(T, SG):
                for (t0, tn) in blocks(gn, 128):
                    b = nxt("x", 2)
                    P.op("sp", lambda E, b=b, t0=t0, tn=tn, g0=g0: E.dma_start(out=xt[b][:tn, :], in_=xsrc[g0 + t0:g0 + t0 + tn, :]),
                         writes=[Bxt[b]], key=f"x{b}")
                    P.op("act", lambda E, b=b, tn=tn: E.activation(junk[:tn, :], xt[b][:tn, :], AF.Square, accum_out=ss[b][:tn, 0:1]),
                         reads=[Bxt[b]], writes=[Bjunk, Bss[b]])
                    P.op("dve", lambda E, b=b, tn=tn: E.tensor_scalar(ss[b][:tn, 1:2], ss[b][:tn, 0:1], 1.0 / D, EPS, ALU.mult, ALU.add),
                         reads=[Bss[b]], writes=[Bss[b]])
                    P.op("act", lambda E, b=b, tn=tn: E.activation(ss[b][:tn, 1:2], ss[b][:tn, 1:2], AF.Sqrt), reads=[Bss[b]], writes=[Bss[b]])
                    P.op("dve", lambda E, b=b, tn=tn: E.reciprocal(ss[b][:tn, 1:2], ss[b][:tn, 1:2]), reads=[Bss[b]], writes=[Bss[b]])
                    P.op("dve", lambda E, b=b, tn=tn: E.tensor_scalar(xt[b][:tn, :], xt[b][:tn, :], ss[b][:tn, 1:2], None, ALU.mult),
                         reads=[Bxt[b], Bss[b]], writes=[Bxt[b]])
                    P.op("dve", lambda E, b=b, tn=tn: E.tensor_tensor(hb[b][:tn, :], xt[b][:tn, :], gt[:tn, :], ALU.mult),
                         reads=[Bxt[b], Bgt], writes=[Bhb[b]])
                    for half in range(2):
                        pb = nxt("pT", 2)
                        for cc in range(8):
                            ch = half * 8 + cc
                            P.op("pe", lambda E, b=b, pb=pb, cc=cc, ch=ch, tn=tn: E.transpose(pT[pb][:, cc * 128:cc * 128 + tn], hb[b][:tn, ch * 128:(ch + 1) * 128], ident[:tn, :tn]),
                                 reads=[Bhb[b], Bid], writes=[BpT[pb]])
                        src = lambda pb=pb, tn=tn: pT[pb][:, :].rearrange("p (c t) -> p c t", t=128)[:, :, :tn]
                        dst = lambda half=half, t0=t0, tn=tn: hT[:, half * 8:half * 8 + 8, t0:t0 + tn]
                        if False:
                            P.op("act", lambda E, src=src, dst=dst: E.activation(dst(), src(), AF.Copy), reads=[BpT[pb]], writes=[BhT])
                        else:
                            P.op("dve", lambda E, src=src, dst=dst: E.tensor_copy(dst(), src()), reads=[BpT[pb]], writes=[BhT])
                for (kind, col0, ncols) in chunks:
                    if variant and kind not in variant.split(","):
                        continue
                    wbi = nxt("w", 2)
                    for hf in range(2):
                        P.op("sp", lambda E, col0=col0, ncols=ncols, hf=hf: E.dma_start(out=wf[:, hf * 8:hf * 8 + 8, :ncols], in_=w_l[:, hf * 8:hf * 8 + 8, col0:col0 + ncols]),
                             writes=[Bwf], key="wf")
                    P.op("dve", lambda E, wbi=wbi, ncols=ncols: E.tensor_copy(wb[wbi][:, :, :ncols], wf[:, :, :ncols]), reads=[Bwf], writes=[Bwb[wbi]])
                    if kind == "kr":
                        P.op("dve", lambda E: E.tensor_copy(wsw[:, :, 0:32], wf[:, :, 32:64]), reads=[Bwf], writes=[Bwsw])
                        P.op("dve", lambda E: E.tensor_copy(wsw[:, :, 32:64], wf[:, :, 0:32]), reads=[Bwf], writes=[Bwsw])
                    for (q0, qn) in blocks(gn, 512):
                        tg = g0 + q0
                        if kind == "vd":
                            for (t0, tn) in blocks(qn, 128):
                                pb = nxt("pm", 4)
                                for ch in range(16):
                                    P.op("pe", lambda E, pb=pb, ch=ch, tn=tn, a=q0 + t0, wbi=wbi: E.matmul(pm[pb][:tn, :512], hT[:, ch, a:a + tn], wb[wbi][:, ch, :512], start=(ch == 0), stop=(ch == 15)),
                                         reads=[BhT, Bwb[wbi]], writes=[Bpm[pb]])
                                sa = nxt("sa", 3)
                                P.op("act", lambda E, pb=pb, sa=sa, tn=tn: E.activation(stA[sa][:tn, :], pm[pb][:tn, :], AF.Copy), reads=[Bpm[pb]], writes=[BstA[sa]])
                                P.op("sp", lambda E, sa=sa, tn=tn, a=tg + t0: E.dma_start(out=c.Vd[a:a + tn, :], in_=stA[sa][:tn, :]), reads=[BstA[sa]], key=f"sa{sa}")
                            continue
                        if kind == "kr":
                            cb = nxt("cs", 2)
                            P.op("sp", lambda E, cb=cb, qn=qn, tg=tg: E.dma_start(out=cst[cb][:, :qn], in_=cos2[si][:, tg:tg + qn]), writes=[Bcst[cb]], key=f"cs{cb}")
                            P.op("sp", lambda E, cb=cb, qn=qn, tg=tg: E.dma_start(out=snt[cb][:, :qn], in_=sin2[si][:, tg:tg + qn]), writes=[Bcst[cb]], key=f"sn{cb}")
                            pu, pv = nxt("pm", 4), nxt("pm", 4)
                            for ch in range(16):
                                P.op("pe", lambda E, pu=pu, ch=ch, qn=qn, q0=q0, wbi=wbi: E.matmul(pm[pu][:64, :qn], wb[wbi][:, ch, 0:64], hT[:, ch, q0:q0 + qn], start=(ch == 0), stop=(ch == 15)),
                                     reads=[BhT, Bwb[wbi]], writes=[Bpm[pu]])
                            for ch in range(16):
                                P.op("pe", lambda E, pv=pv, ch=ch, qn=qn, q0=q0: E.matmul(pm[pv][:64, :qn], wsw[:, ch, 0:64], hT[:, ch, q0:q0 + qn], start=(ch == 0), stop=(ch == 15)),
                                     reads=[BhT, Bwsw], writes=[Bpm[pv]])
                            rb_ = nxt("r", 2)
                            sa = nxt("sa", 3)
                            P.op("dve", lambda E, pu=pu, rb_=rb_, cb=cb, qn=qn: E.tensor_tensor(r1[rb_][:, :qn], pm[pu][:64, :qn], cst[cb][:, :qn], ALU.mult),
                                 reads=[Bpm[pu], Bcst[cb]], writes=[Br[rb_]])
                            P.op("dve", lambda E, pv=pv, rb_=rb_, cb=cb, qn=qn: E.tensor_tensor(r2[rb_][:, :qn], pm[pv][:64, :qn], snt[cb][:, :qn], ALU.mult),
                                 reads=[Bpm[pv], Bcst[cb]], writes=[Br[rb_]])
                            P.op("dve", lambda E, rb_=rb_, sa=sa, qn=qn: E.tensor_tensor(stA[sa][:64, :qn], r1[rb_][:, :qn], r2[rb_][:, :qn], ALU.add),
                                 reads=[Br[rb_]], writes=[BstA[sa]])
                            P.op("sp", lambda E, sa=sa, qn=qn, tg=tg: E.dma_start(out=c.KrT[:, tg:tg + qn], in_=stA[sa][:64, :qn]), reads=[BstA[sa]], key=f"sa{sa}")
                            continue
                        for (m0, mn) in blocks(ncols, 128):
                            pb = nxt("pm", 4)
                            for ch in range(16):
                                P.op("pe", lambda E, pb=pb, ch=ch, qn=qn, q0=q0, m0=m0, mn=mn, wbi=wbi: E.matmul(pm[pb][:mn, :qn], wb[wbi][:, ch, m0:m0 + mn], hT[:, ch, q0:q0 + qn], start=(ch == 0), stop=(ch == 15)),
                                     reads=[BhT, Bwb[wbi]], writes=[Bpm[pb]])
                            gi = (col0 + m0)
                            if kind == "uf":
                                ub = nxt("u", 2)
                                g = m0 // 128
                                P.op("act", lambda E, pb=pb, ub=ub, qn=qn: E.activation(uT[ub][:, :qn], pm[pb][:, :qn], AF.Copy), reads=[Bpm[pb]], writes=[BuT[ub]])
                                for (t0, tn) in blocks(qn, 128):
                                    p2b = nxt("p2", 2)
                                    P.op("pe", lambda E, p2b=p2b, ub=ub, t0=t0, tn=tn: E.matmul(p2[p2b][:tn, :256], uT[ub][:, t0:t0 + tn], csb[:, :], start=True, stop=True),
                                         reads=[BuT[ub], Bcs], writes=[Bp2[p2b]])
                                    sb_ = nxt("sb", 3)
                                    P.op("dve", lambda E, p2b=p2b, sb_=sb_, tn=tn: E.tensor_copy(stB[sb_][:tn, :256], p2[p2b][:tn, :256]), reads=[Bp2[p2b]], writes=[BstB[sb_]])
                                    P.op("sp", lambda E, sb_=sb_, tn=tn, a=tg + t0, g=g: E.dma_start(out=c.AB[a:a + tn, g, :], in_=stB[sb_][:tn, :256]), reads=[BstB[sb_]], key=f"sb{sb_}")
                            elif kind in ("cq", "ckv"):
                                ci = (gi - (C_CQ if kind == "cq" else C_CKV)) // 128
                                gv = gq if kind == "cq" else gk
                                dg, dsq = (c.cqg, c.cqs) if kind == "cq" else (c.ckg, c.cks)
                                sa, sb_ = nxt("sa", 3), nxt("sb", 3)
                                P.op("dve", lambda E, pb=pb, sa=sa, qn=qn, gv=gv, ci=ci: E.tensor_scalar(stA[sa][:, :qn], pm[pb][:, :qn], gv[:, ci:ci + 1], None, ALU.mult),
                                     reads=[Bpm[pb], Bgq], writes=[BstA[sa]])
                                P.op("act", lambda E, pb=pb, sb_=sb_, qn=qn: E.activation(stB[sb_][:, :qn], pm[pb][:, :qn], AF.Square), reads=[Bpm[pb]], writes=[BstB[sb_]])
                                P.op("sp", lambda E, sa=sa, qn=qn, tg=tg, dg=dg, ci=ci: E.dma_start(out=dg[ci, :, tg:tg + qn], in_=stA[sa][:, :qn]), reads=[BstA[sa]], key=f"sa{sa}")
                                P.op("sp", lambda E, sb_=sb_, qn=qn, tg=tg, dsq=dsq, ci=ci: E.dma_start(out=dsq[ci, :, tg:tg + qn], in_=stB[sb_][:, :qn]), reads=[BstB[sb_]], key=f"sb{sb_}")
                            elif kind in ("qd", "kd"):
                                h = m0 // 128
                                dd = c.qdT if kind == "qd" else c.kdT
                                sa = nxt("sa", 3)
                                P.op("act", lambda E, pb=pb, sa=sa, qn=qn: E.activation(stA[sa][:, :qn], pm[pb][:, :qn], AF.Copy), reads=[Bpm[pb]], writes=[BstA[sa]])
                                P.op("sp", lambda E, sa=sa, qn=qn, tg=tg, dd=dd, h=h: E.dma_start(out=dd[2 * h, :, tg:tg + qn], in_=stA[sa][0:64, :qn]), reads=[BstA[sa]], key=f"sa{sa}")
                                P.op("sp", lambda E, sa=sa, qn=qn, tg=tg, dd=dd, h=h: E.dma_start(out=dd[2 * h + 1, :, tg:tg + qn], in_=stA[sa][64:128, :qn]), reads=[BstA[sa]], key=f"sa{sa}")
                            elif kind == "gate":
                                ci = (gi - C_G) // 128
                                sb_ = nxt("sb", 3)
                                P.op("act", lambda E, pb=pb, sb_=sb_, qn=qn: E.activation(stB[sb_][:, :qn], pm[pb][:, :qn], AF.Silu), reads=[Bpm[pb]], writes=[BstB[sb_]])
                                P.op("sp", lambda E, sb_=sb_, qn=qn, tg=tg, ci=ci: E.dma_start(out=c.gT[ci, :, tg:tg + qn], in_=stB[sb_][:, :qn]), reads=[BstB[sb_]], key=f"sb{sb_}")
            if debug:
                dh = dsc(f"dbg_hT{l}{si}", [128, 128])
                dw = dsc(f"dbg_wb{l}{si}", [128, 128])
                dp = dsc(f"dbg_hb{l}{si}", [128, 128])
                P.op("sp", lambda E: E.dma_start(out=dh[:, :], in_=hT[:, 0, 0:128]), reads=[BhT], key="dbg0")
                P.op("sp", lambda E: E.dma_start(out=dw[:, :], in_=wb[0][:, 0, 0:128]), reads=[Bwb[0]], key="dbg1")
                P.op("sp", lambda E: E.dma_start(out=dp[:, :], in_=hb[0][:, 0:128]), reads=[Bhb[0]], key="dbg2")
            P.emit(nc, f"ip{l}{si}")


    def rstd_ops(P, src_ap, dst_ap, mult, eps, Bsrc, Bdst):
        P.op("dve", lambda E: E.tensor_scalar(dst_ap(), src_ap(), mult, eps, ALU.mult, ALU.add), reads=[Bsrc], writes=[Bdst])
        P.op("act", lambda E: E.activation(dst_ap(), dst_ap(), AF.Sqrt), reads=[Bdst], writes=[Bdst])
        P.op("dve", lambda E: E.reciprocal(dst_ap(), dst_ap()), reads=[Bdst], writes=[Bdst])

    def phase_q(l, si):
        c = S[si]
        T = c.T
        wq_l = w_uq[l].rearrange("(c p) n -> p c n", p=128)
        with ExitStack() as es:
            P = Prog()
            wf = sb(es, "wf", [128, 6, 1536], F32)
            wq = sb(es, "wq", [128, 6, 1536], BF16)
            wsw = sb(es, "wsw", [128, 6, 8, 64], BF16)
            ones = sb(es, "ones", [128, 128], BF16)
            cg = [sb(es, f"cg{i}", [128, 6, 512], BF16) for i in range(2)]
            cq2 = [sb(es, f"cq2{i}", [128, 6, 512], BF16) for i in range(2)]
            cst = [sb(es, f"cst{i}", [64, 512], F32) for i in range(2)]
            snt = [sb(es, f"snt{i}", [64, 512], F32) for i in range(2)]
            rst = [sb(es, f"rst{i}", [128, 512], F32) for i in range(2)]
            stn = [sb(es, f"stn{i}", [128, 512], BF16) for i in range(3)]
            r1 = [sb(es, f"r1{i}", [64, 512], F32) for i in range(2)]
            r2 = [sb(es, f"r2{i}", [64, 512], F32) for i in range(2)]
            pss = ps(es, "pss")
            pn = [ps(es, f"pn{i}") for i in range(2)]
            pu = [ps(es, f"pu{i}") for i in range(2)]
            pv = [ps(es, f"pv{i}") for i in range(2)]
            Bwf, Bwq, Bones = Buf("wf"), Buf("wq"), Buf("ones")
            Bcg = [Buf("cg0"), Buf("cg1")]
            Bcs = [Buf("cs0"), Buf("cs1")]
            Brst = [Buf("rst0"), Buf("rst1")]
            Bstn = [Buf(f"stn{i}") for i in range(3)]
            Br = [Buf("r0"), Buf("r1")]
            Bpss = PB("pss")
            Bpn, Bpu, Bpv = [PB("pn0"), PB("pn1")], [PB("pu0"), PB("pu1")], [PB("pv0"), PB("pv1")]
            P.op("sp", lambda E: E.dma_start(out=wf[:], in_=wq_l), writes=[Bwf], key="wf")
            P.op("dve", lambda E: E.tensor_copy(wq[:], wf[:]), reads=[Bwf], writes=[Bwq])
            wfv = wf[:].rearrange("p c (h f) -> p c h f", f=192)
            for cc in range(6):
                P.op("dve", lambda E, cc=cc: E.tensor_copy(wsw[:, cc, :, 0:32], wfv[:, cc, :, 160:192]), reads=[Bwf], writes=[Bwq])
                P.op("dve", lambda E, cc=cc: E.tensor_copy(wsw[:, cc, :, 32:64], wfv[:, cc, :, 128:160]), reads=[Bwf], writes=[Bwq])
            P.op("dve", lambda E: E.memset(ones[:], 1.0), writes=[Bones])
            cnt = {}

            def nxt(k, n):
                v = cnt.get(k, 0)
                cnt[k] = v + 1
                return v % n

            for (q0, qn) in blocks(T, 512):
                b = nxt("b", 2)
                P.op("sp", lambda E, b=b, q0=q0, qn=qn: E.dma_start(out=cg[b][:, :, :qn], in_=c.cqg[:, :, q0:q0 + qn].rearrange("c p t -> p c t")), writes=[Bcg[b]], key=f"cg{b}")
                P.op("sp", lambda E, b=b, q0=q0, qn=qn: E.dma_start(out=cq2[b][:, :, :qn], in_=c.cqs[:, :, q0:q0 + qn].rearrange("c p t -> p c t")), writes=[Bcg[b]], key=f"cs{b}")
                P.op("sp", lambda E, b=b, q0=q0, qn=qn: E.dma_start(out=cst[b][:, :qn], in_=cos2[si][:, q0:q0 + qn]), writes=[Bcs[b]], key=f"co{b}")
                P.op("sp", lambda E, b=b, q0=q0, qn=qn: E.dma_start(out=snt[b][:, :qn], in_=sin2[si][:, q0:q0 + qn]), writes=[Bcs[b]], key=f"si{b}")
                for cc in range(6):
                    P.op("pe", lambda E, b=b, cc=cc, qn=qn: E.matmul(pss[:, :qn], ones[:, :], cq2[b][:, cc, :qn], start=(cc == 0), stop=(cc == 5)),
                         reads=[Bones, Bcg[b]], writes=[Bpss])
                rstd_ops(P, lambda qn=qn: pss[:, :qn], lambda b=b, qn=qn: rst[b][:, :qn], 1.0 / 768, EPS, Bpss, Brst[b])
                for h in range(8):
                    i = nxt("pn", 2)
                    for cc in range(6):
                        P.op("pe", lambda E, i=i, b=b, cc=cc, qn=qn, h=h: E.matmul(pn[i][:, :qn], wq[:, cc, h * 192:h * 192 + 128], cg[b][:, cc, :qn], start=(cc == 0), stop=(cc == 5)),
                             reads=[Bwq, Bcg[b]], writes=[Bpn[i]])
                    for cc in range(6):
                        P.op("pe", lambda E, i=i, b=b, cc=cc, qn=qn, h=h: E.matmul(pu[i][:64, :qn], wq[:, cc, h * 192 + 128:h * 192 + 192], cg[b][:, cc, :qn], start=(cc == 0), stop=(cc == 5)),
                             reads=[Bwq, Bcg[b]], writes=[Bpu[i]])
                    for cc in range(6):
                        P.op("pe", lambda E, i=i, b=b, cc=cc, qn=qn, h=h: E.matmul(pv[i][:64, :qn], wsw[:, cc, h, :], cg[b][:, cc, :qn], start=(cc == 0), stop=(cc == 5)),
                             reads=[Bwq, Bcg[b]], writes=[Bpv[i]])
                    s1 = nxt("st", 3)
                    P.op("dve", lambda E, i=i, b=b, s1=s1, qn=qn: E.tensor_tensor(stn[s1][:, :qn], pn[i][:, :qn], rst[b][:, :qn], ALU.mult),
                         reads=[Bpn[i], Brst[b]], writes=[Bstn[s1]])
                    P.op("sp", lambda E, s1=s1, h=h, q0=q0, qn=qn: E.dma_start(out=c.QnT[h, :, q0:q0 + qn], in_=stn[s1][:, :qn]), reads=[Bstn[s1]], key=f"st{s1}")
                    rb_ = nxt("r", 2)
                    s2 = nxt("st", 3)
                    P.op("dve", lambda E, i=i, b=b, rb_=rb_, qn=qn: E.tensor_tensor(r1[rb_][:, :qn], pu[i][:64, :qn], cst[b][:, :qn], ALU.mult),
                         reads=[Bpu[i], Bcs[b]], writes=[Br[rb_]])
                    P.op("dve", lambda E, i=i, b=b, rb_=rb_, qn=qn: E.tensor_tensor(r2[rb_][:, :qn], pv[i][:64, :qn], snt[b][:, :qn], ALU.mult),
                         reads=[Bpv[i], Bcs[b]], writes=[Br[rb_]])
                    P.op("dve", lambda E, rb_=rb_, qn=qn: E.tensor_tensor(r1[rb_][:, :qn], r1[rb_][:, :qn], r2[rb_][:, :qn], ALU.add), reads=[Br[rb_]], writes=[Br[rb_]])
                    P.op("dve", lambda E, rb_=rb_, b=b, s2=s2, qn=qn: E.tensor_tensor(stn[s2][:64, :qn], r1[rb_][:, :qn], rst[b][:64, :qn], ALU.mult),
                         reads=[Br[rb_], Brst[b]], writes=[Bstn[s2]])
                    P.op("sp", lambda E, s2=s2, h=h, q0=q0, qn=qn: E.dma_start(out=c.QrT[h, :, q0:q0 + qn], in_=stn[s2][:64, :qn]), reads=[Bstn[s2]], key=f"st{s2}")
            P.emit(nc, f"q{l}{si}")

    def phase_kv(l, si):
        c = S[si]
        T = c.T
        wk_l = w_ukv[l].rearrange("(c p) n -> p c n", p=128)
        with ExitStack() as es:
            P = Prog()
            wf = sb(es, "wf", [128, 4, 2048], F32)
            wk = sb(es, "wk", [128, 4, 1024], BF16)
            wv = sb(es, "wv", [128, 4, 1024], BF16)
            ones = sb(es, "ones", [128, 128], BF16)
            cg = [sb(es, f"cg{i}", [128, 4, 512], BF16) for i in range(2)]
            cq2 = [sb(es, f"cq2{i}", [128, 4, 512], BF16) for i in range(2)]
            rst = [sb(es, f"rst{i}", [128, 512], F32) for i in range(2)]
            rcol = [sb(es, f"rcol{i}", [128, 1], F32) for i in range(2)]
            stn = [sb(es, f"stn{i}", [128, 512], BF16) for i in range(3)]
            pss = ps(es, "pss")
            pc = ps(es, "pc")
            pk = [ps(es, f"pk{i}") for i in range(3)]
            pvv = [ps(es, f"pvv{i}") for i in range(3)]
            Bwf, Bwk, Bones = Buf("wf"), Buf("wk"), Buf("ones")
            Bcg = [Buf("cg0"), Buf("cg1")]
            Brst = [Buf("rst0"), Buf("rst1")]
            Brc = [Buf("rc0"), Buf("rc1")]
            Bstn = [Buf(f"stn{i}") for i in range(3)]
            Bpss, Bpc = PB("pss"), PB("pc")
            Bpk = [PB(f"pk{i}") for i in range(3)]
            Bpvv = [PB(f"pvv{i}") for i in range(3)]
            P.op("sp", lambda E: E.dma_start(out=wf[:], in_=wk_l), writes=[Bwf], key="wf")
            wfv = wf[:].rearrange("p c (h two f) -> p c h two f", two=2, f=128)
            for cc in range(4):
                P.op("dve", lambda E, cc=cc: E.tensor_copy(wk[:, cc, :].rearrange("p (h f) -> p h f", f=128), wfv[:, cc, :, 0, :]), reads=[Bwf], writes=[Bwk])
                P.op("dve", lambda E, cc=cc: E.tensor_copy(wv[:, cc, :].rearrange("p (h f) -> p h f", f=128), wfv[:, cc, :, 1, :]), reads=[Bwf], writes=[Bwk])
            P.op("dve", lambda E: E.memset(ones[:], 1.0), writes=[Bones])
            cnt = {}

            def nxt(k, n):
                v = cnt.get(k, 0)
                cnt[k] = v + 1
                return v % n

            for (q0, qn) in blocks(T, 512):
                b = nxt("b", 2)
                P.op("sp", lambda E, b=b, q0=q0, qn=qn: E.dma_start(out=cg[b][:, :, :qn], in_=c.ckg[:, :, q0:q0 + qn].rearrange("c p t -> p c t")), writes=[Bcg[b]], key=f"cg{b}")
                P.op("sp", lambda E, b=b, q0=q0, qn=qn: E.dma_start(out=cq2[b][:, :, :qn], in_=c.cks[:, :, q0:q0 + qn].rearrange("c p t -> p c t")), writes=[Bcg[b]], key=f"cs{b}")
                for cc in range(4):
                    P.op("pe", lambda E, b=b, cc=cc, qn=qn: E.matmul(pss[:, :qn], ones[:, :], cq2[b][:, cc, :qn], start=(cc == 0), stop=(cc == 3)),
                         reads=[Bones, Bcg[b]], writes=[Bpss])
                rstd_ops(P, lambda qn=qn: pss[:, :qn], lambda b=b, qn=qn: rst[b][:, :qn], 1.0 / 512, EPS, Bpss, Brst[b])
                for h in range(8):
                    i = nxt("pk", 3)
                    for cc in range(4):
                        P.op("pe", lambda E, i=i, b=b, cc=cc, qn=qn, h=h: E.matmul(pk[i][:, :qn], wk[:, cc, h * 128:(h + 1) * 128], cg[b][:, cc, :qn], start=(cc == 0), stop=(cc == 3)),
                             reads=[Bwk, Bcg[b]], writes=[Bpk[i]])
                    s1 = nxt("st", 3)
                    P.op("dve", lambda E, i=i, b=b, s1=s1, qn=qn: E.tensor_tensor(stn[s1][:, :qn], pk[i][:, :qn], rst[b][:, :qn], ALU.mult),
                         reads=[Bpk[i], Brst[b]], writes=[Bstn[s1]])
                    P.op("sp", lambda E, s1=s1, h=h, q0=q0, qn=qn: E.dma_start(out=c.KnT[h, :, q0:q0 + qn], in_=stn[s1][:, :qn]), reads=[Bstn[s1]], key=f"st{s1}")
                for (t0, tn) in blocks(qn, 128):
                    rc = nxt("rc", 2)
                    for cc in range(4):
                        P.op("pe", lambda E, b=b, cc=cc, t0=t0, tn=tn: E.matmul(pc[:tn, 0:1], cq2[b][:, cc, t0:t0 + tn], ones[:, 0:1], start=(cc == 0), stop=(cc == 3)),
                             reads=[Bones, Bcg[b]], writes=[Bpc])
                    rstd_ops(P, lambda tn=tn: pc[:tn, 0:1], lambda rc=rc, tn=tn: rcol[rc][:tn, 0:1], 1.0 / 512, EPS, Bpc, Brc[rc])
                    for hh in range(2):
                        i = nxt("pvv", 3)
                        for cc in range(4):
                            P.op("pe", lambda E, i=i, b=b, cc=cc, t0=t0, tn=tn, hh=hh: E.matmul(pvv[i][:tn, :512], cg[b][:, cc, t0:t0 + tn], wv[:, cc, hh * 512:(hh + 1) * 512], start=(cc == 0), stop=(cc == 3)),
                                 reads=[Bwk, Bcg[b]], writes=[Bpvv[i]])
                        s1 = nxt("st", 3)
                        P.op("dve", lambda E, i=i, rc=rc, s1=s1, tn=tn: E.tensor_scalar(stn[s1][:tn, :], pvv[i][:tn, :], rcol[rc][:tn, 0:1], None, ALU.mult),
                             reads=[Bpvv[i], Brc[rc]], writes=[Bstn[s1]])
                        P.op("sp", lambda E, s1=s1, hh=hh, a=q0 + t0, tn=tn: E.dma_start(out=c.V[a:a + tn, hh * 512:(hh + 1) * 512], in_=stn[s1][:tn, :]), reads=[Bstn[s1]], key=f"st{s1}")
            P.emit(nc, f"kv{l}{si}")

    G.phase_q, G.phase_kv = phase_q, phase_kv

    MLA_SCALE = 1.0 / math.sqrt(192.0)

    def phase_mla(l, si):
        c = S[si]
        T = c.T
        kbs = blocks(T, 128)
        nkb = len(kbs)
        nfull = T // 128
        with ExitStack() as es:
            P = Prog()
            ones = sb(es, "ones", [128, 128], BF16)
            Kr = sb(es, "Kr", [64, T], BF16)
            Kn_ = [sb(es, f"Kn{i}", [128, T], BF16) for i in range(2)]
            Qn_ = [sb(es, f"Qn{i}", [128, T], BF16) for i in range(2)]
            Qr_ = [sb(es, f"Qr{i}", [64, T], BF16) for i in range(2)]
            Vh_ = [sb(es, f"Vh{i}", [128, nkb, 128], BF16) for i in range(2)]
            gt = [sb(es, f"gt{i}", [128, 512], BF16) for i in range(2)]
            PT = [sb(es, f"PT{i}", [128, 512], BF16) for i in range(4)]
            acc = [sb(es, f"acc{i}", [128, 512], F32) for i in range(2)]
            accb = sb(es, "accb", [128, 512], BF16)
            Bacc = [Buf("acc0"), Buf("acc1")]
            Baccb = Buf("accb")
            R = [sb(es, f"R{i}", [128, 512], F32) for i in range(2)]
            Y = [sb(es, f"Y{i}", [128, 512], F32) for i in range(2)]
            Yb = [sb(es, f"Yb{i}", [128, 512], BF16) for i in range(2)]
            pS = [ps(es, f"pS{i}") for i in range(4)]
            pO = [ps(es, f"pO{i}") for i in range(2)]
            pL = [ps(es, f"pL{i}") for i in range(2)]
            Bones, BKr = Buf("ones"), Buf("Kr")
            BK_, BQ_, BV_ = [Buf("K0"), Buf("K1")], [Buf("Q0"), Buf("Q1")], [Buf("V0"), Buf("V1")]
            Bgt = [Buf("gt0"), Buf("gt1")]
            BPT = [Buf(f"PT{i}") for i in range(4)]
            BR = [Buf("R0"), Buf("R1")]
            BY = [Buf("Y0"), Buf("Y1")]
            BYb = [Buf("Yb0"), Buf("Yb1")]
            BpS = [PB(f"pS{i}") for i in range(4)]
            BpO = [PB("pO0"), PB("pO1")]
            BpL = [PB("pL0"), PB("pL1")]
            P.op("dve", lambda E: E.memset(ones[:], 1.0), writes=[Bones])
            P.op("sp", lambda E: E.dma_start(out=Kr[:, :], in_=c.KrT[:, :]), writes=[BKr], key="kr")
            cnt = {}

            def nxt(k, n):
                v = cnt.get(k, 0)
                cnt[k] = v + 1
                return v % n

            def load_head(h):
                s = h % 2
                P.op("sp", lambda E: E.dma_start(out=Kn_[s][:, :], in_=c.KnT[h, :, :]), writes=[BK_[s]], key=f"kn{s}")
                P.op("sp", lambda E: E.dma_start(out=Qn_[s][:, :], in_=c.QnT[h, :, :]), writes=[BQ_[s]], key=f"qn{s}")
                P.op("sp", lambda E: E.dma_start(out=Qr_[s][:, :], in_=c.QrT[h, :, :]), writes=[BQ_[s]], key=f"qr{s}")
                for (b0, bn) in blocks(nfull, 8):
                    P.op("sp", lambda E, b0=b0, bn=bn: E.dma_start(out=Vh_[s][:, b0:b0 + bn, :], in_=c.V[b0 * 128:(b0 + bn) * 128, h * 128:(h + 1) * 128].rearrange("(k p) d -> p k d", p=128)),
                         writes=[BV_[s]], key=f"v{s}")
                if T % 128:
                    P.op("sp", lambda E: E.dma_start(out=Vh_[s][:T % 128, nfull, :], in_=c.V[nfull * 128:T, h * 128:(h + 1) * 128]), writes=[BV_[s]], key=f"w{s}")

            load_head(0)
            for h in range(8):
                if h + 1 < 8:
                    load_head(h + 1)
                Kn, Qn, Qr, Vh = Kn_[h % 2], Qn_[h % 2], Qr_[h % 2], Vh_[h % 2]
                BK, BQ, BV = BK_[h % 2], BQ_[h % 2], BV_[h % 2]
                for (q0, qn) in blocks(T, 512):
                    gb = nxt("g", 2)
                    ob = nxt("o", 2)
                    P.op("sp", lambda E, gb=gb, h=h, q0=q0, qn=qn: E.dma_start(out=gt[gb][:, :qn], in_=c.gT[4 + h, :, q0:q0 + qn]), writes=[Bgt[gb]], key=f"g{gb}")

                    def qk(kb, q0=q0, qn=qn, Kn=Kn, Qn=Qn, Qr=Qr, BK=BK, BQ=BQ):
                        k0, nk = kbs[kb]
                        sbk = kb % 4
                        P.op("pe", lambda E: E.matmul(pS[sbk][:nk, :qn], Kn[:, k0:k0 + nk], Qn[:, q0:q0 + qn], start=True, stop=False),
                             reads=[BK, BQ], writes=[BpS[sbk]])
                        P.op("pe", lambda E: E.matmul(pS[sbk][:nk, :qn], Kr[:, k0:k0 + nk], Qr[:, q0:q0 + qn], start=False, stop=True),
                             reads=[BKr, BQ], writes=[BpS[sbk]])

                    P.op("dve", lambda E, ob=ob, qn=qn: E.memset(acc[ob][:, :qn], 0.0), writes=[Bacc[ob]])
                    for kk in range(min(3, nkb)):
                        qk(kk)
                    for kb in range(nkb):
                        k0, nk = kbs[kb]
                        if kb + 3 < nkb:
                            qk(kb + 3)
                        sbk = kb % 4
                        pb = nxt("pt", 4)
                        P.op("act", lambda E, sbk=sbk, pb=pb, nk=nk, qn=qn: E.activation(PT[pb][:nk, :qn], pS[sbk][:nk, :qn], AF.Exp, scale=MLA_SCALE),
                             reads=[BpS[sbk]], writes=[BPT[pb]])
                        P.op("pe", lambda E, pb=pb, ob=ob, kb=kb, nk=nk, qn=qn, Vh=Vh: E.matmul(pO[ob][:, :qn], Vh[:nk, kb, :], PT[pb][:nk, :qn], start=(kb == 0), stop=(kb == nkb - 1)),
                             reads=[BV, BPT[pb]], writes=[BpO[ob]])
                        if kb % 2 == 0:
                            P.op("pe", lambda E, pb=pb, ob=ob, kb=kb, nk=nk, qn=qn: E.matmul(pL[ob][:, :qn], ones[:nk, :], PT[pb][:nk, :qn], start=(kb == 0), stop=False),
                                 reads=[Bones, BPT[pb]], writes=[BpL[ob]])
                        else:
                            P.op("dve", lambda E, pb=pb, ob=ob, nk=nk, qn=qn: E.tensor_tensor(acc[ob][:nk, :qn], acc[ob][:nk, :qn], PT[pb][:nk, :qn], ALU.add),
                                 reads=[BPT[pb], Bacc[ob]], writes=[Bacc[ob]])
                    P.op("dve", lambda E, ob=ob, qn=qn: E.tensor_copy(accb[:, :qn], acc[ob][:, :qn]), reads=[Bacc[ob]], writes=[Baccb])
                    P.op("pe", lambda E, ob=ob, qn=qn: E.matmul(pL[ob][:, :qn], ones[:, :], accb[:, :qn], start=False, stop=True),
                         reads=[Bones, Baccb], writes=[BpL[ob]])
                    P.op("dve", lambda E, ob=ob, qn=qn: E.reciprocal(R[ob][:, :qn], pL[ob][:, :qn]), reads=[BpL[ob]], writes=[BR[ob]])
                    P.op("dve", lambda E, ob=ob, qn=qn: E.tensor_tensor(Y[ob][:, :qn], pO[ob][:, :qn], R[ob][:, :qn], ALU.mult), reads=[BpO[ob], BR[ob]], writes=[BY[ob]])
                    P.op("dve", lambda E, ob=ob, gb=gb, qn=qn: E.tensor_tensor(Yb[ob][:, :qn], Y[ob][:, :qn], gt[gb][:, :qn], ALU.mult), reads=[BY[ob], Bgt[gb]], writes=[BYb[ob]])
                    P.op("sp", lambda E, ob=ob, h=h, q0=q0, qn=qn: E.dma_start(out=c.yT[4 + h, :, q0:q0 + qn], in_=Yb[ob][:, :qn]), reads=[BYb[ob]], key=f"y{ob}")
            P.emit(nc, f"mla{l}{si}")

    def phase_diff(l, si):
        c = S[si]
        T = c.T
        kbs = blocks(T, 128)
        nkb = len(kbs)
        nfull = T // 128
        lam_init = 0.8 - 0.6 * math.exp(-0.3 * l)
        with ExitStack() as es:
            P = Prog()
            ones = sb(es, "ones", [128, 128], BF16)
            onesf = sb(es, "onesf", [128, 128], F32)
            identf = sb(es, "identf", [128, 128], F32)
            ident = sb(es, "ident", [128, 128], BF16)
            lv = sb(es, "lv", [64, 4], F32)
            lp = sb(es, "lp", [64, 2], BF16)
            ee = sb(es, "ee", [128, 2], F32)
            neglam = sb(es, "neglam", [128, 1], F32)
            gd = sb(es, "gd", [128, 1], F32)
            Kd_ = [[sb(es, f"Kd{s}{a}", [64, T], BF16) for a in range(2)] for s in range(2)]
            Qd_ = [[sb(es, f"Qd{s}{a}", [64, T], BF16) for a in range(2)] for s in range(2)]
            Vh_ = [sb(es, f"Vh{s}", [128, nkb, 128], BF16) for s in range(2)]
            Bt_ = [sb(es, f"Bt{s}", [128, 6, 512], BF16) for s in range(2)]
            cfb_ = [sb(es, f"cfb{s}", [128, 2], BF16) for s in range(2)]
            cf_ = [sb(es, f"cf{s}", [128, 2], F32) for s in range(2)]
            gt = [sb(es, f"gt{i}", [128, 512], BF16) for i in range(2)]
            PT = [sb(es, f"PT{i}", [128, 512], BF16) for i in range(4)]
            acc = [sb(es, f"acc{i}", [128, 512], F32) for i in range(2)]
            accb = [sb(es, f"accb{i}", [128, 512], BF16) for i in range(2)]
            Bacc = [Buf("acc0"), Buf("acc1")]
            Baccb = [Buf("accb0"), Buf("accb1")]
            R = [sb(es, f"R{a}", [128, 512], F32) for a in range(2)]
            On = [sb(es, f"On{a}", [128, 512], F32) for a in range(2)]
            Dm = sb(es, "Dm", [128, 512], F32)
            D2 = sb(es, "D2", [128, 512], BF16)
            rs = sb(es, "rs", [128, 512], F32)
            Yb = [sb(es, f"Yb{i}", [128, 512], BF16) for i in range(2)]
            pS = [ps(es, f"pS{i}") for i in range(3)]
            pO = [ps(es, f"pO{i}") for i in range(2)]
            pL = [ps(es, f"pL{i}") for i in range(2)]
            pX = ps(es, "pX")
            Bones, Bid, Blv, Bee, Bnl, Bgd, Bcf = Buf("ones"), Buf("id"), Buf("lv"), Buf("ee"), Buf("nl"), Buf("gd"), Buf("cf")
            BK_, BQ_, BV_, BBt_, Bcf_ = ([Buf("K0"), Buf("K1")], [Buf("Q0"), Buf("Q1")], [Buf("V0"), Buf("V1")],
                                         [Buf("Bt0"), Buf("Bt1")], [Buf("cf0"), Buf("cf1")])
            Bgt = [Buf("gt0"), Buf("gt1")]
            BPT = [Buf(f"PT{i}") for i in range(4)]
            BR = [Buf("R0"), Buf("R1")]
            BOn = [Buf("On0"), Buf("On1")]
            BDm, BD2, Brs = Buf("Dm"), Buf("D2"), Buf("rs")
            BYb = [Buf("Yb0"), Buf("Yb1")]
            BpS = [PB(f"pS{i}") for i in range(3)]
            BpO = [PB("pO0"), PB("pO1")]
            BpL = [PB("pL0"), PB("pL1")]
            BpX = PB("pX")
            P.op("dve", lambda E: E.memset(ones[:], 1.0), writes=[Bones])
            P.op("dve", lambda E: E.memset(onesf[:], 1.0), writes=[Bones])
            P.op("sp", lambda E: E.dma_start(out=identf[:], in_=ident_d[:, :]), writes=[Bid], key="id")
            P.op("dve", lambda E: E.tensor_copy(ident[:], identf[:]), reads=[Bid], writes=[Bid])
            for i in range(4):
                P.op("sp", lambda E, i=i: E.dma_start(out=lv[:, i:i + 1], in_=lam_in[i][l].rearrange("(p o) -> p o", o=1)), writes=[Blv], key=f"lv{i}")
            P.op("sp", lambda E: E.dma_start(out=gd[:, :], in_=diff_norm[l].rearrange("(p o) -> p o", o=1)), writes=[Bgd], key="gd")
            P.op("dve", lambda E: E.tensor_scalar(gd[:, :], gd[:, :], 1.0 - lam_init, None, ALU.mult), reads=[Bgd], writes=[Bgd])
            P.op("dve", lambda E: E.tensor_tensor(lp[:, 0:1], lv[:, 0:1], lv[:, 1:2], ALU.mult), reads=[Blv], writes=[Blv])
            P.op("dve", lambda E: E.tensor_tensor(lp[:, 1:2], lv[:, 2:3], lv[:, 3:4], ALU.mult), reads=[Blv], writes=[Blv])
            P.op("pe", lambda E: E.matmul(pX[:, 0:2], ones[:64, :], lp[:, 0:2], start=True, stop=True), reads=[Bones, Blv], writes=[BpX])
            P.op("act", lambda E: E.activation(ee[:, :], pX[:, 0:2], AF.Exp), reads=[BpX], writes=[Bee])
            P.op("dve", lambda E: E.tensor_tensor(neglam[:, :], ee[:, 1:2], ee[:, 0:1], ALU.subtract), reads=[Bee], writes=[Bnl])
            P.op("dve", lambda E: E.tensor_scalar(neglam[:, :], neglam[:, :], -lam_init, None, ALU.add), reads=[Bnl], writes=[Bnl])
            cnt = {}

            def nxt(k, n):
                v = cnt.get(k, 0)
                cnt[k] = v + 1
                return v % n

            def load_head(h):
                s = h % 2
                for a in range(2):
                    P.op("sp", lambda E, a=a: E.dma_start(out=Kd_[s][a][:, :], in_=c.kdT[2 * h + a, :, :]), writes=[BK_[s]], key=f"k{s}{a}")
                    P.op("sp", lambda E, a=a: E.dma_start(out=Qd_[s][a][:, :], in_=c.qdT[2 * h + a, :, :]), writes=[BQ_[s]], key=f"q{s}{a}")
                for (b0, bn) in blocks(nfull, 8):
                    P.op("sp", lambda E, b0=b0, bn=bn: E.dma_start(out=Vh_[s][:, b0:b0 + bn, :], in_=c.Vd[b0 * 128:(b0 + bn) * 128, h * 128:(h + 1) * 128].rearrange("(k p) d -> p k d", p=128)),
                         writes=[BV_[s]], key=f"v{s}")
                if T % 128:
                    P.op("sp", lambda E: E.dma_start(out=Vh_[s][:T % 128, nfull, :], in_=c.Vd[nfull * 128:T, h * 128:(h + 1) * 128]), writes=[BV_[s]], key=f"w{s}")
                gflat = Gb[h].rearrange("r w -> (r w)")
                for i in range(6):
                    off = RMAX - (-128 + 128 * i)
                    P.op("sp", lambda E, i=i, off=off: E.dma_start(out=Bt_[s][:, i, :], in_=gflat[off:off + 128 * (WFULL - 1)].rearrange("(p w) -> p w", w=WFULL - 1)[:, 0:512]),
                         writes=[BBt_[s]], key=f"bt{s}")
                P.op("sp", lambda E: E.dma_start(out=cfb_[s][:, 0:1], in_=Gb[h, 0:128, 0:1], allow_slow_non_contiguous=True), writes=[Bcf_[s]], key=f"cfa{s}")
                P.op("sp", lambda E: E.dma_start(out=cfb_[s][:, 1:2], in_=Gb[h, 0:128, WFULL - 1:WFULL], allow_slow_non_contiguous=True), writes=[Bcf_[s]], key=f"cfb{s}")
                P.op("dve", lambda E: E.tensor_scalar(cf_[s][:, :], cfb_[s][:, :], 0.125, None, ALU.mult), reads=[Bcf_[s]], writes=[Bcf_[s]])

            load_head(0)
            for h in range(4):
                if h + 1 < 4:
                    load_head(h + 1)
                Kd, Qd, Vh, Bt, cf = Kd_[h % 2], Qd_[h % 2], Vh_[h % 2], Bt_[h % 2], cf_[h % 2]
                BK, BQ, BV, BBt, Bcf = BK_[h % 2], BQ_[h % 2], BV_[h % 2], BBt_[h % 2], Bcf_[h % 2]
                for (q0, qn) in blocks(T, 512):
                    gb = nxt("g", 2)
                    P.op("sp", lambda E, gb=gb, h=h, q0=q0, qn=qn: E.dma_start(out=gt[gb][:, :qn], in_=c.gT[12 + h, :, q0:q0 + qn]), writes=[Bgt[gb]], key=f"g{gb}")
                    steps = [(kb, a) for kb in range(nkb) for a in range(2)]

                    def klass(kb, q0=q0, qn=qn):
                        k0, nk = kbs[kb]
                        relmin, relmax = k0 - (q0 + qn - 1), k0 + nk - 1 - q0
                        if relmin >= 91:
                            return ("far", 0)
                        if relmax <= -91:
                            return ("far", 1)
                        d = k0 - q0
                        i = (d + 128) // 128
                        assert (d + 128) % 128 == 0 and 0 <= i < 6, (d, i)
                        return ("near", i)

                    def qk(st, q0=q0, qn=qn, Kd=Kd, Qd=Qd, Bt=Bt, BK=BK, BQ=BQ, BBt=BBt):
                        kb, a = steps[st]
                        k0, nk = kbs[kb]
                        sbk = st % 3
                        kl = klass(kb)
                        near = kl[0] == "near" and not DBG.get("nobias")
                        P.op("pe", lambda E: E.matmul(pS[sbk][:nk, :qn], Kd[a][:, k0:k0 + nk], Qd[a][:, q0:q0 + qn], start=True, stop=not near),
                             reads=[BK, BQ], writes=[BpS[sbk]])
                        if near:
                            P.op("pe", lambda E: E.matmul(pS[sbk][:nk, :qn], ident[:nk, :nk], Bt[:nk, kl[1], :qn], start=False, stop=True),
                                 reads=[Bid, BBt], writes=[BpS[sbk]])

                    for a in range(2):
                        P.op("dve", lambda E, a=a, qn=qn: E.memset(acc[a][:, :qn], 0.0), writes=[Bacc[a]])
                    qk(0)
                    if len(steps) > 1:
                        qk(1)
                    for st in range(len(steps)):
                        kb, a = steps[st]
                        k0, nk = kbs[kb]
                        if st + 2 < len(steps):
                            qk(st + 2)
                        sbk = st % 3
                        pb = nxt("pt", 4)
                        kl = klass(kb)
                        if kl[0] == "far":
                            P.op("act", lambda E, sbk=sbk, pb=pb, nk=nk, qn=qn, j=kl[1], cf=cf: E.activation(PT[pb][:nk, :qn], pS[sbk][:nk, :qn], AF.Exp, bias=cf[:nk, j:j + 1], scale=0.125),
                                 reads=[BpS[sbk], Bcf], writes=[BPT[pb]])
                        else:
                            P.op("act", lambda E, sbk=sbk, pb=pb, nk=nk, qn=qn: E.activation(PT[pb][:nk, :qn], pS[sbk][:nk, :qn], AF.Exp, scale=0.125),
                                 reads=[BpS[sbk]], writes=[BPT[pb]])
                        P.op("pe", lambda E, pb=pb, a=a, kb=kb, nk=nk, qn=qn, Vh=Vh: E.matmul(pO[a][:, :qn], Vh[:nk, kb, :], PT[pb][:nk, :qn], start=(kb == 0), stop=(kb == nkb - 1)),
                             reads=[BV, BPT[pb]], writes=[BpO[a]])
                        if kb % 2 == 0:
                            P.op("pe", lambda E, pb=pb, a=a, kb=kb, nk=nk, qn=qn: E.matmul(pL[a][:, :qn], ones[:nk, :], PT[pb][:nk, :qn], start=(kb == 0), stop=False),
                                 reads=[Bones, BPT[pb]], writes=[BpL[a]])
                        else:
                            P.op("dve", lambda E, pb=pb, a=a, nk=nk, qn=qn: E.tensor_tensor(acc[a][:nk, :qn], acc[a][:nk, :qn], PT[pb][:nk, :qn], ALU.add),
                                 reads=[BPT[pb], Bacc[a]], writes=[Bacc[a]])
                    for a in range(2):
                        P.op("dve", lambda E, a=a, qn=qn: E.tensor_copy(accb[a][:, :qn], acc[a][:, :qn]), reads=[Bacc[a]], writes=[Baccb[a]])
                        P.op("pe", lambda E, a=a, qn=qn: E.matmul(pL[a][:, :qn], ones[:, :], accb[a][:, :qn], start=False, stop=True),
                             reads=[Bones, Baccb[a]], writes=[BpL[a]])
                    for a in range(2):
                        P.op("dve", lambda E, a=a, qn=qn: E.reciprocal(R[a][:, :qn], pL[a][:, :qn]), reads=[BpL[a]], writes=[BR[a]])
                        P.op("dve", lambda E, a=a, qn=qn: E.tensor_tensor(On[a][:, :qn], pO[a][:, :qn], R[a][:, :qn], ALU.mult), reads=[BpO[a], BR[a]], writes=[BOn[a]])
                    if debug and h == 3 and q0 == 0:
                        d0 = dsc(f"dbg_On0_{l}{si}", [128, 512], F32)
                        d1 = dsc(f"dbg_On1_{l}{si}", [128, 512], F32)
                        d2 = dsc(f"dbg_nl_{l}{si}", [128, 1], F32)
                        P.op("sp", lambda E, qn=qn: E.dma_start(out=d0[:, :qn], in_=On[0][:, :qn]), reads=[BOn[0]], key="dbg0")
                        P.op("sp", lambda E, qn=qn: E.dma_start(out=d1[:, :qn], in_=On[1][:, :qn]), reads=[BOn[1]], key="dbg1")
                        P.op("sp", lambda E: E.dma_start(out=d2[:, :], in_=neglam[:, :]), reads=[Bnl], key="dbg2")
                    P.op("dve", lambda E, qn=qn: E.tensor_scalar(On[1][:, :qn], On[1][:, :qn], neglam[:, 0:1], None, ALU.mult), reads=[BOn[1], Bnl], writes=[BOn[1]])
                    P.op("dve", lambda E, qn=qn: E.tensor_tensor(Dm[:, :qn], On[0][:, :qn], On[1][:, :qn], ALU.add), reads=[BOn[0], BOn[1]], writes=[BDm])
                    P.op("dve", lambda E, qn=qn: E.tensor_tensor(D2[:, :qn], Dm[:, :qn], Dm[:, :qn], ALU.mult), reads=[BDm], writes=[BD2])
                    P.op("pe", lambda E, qn=qn: E.matmul(pX[:, :qn], ones[:, :], D2[:, :qn], start=True, stop=True), reads=[Bones, BD2], writes=[BpX])
                    rstd_ops(P, lambda qn=qn: pX[:, :qn], lambda qn=qn: rs[:, :qn], 1.0 / 128, 1e-5, BpX, Brs)
                    P.op("dve", lambda E, qn=qn: E.tensor_tensor(Dm[:, :qn], Dm[:, :qn], rs[:, :qn], ALU.mult), reads=[BDm, Brs], writes=[BDm])
                    P.op("dve", lambda E, qn=qn: E.tensor_scalar(Dm[:, :qn], Dm[:, :qn], gd[:, 0:1], None, ALU.mult), reads=[BDm, Bgd], writes=[BDm])
                    if debug and h == 3 and q0 == 0:
                        d3 = dsc(f"dbg_lv_{l}{si}", [64, 4], F32)
                        d4 = dsc(f"dbg_ee_{l}{si}", [128, 2], F32)
                        d5 = dsc(f"dbg_rs_{l}{si}", [128, 512], F32)
                        d6 = dsc(f"dbg_Dm_{l}{si}", [128, 512], F32)
                        P.op("sp", lambda E: E.dma_start(out=d3[:, :], in_=lv[:, :]), reads=[Blv], key="dbg0")
                        P.op("sp", lambda E: E.dma_start(out=d4[:, :], in_=ee[:, :]), reads=[Bee], key="dbg1")
                        P.op("sp", lambda E, qn=qn: E.dma_start(out=d5[:, :qn], in_=rs[:, :qn]), reads=[Brs], key="dbg2")
                        P.op("sp", lambda E, qn=qn: E.dma_start(out=d6[:, :qn], in_=Dm[:, :qn]), reads=[BDm], key="dbg0")
                    yb = nxt("yb", 2)
                    P.op("dve", lambda E, yb=yb, gb=gb, qn=qn: E.tensor_tensor(Yb[yb][:, :qn], Dm[:, :qn], gt[gb][:, :qn], ALU.mult), reads=[BDm, Bgt[gb]], writes=[BYb[yb]])
                    P.op("sp", lambda E, yb=yb, h=h, q0=q0, qn=qn: E.dma_start(out=c.yT[12 + h, :, q0:q0 + qn], in_=Yb[yb][:, :qn]), reads=[BYb[yb]], key=f"y{yb}")
            P.emit(nc, f"df{l}{si}")

    def phase_dft(l, si):
        c = S[si]
        T = c.T
        sbs = blocks(T, 128)
        fscale = 1.0 / math.sqrt(T * 128.0)
        with ExitStack() as es:
            P = Prog()
            wff = sb(es, "wff", [128, 4, 128], F32)
            wfm = sb(es, "wfm", [128, 4, 128], BF16)
            ABt = [sb(es, f"ABt{i}", [128, 4, 256], BF16) for i in range(3)]
            Ct = [sb(es, f"Ct{i}", [128, 512], BF16) for i in range(3)]
            Nt = [sb(es, f"Nt{i}", [128, 512], BF16) for i in range(3)]
            fT = [sb(es, f"fT{i}", [128, 512], BF16) for i in range(2)]
            gt = [sb(es, f"gt{i}", [128, 512], BF16) for i in range(2)]
            Yb = [sb(es, f"Yb{i}", [128, 512], BF16) for i in range(2)]
            pF = [ps(es, f"pF{i}") for i in range(4)]
            pY = [ps(es, f"pY{i}") for i in range(2)]
            Bw = Buf("w")
            BAB = [Buf(f"AB{i}") for i in range(3)]
            BC = [Buf(f"C{i}") for i in range(3)]
            BfT = [Buf("fT0"), Buf("fT1")]
            Bgt = [Buf("gt0"), Buf("gt1")]
            BYb = [Buf("Yb0"), Buf("Yb1")]
            BpF = [PB(f"pF{i}") for i in range(4)]
            BpY = [PB("pY0"), PB("pY1")]
            P.op("sp", lambda E: E.dma_start(out=wff[:], in_=w_fmix[l].rearrange("g c d -> c g d")), writes=[Bw], key="w")
            P.op("dve", lambda E: E.tensor_copy(wfm[:], wff[:]), reads=[Bw], writes=[Bw])
            cnt = {}

            def nxt(k, n):
                v = cnt.get(k, 0)
                cnt[k] = v + 1
                return v % n

            for (k0, nk) in blocks(T, 512):
                for si_, (s0, ns) in enumerate(sbs):
                    b = nxt("in", 3)
                    P.op("sp", lambda E, b=b, s0=s0, ns=ns: E.dma_start(out=ABt[b][:ns], in_=c.AB[s0:s0 + ns]), writes=[BAB[b]], key=f"ab{b}")
                    P.op("sp", lambda E, b=b, s0=s0, ns=ns, k0=k0, nk=nk: E.dma_start(out=Ct[b][:ns, :nk], in_=c.CT[s0:s0 + ns, k0:k0 + nk]), writes=[BC[b]], key=f"ct{b}")
                    P.op("sp", lambda E, b=b, s0=s0, ns=ns, k0=k0, nk=nk: E.dma_start(out=Nt[b][:ns, :nk], in_=c.NST[s0:s0 + ns, k0:k0 + nk]), writes=[BC[b]], key=f"nt{b}")
                    for g in range(4):
                        P.op("pe", lambda E, b=b, g=g, ns=ns, nk=nk, si_=si_: E.matmul(pF[g][:, :nk], ABt[b][:ns, g, 0:128], Ct[b][:ns, :nk], start=(si_ == 0), stop=False),
                             reads=[BAB[b], BC[b]], writes=[BpF[g]])
                        P.op("pe", lambda E, b=b, g=g, ns=ns, nk=nk, si_=si_: E.matmul(pF[g][:, :nk], ABt[b][:ns, g, 128:256], Nt[b][:ns, :nk], start=False, stop=(si_ == len(sbs) - 1)),
                             reads=[BAB[b], BC[b]], writes=[BpF[g]])
                for g in range(4):
                    fb = nxt("f", 2)
                    P.op("act", lambda E, fb=fb, g=g, nk=nk: E.activation(fT[fb][:, :nk], pF[g][:, :nk], AF.Copy, scale=fscale), reads=[BpF[g]], writes=[BfT[fb]])
                    yb = nxt("y", 2)
                    P.op("pe", lambda E, fb=fb, yb=yb, g=g, nk=nk: E.matmul(pY[yb][:, :nk], wfm[:, g, :], fT[fb][:, :nk], start=True, stop=True),
                         reads=[Bw, BfT[fb]], writes=[BpY[yb]])
                    gb = nxt("g", 2)
                    P.op("sp", lambda E, gb=gb, g=g, k0=k0, nk=nk: E.dma_start(out=gt[gb][:, :nk], in_=c.gT[g, :, k0:k0 + nk]), writes=[Bgt[gb]], key=f"g{gb}")
                    P.op("dve", lambda E, yb=yb, gb=gb, nk=nk: E.tensor_tensor(Yb[yb][:, :nk], pY[yb][:, :nk], gt[gb][:, :nk], ALU.mult), reads=[BpY[yb], Bgt[gb]], writes=[BYb[yb]])
                    P.op("sp", lambda E, yb=yb, g=g, k0=k0, nk=nk: E.dma_start(out=c.yT[g, :, k0:k0 + nk], in_=Yb[yb][:, :nk]), reads=[BYb[yb]], key=f"y{yb}")
            P.emit(nc, f"dft{l}{si}")

    def phase_out(l, si, xsrc, xdst, final):
        c = S[si]
        T = c.T
        wo_l = w_o[l].rearrange("(c p) n -> p c n", p=128)
        with ExitStack() as es:
            P = Prog()
            wf = sb(es, "wf", [128, 16, 256], F32)
            wo = sb(es, "wo", [128, 16, 2048], BF16)
            yt = sb(es, "yt", [128, 16, 512], BF16)
            xt = [sb(es, f"xt{i}", [128, D], F32) for i in range(2)]
            xo = [sb(es, f"xo{i}", [128, D], F32) for i in range(2)]
            junk = sb(es, "junk", [128, D], BF16)
            gt = sb(es, "gt", [128, D], F32)
            grow = sb(es, "grow", [1, D], F32)
            one1 = sb(es, "one1", [1, 128], F32)
            ss = [sb(es, f"ss{i}", [128, 2], F32) for i in range(2)]
            po = [ps(es, f"po{i}") for i in range(4)]
            Bwf, Bwo, Byt = Buf("wf"), Buf("wo"), Buf("yt")
            Bxt = [Buf("xt0"), Buf("xt1")]
            Bxo = [Buf("xo0"), Buf("xo1")]
            Bjunk, Bgt, Bgrow, Bone1 = Buf("junk"), Buf("gt"), Buf("grow"), Buf("one1")
            Bss = [Buf("ss0"), Buf("ss1")]
            Bpo = [PB(f"po{i}") for i in range(4)]
            for i in range(8):
                for hf in range(2):
                    P.op("sp", lambda E, i=i, hf=hf: E.dma_start(out=wf[:, hf * 8:hf * 8 + 8, :], in_=wo_l[:, hf * 8:hf * 8 + 8, i * 256:(i + 1) * 256]), writes=[Bwf], key="wf")
                P.op("dve", lambda E, i=i: E.tensor_copy(wo[:, :, i * 256:(i + 1) * 256], wf[:]), reads=[Bwf], writes=[Bwo])
            if final:
                P.op("sp", lambda E: E.dma_start(out=grow[:], in_=final_norm.rearrange("(o n) -> o n", o=1)), writes=[Bgrow], key="c4")
                P.op("dve", lambda E: E.memset(one1[:], 1.0), writes=[Bone1])
                for i in range(4):
                    P.op("pe", lambda E, i=i: E.matmul(po[i][:, :], one1[:, :], grow[:, i * 512:(i + 1) * 512], start=True, stop=True),
                         reads=[Bone1, Bgrow], writes=[Bpo[i]])
                    P.op("dve", lambda E, i=i: E.tensor_copy(gt[:, i * 512:(i + 1) * 512], po[i][:, :]), reads=[Bpo[i]], writes=[Bgt])
            cnt = {}

            def nxt(k, n):
                v = cnt.get(k, 0)
                cnt[k] = v + 1
                return v % n

            for (q0, qn) in blocks(T, 512):
                for hf in range(2):
                    P.op("sp", lambda E, q0=q0, qn=qn, hf=hf: E.dma_start(out=yt[:, hf * 8:hf * 8 + 8, :qn], in_=c.yT[hf * 8:hf * 8 + 8, :, q0:q0 + qn].rearrange("c p t -> p c t")), writes=[Byt], key="yt")
                for (t0, tn) in blocks(qn, 128):
                    a = q0 + t0
                    b = nxt("x", 2)
                    P.op("sp", lambda E, b=b, a=a, tn=tn: E.dma_start(out=xt[b][:tn, :], in_=xsrc[a:a + tn, :]), writes=[Bxt[b]], key=f"x{b}")
                    for dc in range(4):
                        for cc in range(16):
                            P.op("pe", lambda E, dc=dc, cc=cc, t0=t0, tn=tn: E.matmul(po[dc][:tn, :], yt[:, cc, t0:t0 + tn], wo[:, cc, dc * 512:(dc + 1) * 512], start=(cc == 0), stop=(cc == 15)),
                                 reads=[Byt, Bwo], writes=[Bpo[dc]])
                        P.op("dve", lambda E, dc=dc, b=b, tn=tn: E.tensor_tensor(xo[b][:tn, dc * 512:(dc + 1) * 512], po[dc][:tn, :], xt[b][:tn, dc * 512:(dc + 1) * 512], ALU.add),
                             reads=[Bpo[dc], Bxt[b]], writes=[Bxo[b]])
                    if not final:
                        P.op("sp", lambda E, b=b, a=a, tn=tn: E.dma_start(out=xdst[a:a + tn, :], in_=xo[b][:tn, :]), reads=[Bxo[b]], key=f"o{b}")
                    else:
                        P.op("act", lambda E, b=b, tn=tn: E.activation(junk[:tn, :], xo[b][:tn, :], AF.Square, accum_out=ss[b][:tn, 0:1]),
                             reads=[Bxo[b]], writes=[Bjunk, Bss[b]])
                        rstd_ops(P, lambda b=b, tn=tn: ss[b][:tn, 0:1], lambda b=b, tn=tn: ss[b][:tn, 1:2], 1.0 / D, EPS, Bss[b], Bss[b])
                        P.op("dve", lambda E, b=b, tn=tn: E.tensor_scalar(xo[b][:tn, :], xo[b][:tn, :], ss[b][:tn, 1:2], None, ALU.mult), reads=[Bxo[b], Bss[b]], writes=[Bxo[b]])
                        P.op("dve", lambda E, b=b, tn=tn: E.tensor_tensor(xo[b][:tn, :], xo[b][:tn, :], gt[:tn, :], ALU.mult), reads=[Bxo[b], Bgt], writes=[Bxo[b]])
                        lo = max(a, N_META)
                        if lo < a + tn:
                            P.op("sp", lambda E, b=b, a=a, tn=tn, lo=lo: E.dma_start(out=xdst[lo - N_META:a + tn - N_META, :], in_=xo[b][lo - a:tn, :]), reads=[Bxo[b]], key=f"o{b}")
            P.emit(nc, f"out{l}{si}")

    def full():
        for si in range(nseg):
            phase_tables(si)
        phase_bias()
        for l in range(NL):
            for si in range(nseg):
                xs = x_in[si] if l == 0 else S[si].x1
                phase_inproj(l, si, xs)
                phase_q(l, si)
                phase_kv(l, si)
                phase_dft(l, si)
                phase_mla(l, si)
                phase_diff(l, si)
                if l == NL - 1:
                    phase_out(l, si, xs, y_out[si], True)
                else:
                    phase_out(l, si, xs, S[si].x1, False)

    G.phase_mla, G.phase_diff, G.phase_dft, G.phase_out, G.full = phase_mla, phase_diff, phase_dft, phase_out, full
    G.phase_tables, G.phase_bias, G.phase_inproj = phase_tables, phase_bias, phase_inproj
    G.nc, G.S, G.x_in, G.y_out, G.Gb = nc, S, x_in, y_out, Gb
    G.din = dict(w_uq=w_uq, w_ukv=w_ukv, q_norm=q_norm, kv_norm=kv_norm, cos2=cos2, sin2=sin2, w_fmix=w_fmix, w_o=w_o,
                 lam=lam_in, diff_norm=diff_norm, final_norm=final_norm, rel_bias=rel_bias, ident=ident_d)
    G.sb, G.ps = sb, ps
    return G


SEG_T = [4096 + N_META, 8192 + N_META]


def kernel(x_prompt, x_sample, meta_tokens, rel_bias, final_norm, norm_w, w_in, w_fmix, q_norm, w_uq,
           kv_norm, w_ukv, lam_q1, lam_k1, lam_q2, lam_k2, diff_norm, w_o):
    f32 = lambda a: np.ascontiguousarray(np.asarray(a, dtype=np.float32))
    x_prompt, x_sample, meta = f32(x_prompt), f32(x_sample), f32(meta_tokens)
    G = build(SEG_T, debug=False)
    G.full()
    ident, cs, oh = misc_consts()
    shared = dict(ident=ident, cs=cs, oh=oh, rel_bias=f32(rel_bias), final_norm=f32(final_norm), norm_w=f32(norm_w),
                  w_in=f32(w_in), w_fmix=f32(w_fmix), q_norm=f32(q_norm), w_uq=f32(w_uq), kv_norm=f32(kv_norm),
                  w_ukv=f32(w_ukv), lam_q1=f32(lam_q1), lam_k1=f32(lam_k1), lam_q2=f32(lam_q2), lam_k2=f32(lam_k2),
                  diff_norm=f32(diff_norm), w_o=f32(w_o))
    for s, T in enumerate(SEG_T):
        shared[f"cos2_{s}"], shared[f"sin2_{s}"] = host_consts(T)
        shared[f"tj_{s}"], shared[f"tk_{s}"] = dft_consts(T)
    xp = [np.concatenate([meta, x_prompt[g]], 0) for g in range(x_prompt.shape[0])]
    in_maps = []
    for c in range(8):
        m = dict(shared)
        m["x0"] = np.concatenate([meta, x_sample[c]], 0)
        m["x1"] = xp[c // 4]
        in_maps.append(m)
    res = run_bass_kernel_spmd(G.nc, in_maps, core_ids=list(range(8)))
    y_sample = np.stack([np.asarray(res.results[c]["y0"], dtype=np.float32) for c in range(8)], 0)
    y_prompt = np.stack([np.asarray(res.results[4 * g]["y1"], dtype=np.float32) for g in range(2)], 0)
    return (y_prompt, y_sample)
```
